# Optimizing a Trainium2 kernel written in Bass

```python
import jax
import jax.numpy as jnp
from jax import lax
import numpy as np

D_MODEL = 1024
BATCH = 4
SEQ = 8192
DEPTH = 2

A_HEADS = 4
A_KDIM = 128
A_VDIM = 128
A_CHUNK = 64
B_GROUPS = 4
B_GDIM = 128
B_CHUNK = 128
C_QHEADS = 8
C_KVHEADS = 2
C_HDIM = 64
C_WINDOW = 128
C_BLOCK = 128
ROPE_THETA = 500000.0
ROPE_DIM = C_HDIM // 4
N_BRANCH = 3
D_FF = 4 * D_MODEL
EPS = 1e-6

A_KW = A_HEADS * A_KDIM
A_VW = A_HEADS * A_VDIM
B_W = B_GROUPS * B_GDIM
C_QW = C_QHEADS * C_HDIM
C_KVW = C_KVHEADS * C_HDIM
BRANCH_W = A_VW
SPLIT_SIZES = (A_KW, A_VW, A_KW, A_KW, A_VW, B_W, B_W, C_QW, C_KVW, C_KVW, N_BRANCH * D_MODEL)
D_IN = sum(SPLIT_SIZES)

kernel_name = 'hybrid_hgrn2_gmlp_swa_encoder'


def rms_norm(x, w):
    xf = x.astype(jnp.float32)
    y = xf * lax.rsqrt(jnp.mean(xf * xf, axis=-1, keepdims=True) + EPS)
    return (y * w.astype(jnp.float32)).astype(x.dtype)


def layer_norm(x):
    xf = x.astype(jnp.float32)
    xc = xf - jnp.mean(xf, axis=-1, keepdims=True)
    return (xc * lax.rsqrt(jnp.mean(xc * xc, axis=-1, keepdims=True) + EPS)).astype(x.dtype)


def partial_rope(t, pos):
    half = ROPE_DIM // 2
    inv = ROPE_THETA ** (-jnp.arange(half, dtype=jnp.float32) * (2.0 / ROPE_DIM))
    ang = pos[:, None] * inv[None, :]
    cos = jnp.cos(ang)[None, :, None, :]
    sin = jnp.sin(ang)[None, :, None, :]
    tf = t[..., :ROPE_DIM].astype(jnp.float32)
    t1, t2 = tf[..., :half], tf[..., half:]
    rot = jnp.concatenate([t1 * cos - t2 * sin, t2 * cos + t1 * sin], axis=-1)
    return jnp.concatenate([rot.astype(t.dtype), t[..., ROPE_DIM:]], axis=-1)


def window_attention(q, k, v, sink):
    bsz, s_len = q.shape[0], q.shape[1]
    nb = s_len // C_BLOCK
    grp = C_QHEADS // C_KVHEADS
    qb = q.reshape(bsz, nb, C_BLOCK, C_KVHEADS, grp, C_HDIM)
    pad = ((0, 0), (C_BLOCK, C_BLOCK), (0, 0), (0, 0))

    def band(t):
        tp = jnp.pad(t, pad).reshape(bsz, nb + 2, C_BLOCK, C_KVHEADS, C_HDIM)
        return jnp.concatenate([tp[:, :-2], tp[:, 1:-1], tp[:, 2:]], axis=2)

    kb, vb = band(k), band(v)
    scores = jnp.einsum('bnqhgd,bnkhd->bnhgqk', qb, kb).astype(jnp.float32) * (C_HDIM ** -0.5)
    blk = jnp.arange(nb)[:, None, None]
    qpos = blk * C_BLOCK + jnp.arange(C_BLOCK)[None, :, None]
    kpos = (blk - 1) * C_BLOCK + jnp.arange(3 * C_BLOCK)[None, None, :]
    mask = (jnp.abs(qpos - kpos) <= C_WINDOW) & (kpos >= 0) & (kpos < s_len)
    scores = jnp.where(mask[None, :, None, None], scores, -jnp.inf)
    sink_l = sink.astype(jnp.float32).reshape(C_KVHEADS, grp)[None, None, :, :, None, None]
    m = jnp.maximum(jnp.max(scores, axis=-1, keepdims=True), sink_l)
    p = jnp.exp(scores - m)
    p = p / (jnp.sum(p, axis=-1, keepdims=True) + jnp.exp(sink_l - m))
    o = jnp.einsum('bnhgqk,bnkhd->bnqhgd', p.astype(v.dtype), vb)
    return o.reshape(bsz, s_len, C_QW)


def hgrn2_gate(z, lb):
    zf = z.astype(jnp.float32)
    log_f = jnp.logaddexp(jnp.log(lb), jnp.log1p(-lb) + jax.nn.log_sigmoid(zf))
    key = (1.0 - lb) * jax.nn.sigmoid(-zf)
    return log_f, key


def gated_linear_scan(q, k, v, log_f):
    n, s_len = q.shape[0], q.shape[1]
    nc = s_len // A_CHUNK

    def to_chunks(t):
        return t.reshape(n, nc, A_CHUNK, A_HEADS, t.shape[-1]).transpose(1, 0, 3, 2, 4)

    qc, kc, vc = to_chunks(q), to_chunks(k), to_chunks(v)
    bc = jnp.cumsum(to_chunks(log_f), axis=3)
    tri = jnp.tril(jnp.ones((A_CHUNK, A_CHUNK), dtype=bool))

    def step(state, inp):
        qt, kt, vt, bt = inp
        b_last = bt[:, :, -1:, :]
        o_inter = jnp.einsum('nhck,nhkv->nhcv', qt * jnp.exp(bt), state)
        diff = bt[:, :, :, None, :] - bt[:, :, None, :, :]
        decay = jnp.exp(jnp.where(tri[:, :, None], diff, -jnp.inf))
        attn = jnp.einsum('nhtk,nhsk,nhtsk->nhts', qt, kt, decay)
        o_intra = jnp.einsum('nhts,nhsv->nhtv', attn, vt)
        state = jnp.exp(b_last[:, :, 0, :])[..., None] * state + jnp.einsum(
            'nhsk,nhsv->nhkv', kt * jnp.exp(b_last - bt), vt)
        return state, o_inter + o_intra

    state0 = jnp.zeros((n, A_HEADS, A_KDIM, A_VDIM), jnp.float32)
    _, o = lax.scan(step, state0, (qc, kc, vc, bc))
    return o.transpose(1, 0, 3, 2, 4).reshape(n, s_len, A_HEADS, A_VDIM)


def hgrn2_bidirectional(q, i, zf_fwd, zf_bwd, lb):
    bsz, s_len = q.shape[0], q.shape[1]
    shp = (bsz, s_len, A_HEADS, A_KDIM)
    qf = q.astype(jnp.float32).reshape(shp)
    vf = i.astype(jnp.float32).reshape(bsz, s_len, A_HEADS, A_VDIM)
    lf_f, k_f = hgrn2_gate(zf_fwd.reshape(shp), lb[0])
    lf_b, k_b = hgrn2_gate(zf_bwd.reshape(shp), lb[1])

    def flip(t):
        return t[:, ::-1]

    o = gated_linear_scan(
        jnp.concatenate([qf, flip(qf)], axis=0),
        jnp.concatenate([k_f, flip(k_b)], axis=0),
        jnp.concatenate([vf, flip(vf)], axis=0),
        jnp.concatenate([lf_f, flip(lf_b)], axis=0))
    return o[:bsz] + flip(o[bsz:])


def spatial_gating(u, v, w_s, b_s):
    bsz, s_len = u.shape[0], u.shape[1]
    nb = s_len // B_CHUNK
    vn = layer_norm(v).reshape(bsz, nb, B_CHUNK, B_GROUPS, B_GDIM)
    mixed = jnp.einsum('bnpgc,gqp->bnqgc', vn, w_s) + b_s.T[None, None, :, :, None]
    return u * mixed.reshape(bsz, s_len, B_W)


def setup_inputs(seed: int = 0) -> dict:
    key = jax.random.key(seed)
    ks = jax.random.split(key, 14)
    f32 = jnp.float32

    def nrm(k, shape, scale):
        return jax.random.normal(k, shape, f32) * scale

    return {
        'x': nrm(ks[0], (BATCH, SEQ, D_MODEL), 1.0),
        'w_in': nrm(ks[1], (DEPTH, D_MODEL, D_IN), D_MODEL ** -0.5),
        'ln1': 1.0 + nrm(ks[2], (DEPTH, D_MODEL), 0.02),
        'lb_logits': nrm(ks[3], (DEPTH, 2, A_KW), 0.5),
        'a_norm': 1.0 + nrm(ks[4], (DEPTH, A_VDIM), 0.02),
        'w_s': nrm(ks[5], (DEPTH, B_GROUPS, B_CHUNK, B_CHUNK), B_CHUNK ** -0.5),
        'b_s': 1.0 + nrm(ks[6], (DEPTH, B_GROUPS, B_CHUNK), 0.1),
        'sink': nrm(ks[7], (DEPTH, C_QHEADS), 0.5),
        'w_br': nrm(ks[8], (DEPTH, N_BRANCH, BRANCH_W, D_MODEL), BRANCH_W ** -0.5),
        'w_out': nrm(ks[9], (DEPTH, D_MODEL, D_MODEL), D_MODEL ** -0.5),
        'ln2': 1.0 + nrm(ks[10], (DEPTH, D_MODEL), 0.02),
        'w_up': nrm(ks[11], (DEPTH, D_MODEL, D_FF), D_MODEL ** -0.5),
        'w_down': nrm(ks[12], (DEPTH, D_FF, D_MODEL), D_FF ** -0.5),
        'final_norm': 1.0 + nrm(ks[13], (D_MODEL,), 0.02),
    }


def reference(x, w_in, ln1, lb_logits, a_norm, w_s, b_s, sink, w_br, w_out, ln2, w_up, w_down, final_norm):
    bsz, s_len = x.shape[0], x.shape[1]
    pos = jnp.arange(s_len, dtype=jnp.float32)
    p = jax.nn.softmax(lb_logits.astype(jnp.float32), axis=0)
    cum = jnp.cumsum(p, axis=0)
    lower = (cum - cum[0:1]).reshape(DEPTH, 2, A_HEADS, A_KDIM)
    offsets = np.cumsum(SPLIT_SIZES)[:-1].tolist()
    for l in range(DEPTH):
        xn = rms_norm(x, ln1[l])
        z = xn @ w_in[l]
        a_q, a_i, a_ff, a_fb, a_g, b_u, b_v, c_q, c_k, c_v, gz = jnp.split(z, offsets, axis=-1)
        o_a = hgrn2_bidirectional(a_q, a_i, a_ff, a_fb, lower[l])
        y_a = (rms_norm(o_a, a_norm[l]) * jax.nn.silu(a_g.astype(jnp.float32).reshape(o_a.shape)))
        y_a = y_a.astype(x.dtype).reshape(bsz, s_len, A_VW)
        y_b = spatial_gating(jax.nn.gelu(b_u), jax.nn.gelu(b_v), w_s[l], b_s[l])
        qh = partial_rope(c_q.reshape(bsz, s_len, C_QHEADS, C_HDIM), pos)
        kh = partial_rope(c_k.reshape(bsz, s_len, C_KVHEADS, C_HDIM), pos)
        vh = c_v.reshape(bsz, s_len, C_KVHEADS, C_HDIM)
        y_c = window_attention(qh, kh, vh, sink[l])
        gates = jax.nn.sigmoid(gz.reshape(bsz, s_len, N_BRANCH, D_MODEL))
        merged = (gates[:, :, 0] * (y_a @ w_br[l, 0])
                  + gates[:, :, 1] * (y_b @ w_br[l, 1])
                  + gates[:, :, 2] * (y_c @ w_br[l, 2]))
        x = x + merged @ w_out[l]
        hn = rms_norm(x, ln2[l])
        x = x + jnp.square(jax.nn.relu(hn @ w_up[l])) @ w_down[l]
    return rms_norm(x, final_norm)
```

```python
import contextlib
import numpy as np
import ml_dtypes
import concourse.bass as bass
import concourse.mybir as mybir
from concourse.bass_utils import run_bass_kernel_spmd

F32 = mybir.dt.float32
BF16 = mybir.dt.bfloat16
AF = mybir.ActivationFunctionType
ALU = mybir.AluOpType

D = 1024
KC = 8
TG = 512
EPS = 1e-6
NCORES = 8
NWB = 3
BLK = ['ai', 'h0', 'h1', 'h2', 'h3', 'bu', 'bv', 'cq', 'ckv'] + ['gz%d' % i for i in range(6)] + \
      ['br%d' % i for i in range(4)] + ['wo0', 'wo1'] + ['up%d' % i for i in range(8)] + ['dn%d' % i for i in range(8)]
BI = {n: i for i, n in enumerate(BLK)}
NB = len(BLK)
C_ID, C_PM, C_M3, C_MC, C_MH1, C_MH2, C_ONE, C_NB, C_NA, C_NC, C_END = 0, 128, 256, 640, 768, 832, 896, 1024, 1152, 1280, 1408
P_LN1, P_LN2, P_FN, P_LBL, P_AN, P_SEL, P_END = 0, 16, 32, 40, 56, 58, 60


class Sched:
    ENGS = ('pe', 'act', 'dve', 'pool', 'sp')

    def __init__(self, nc, stack):
        self.nc = nc
        self.stack = stack
        self.lists = {e: [] for e in self.ENGS}
        self.cnt = {}
        self.sem = {}
        self.waited = {e: {} for e in self.ENGS}
        self.lastw = {}
        self.readers = {}
        for e in self.ENGS[:4]:
            self._mksem(e)

    def _mksem(self, name):
        if name not in self.sem:
            self.sem[name] = self.stack.enter_context(self.nc.semaphore("s_" + name))
            self.cnt[name] = 0

    def _deps(self, eng, reads, writes):
        deps = {}
        for b in reads:
            w = self.lastw.get(b)
            if w:
                deps[w[0]] = max(deps.get(w[0], 0), w[1])
            if isinstance(b, tuple) and b[0] == 'ps':
                for s, v in self.readers.get(b, {}).items():
                    if s != eng:
                        deps[s] = max(deps.get(s, 0), v)
        for b in writes:
            w = self.lastw.get(b)
            if w:
                deps[w[0]] = max(deps.get(w[0], 0), w[1])
            for s, v in self.readers.get(b, {}).items():
                deps[s] = max(deps.get(s, 0), v)
        waits = []
        for s, v in deps.items():
            if eng == 'pe' and s == 'pe':
                continue
            if self.waited[eng].get(s, 0) < v:
                waits.append((s, v))
                self.waited[eng][s] = v
        return waits

    def _record(self, semname, val, reads, writes):
        for b in reads:
            r = self.readers.setdefault(b, {})
            r[semname] = max(r.get(semname, 0), val)
        for b in writes:
            self.lastw[b] = (semname, val)
            self.readers[b] = {}

    def op(self, eng, fn, reads=(), writes=()):
        waits = self._deps(eng, reads, writes)
        self.cnt[eng] += 1
        self.lists[eng].append((waits, fn, eng, 1))
        self._record(eng, self.cnt[eng], reads, writes)

    def dma(self, q, out, in_, reads=(), writes=(), slot=None):
        self._mksem(slot)
        waits = self._deps(q, reads, writes)
        self.cnt[slot] += 16
        self.lists[q].append((waits, lambda e: e.dma_start(out=out, in_=in_), slot, 16))
        self._record(slot, self.cnt[slot], reads, writes)

    def coll(self, fn, reads=(), writes=(), slot=None):
        self._mksem(slot)
        assert self.cnt[slot] == 0
        waits = self._deps('pool', reads, writes)
        self.cnt[slot] = 1
        self.lists['pool'].append((waits, fn, slot, None))
        self._record(slot, 1, reads, writes)

    def final_wait(self, q, slots):
        waits = [(s, self.cnt[s]) for s in slots if self.cnt.get(s, 0) > 0]
        self.lists[q].append((waits, None, None, 0))

    def emit(self, block):
        sem = self.sem

        def run(lst):
            def body(e):
                for waits, fn, semname, inc in lst:
                    for s, v in waits:
                        e.wait_ge(sem[s], v)
                    if fn is not None:
                        inst = fn(e)
                        if inc is None:
                            inst.then_inc(sem[semname])
                        else:
                            inst.then_inc(sem[semname], inc)
            return body
        block.tensor(run(self.lists['pe']))
        block.scalar(run(self.lists['act']))
        block.vector(run(self.lists['dve']))
        block.gpsimd(run(self.lists['pool']))
        block.sync(run(self.lists['sp']))


class _Stop(Exception):
    pass


DEBUG_STOP = [None]


def build_program(TOK, NL, S_FULL):
    NG = TOK // TG
    NT = TOK // 128
    nc = bass.Bass("TRN2", target_bir_lowering=False)
    xin = nc.dram_tensor("xin", [KC, 128, TOK], F32, kind="ExternalInput").ap()
    wsrc = nc.dram_tensor("wsrc", [NL, NB, 128, 4096], F32, kind="ExternalInput").ap()
    pp_d = nc.dram_tensor("pp", [128, P_END], F32, kind="ExternalInput").ap()
    cm_d = nc.dram_tensor("cm", [128, C_END], BF16, kind="ExternalInput").ap()
    bs_d = nc.dram_tensor("bsv", [NL, 512], F32, kind="ExternalInput").ap()
    wst_d = nc.dram_tensor("wst", [NL, 128, 512], F32, kind="ExternalInput").ap()
    sink_d = nc.dram_tensor("sinkv", [NL, 8], F32, kind="ExternalInput").ap()
    rope_d = nc.dram_tensor("rope", [2, 128, TOK], F32, kind="ExternalInput").ap()
    yout = nc.dram_tensor("yout", [KC, 128, TOK], F32, kind="ExternalOutput").ap()
    wbf = nc.dram_tensor("wbf", [NL, NB, 128, 4096], BF16).ap()
    xs = [nc.dram_tensor("xs%d" % l, [KC, 128, TOK], F32).ap() for l in range(max(NL - 1, 1))]
    s1s = nc.dram_tensor("s1s", [NL, NG, 128, 512], F32).ap()
    exS_i = [nc.dram_tensor("exSi%d" % l, [128, 512], F32) for l in range(NL)]
    exS_o = [nc.dram_tensor("exSo%d" % l, [256, 512], F32) for l in range(NL)]
    exK_i = [nc.dram_tensor("exKi%d" % l, [128, 512], BF16) for l in range(NL)]
    exK_o = [nc.dram_tensor("exKo%d" % l, [256, 512], BF16) for l in range(NL)]

    st = contextlib.ExitStack()
    with st:
        S = Sched(nc, st)

        def sb(name, shape, dt):
            return st.enter_context(nc.sbuf_tensor(name, shape, dt))

        cm = sb("cm_s", [128, C_END], BF16)
        ppt = sb("pp_s", [128, P_END], F32)
        sc = sb("sc_s", [128, 96], F32)
        bsb = sb("bsb", [128, 512], F32)
        wsT = sb("wsT", [128, 512], BF16)
        esink = sb("esink", [128, 8], F32)
        esink2 = sb("esink2", [128, 4], F32)
        dSb = [sb("dSb%d" % i, [128, 8, 128], F32) for i in range(2)]
        xT = sb("xT", [128, KC, TG], F32)
        xh = sb("xh", [128, KC, 128], F32)
        sq = [sb("sq%d" % i, [128, 640], BF16) for i in range(2)]
        rstd = sb("rstd", [128, 640], F32)
        xnT = sb("xnT", [128, KC, 640], BF16)
        cosT = sb("cosT", [128, 640], F32)
        sinT = sb("sinT", [128, 640], F32)
        KT = sb("KT", [128, 2, 1024], BF16)
        V2 = sb("V2", [128, 8, 256], BF16)
        NF = 9
        Fp = [sb("F%d" % i, [128, 640], F32) for i in range(NF)]
        Lx = [sb("Lx%d" % i, [128, 576], F32) for i in range(2)]
        esub = sb("esub", [128, 2, 24], F32)
        ech = [sb("ech%d" % i, [128, 2, 24], F32) for i in range(2)]
        TRW = 9216
        TR = sb("TR", [128, TRW], BF16)
        S1 = sb("S1", [128, 4, 128], F32)
        S2 = sb("S2", [128, 4, 128], F32)
        Sb = [sb("Sb%d" % i, [128, 128], BF16) for i in range(16)]
        At = [sb("At%d" % i, [128, 64], BF16) for i in range(16)]
        Sall = [sb("Sall%d" % i, [128, 7, 128], F32) for i in range(2)]
        osq = sb("osq", [128, 512], BF16)
        yaT = sb("yaT", [128, 4, TG], BF16)
        ybT = sb("ybT", [128, 4, TG], BF16)
        ycT = sb("ycT", [128, 4, TG], BF16)
        exb = sb("exb", [128, 2, 512], BF16)
        exs = sb("exs", [128, 2, 512], F32)
        big = sb("big", [128, 32, TG], BF16)
        rl = [sb("rl%d" % i, [128, 512], BF16) for i in range(2)]
        bnst = sb("bnst", [128, 16], F32)
        wbuf = [sb("wb%d" % i, [128, 4096], BF16) for i in range(NWB)]
        psb = [st.enter_context(nc.psum_tensor("ps%d" % i, [128, 512], F32)) for i in range(8)]
        print("sbuf bytes remaining:", nc.sbuf_bytes_remaining)

        def trk(i):
            return ('TR', i)
        Vt = TR[:, 0:2048].rearrange("p (t c) -> p t c", t=4)
        def hv(s, j):
            o = 2048 + (s * 7 + j) * 512
            return TR[:, o:o + 512], trk(4 + s * 7 + j)
        uT = TR[:, 0:2048].rearrange("p (f t) -> p f t", f=4)
        vn = TR[:, 2048:4096].rearrange("p (t c) -> p t c", t=4)
        QT = TR[:, 0:2048].rearrange("p (f t) -> p f t", f=4)
        qraw = TR[:, 2048:2688]
        PTt = [TR[:, 3072 + i * 512:3072 + i * 512 + 384] for i in range(2)]
        PT = [[TR[:, 4096 + (hh_ * 6 + i) * 384:4096 + (hh_ * 6 + i + 1) * 384] for i in range(6)] for hh_ in range(2)]
        K_V = [trk(i) for i in range(4)]
        K_UT = [trk(i) for i in range(4)]
        K_VN = [trk(4 + i) for i in range(4)]
        K_QT = [trk(i) for i in range(4)]
        K_QRAW = [trk(4), trk(5)]
        K_PTT = [trk(6), trk(7)]
        K_PT = [[[trk(8 + ((hh_ * 6 + i) * 384) // 512), trk(8 + ((hh_ * 6 + i + 1) * 384 - 1) // 512)] for i in range(6)] for hh_ in range(2)]

        ident = cm[:, C_ID:C_ID + 128]
        Pm = cm[:, C_PM:C_PM + 128]
        M3 = cm[:, C_M3:C_M3 + 384]
        MC = cm[:, C_MC:C_MC + 128]
        MH = [cm[:, C_MH1:C_MH1 + 64], cm[:, C_MH2:C_MH2 + 64]]
        ones_b = cm[:, C_ONE:C_ONE + 128]
        NEGM = {-1: cm[:, C_NB:C_NB + 128], 1: cm[:, C_NA:C_NA + 128], 'c': cm[:, C_NC:C_NC + 128]}
        SC_LN1, SC_LN2, SC_FN, SC_AN = 0, 16, 32, 40
        SC_HB, SC_HA, SC_NHB = 44, 60, 76
        SC_ONE, SC_ZERO = 92, 93

        rot_i = [0]

        def rot():
            i = rot_i[0] % 5
            rot_i[0] += 1
            return psb[i], ('ps', i)
        ACC = [(psb[5], ('ps', 5)), (psb[6], ('ps', 6))]
        TPB = psb[7][:].bitcast(BF16)
        K_TP = ('ps', 7)
        f_i = [0]

        def ftile():
            i = f_i[0] % NF
            f_i[0] += 1
            return Fp[i], ('F', i)

        def mm(out, pairs, reads, writes):
            def fn(e, out=out, pairs=pairs):
                n = len(pairs)
                for i, (l, r) in enumerate(pairs):
                    inst = e.matmul(out, lhsT=l, rhs=r, start=(i == 0), stop=(i == n - 1))
                return inst
            S.op('pe', fn, reads, writes)

        def act(out, in_, func, reads, writes, **kw):
            S.op('act', lambda e: e.activation(out=out, in_=in_, func=func, **kw), reads, writes)

        def tt(eng, out, in0, in1, op, reads, writes):
            S.op(eng, lambda e: e.tensor_tensor(out=out, in0=in0, in1=in1, op=op), reads, writes)

        def ts(eng, out, in0, s1, s2, op0, op1, reads, writes):
            if s2 is None:
                S.op(eng, lambda e: e.tensor_scalar(out=out, in0=in0, scalar1=s1, scalar2=None, op0=op0), reads, writes)
            else:
                S.op(eng, lambda e: e.tensor_scalar(out=out, in0=in0, scalar1=s1, scalar2=s2, op0=op0, op1=op1), reads, writes)

        def stt(eng, out, in0, scalar, in1, op0, op1, reads, writes):
            S.op(eng, lambda e: e.scalar_tensor_tensor(out=out, in0=in0, scalar=scalar, in1=in1, op0=op0, op1=op1), reads, writes)

        S.dma('sp', cm[:], cm_d[:, :], writes=['cm'], slot='c0')
        S.dma('sp', ppt[:], pp_d[:, :], writes=['pp'], slot='c1')
        S.op('pool', lambda e: e.memset(sc[:, SC_ONE:SC_ONE + 1], 1.0), writes=['sc1'])
        S.op('pool', lambda e: e.memset(sc[:, SC_ZERO:SC_ZERO + 1], 0.0), writes=['sc1'])
        for i in range(2):
            S.op('pool', lambda e, i=i: e.memset(Lx[i][:], 0.0), writes=[('Lx', i)])
        sD = float(np.sqrt(D))
        ts('dve', sc[:, SC_LN1:SC_LN1 + 16], ppt[:, P_LN1:P_LN1 + 16], sD, None, ALU.mult, None, ['pp'], ['sc'])
        ts('dve', sc[:, SC_LN2:SC_LN2 + 16], ppt[:, P_LN2:P_LN2 + 16], sD, None, ALU.mult, None, ['pp'], ['sc'])
        ts('dve', sc[:, SC_FN:SC_FN + 8], ppt[:, P_FN:P_FN + 8], sD, None, ALU.mult, None, ['pp'], ['sc'])
        ts('dve', sc[:, SC_AN:SC_AN + 2], ppt[:, P_AN:P_AN + 2], float(np.sqrt(128.0) * 0.5), None, ALU.mult, None, ['pp'], ['sc'])
        lbt = sc[:, SC_HB:SC_HB + 16]
        S.op('pool', lambda e: e.memset(sc[:, SC_HB:SC_HB + 16], 0.0), writes=['sc2'])
        if NL > 1:
            tt('dve', sc[:, SC_HA + 8:SC_HA + 16], ppt[:, P_LBL + 8:P_LBL + 16], ppt[:, P_LBL:P_LBL + 8], ALU.subtract, ['pp'], ['sc3'])
            act(sc[:, SC_HB + 8:SC_HB + 16], sc[:, SC_HA + 8:SC_HA + 16], AF.Sigmoid, ['sc3', 'sc2'], ['sc2'])
        ts('dve', sc[:, SC_HA:SC_HA + 16], sc[:, SC_HB:SC_HB + 16], -0.5, 0.5, ALU.mult, ALU.add, ['sc2', 'sc3'], ['sc3'])
        ts('dve', sc[:, SC_NHB:SC_NHB + 16], sc[:, SC_HA:SC_HA + 16], -1.0, None, ALU.mult, None, ['sc3'], ['sc4'])
        ts('dve', sc[:, SC_HB:SC_HB + 16], sc[:, SC_HB:SC_HB + 16], 0.5, 0.5, ALU.mult, ALU.add, ['sc2', 'sc3'], ['sc2'])
        SCK = ['sc', 'sc1', 'sc2', 'sc3', 'sc4']

        stage = big[:].rearrange("p a b -> p (a b)").bitcast(F32)
        cast_engs = ['act', 'dve', 'dve']
        ci = 0
        def main_body():
            nonlocal ci
            for l in range(NL):
                for bi in range(NB):
                    s2 = ci % 2
                    w = ci % NWB
                    S.dma('sp', stage[:, s2 * 4096:(s2 + 1) * 4096], wsrc[l, bi], writes=[('stg', s2)], slot='pg%d' % s2)
                    eng = cast_engs[ci % 3]
                    if eng == 'act':
                        act(wbuf[w][:], stage[:, s2 * 4096:(s2 + 1) * 4096], AF.Copy, [('stg', s2)], [('wb', w)])
                    else:
                        S.op(eng, lambda e, w=w, s2=s2: e.tensor_copy(out=wbuf[w][:], in_=stage[:, s2 * 4096:(s2 + 1) * 4096]),
                             [('stg', s2)], [('wb', w)])
                    S.dma('sp', wbf[l, bi], wbuf[w][:], reads=[('wb', w)], writes=[('wbf', l, bi)], slot='pw%d' % w)
                    ci += 1
            BIGK = [('big', i) for i in range(32)]
            S.op('pool', lambda e: e.memset(sc[:, SC_ZERO:SC_ZERO + 1], 0.0), writes=BIGK + ['sc1', ('stg', 0), ('stg', 1)])

            seq = []
            for l in range(NL):
                for g in range(NG):
                    seq += [(l, 'ai', False)] + [(l, 'h%d' % h, True) for h in range(4)]
                for g in range(NG - 1, -1, -1):
                    seq += [(l, n, False) for n in ['ckv', 'ai', 'h0', 'h1', 'gz0', 'gz1', 'h2', 'gz2', 'gz3', 'h3', 'gz4', 'gz5', 'bu', 'bv', 'cq'] +
                            ['br%d' % i for i in range(4)] + ['wo0', 'wo1'] +
                            ['up%d' % i for i in range(8)] + ['dn%d' % i for i in range(8)]]
            ws = {'issued': 0, 'used': 0}

            def w_issue():
                k = ws['issued']
                if k >= len(seq):
                    return
                l, n, part = seq[k]
                slot = k % NWB
                bi = BI[n]
                if part:
                    S.dma('sp', wbuf[slot][:, 0:1024], wbf[l, bi, :, 0:1024], reads=[('wbf', l, bi)], writes=[('wb', slot)], slot='w%d' % slot)
                else:
                    S.dma('sp', wbuf[slot][:], wbf[l, bi], reads=[('wbf', l, bi)], writes=[('wb', slot)], slot='w%d' % slot)
                ws['issued'] += 1

            def w_use(l, n):
                k = ws['used']
                assert seq[k][0] == l and seq[k][1] == n, (seq[k], l, n)
                while ws['issued'] < min(k + NWB, len(seq)):
                    w_issue()
                ws['used'] += 1
                slot = k % NWB
                return wbuf[slot], ('wb', slot)

            def wv_f(wb):
                return wb[:].rearrange("p (f k c) -> p f k c", f=4, k=8)

            def rmsnorm(segs, lncol, l, out_f32=False):
                for (x3, xkeys, n, c0, okeys) in segs:
                    bank, bk = rot()
                    for kc in range(KC):
                        s = sq[kc % 2]
                        act(s[:, 0:n], x3[:, kc, :], AF.Square, xkeys, [('sq', kc % 2)])
                        S.op('pe', lambda e, s=s, n=n, kc=kc, bank=bank: e.matmul(bank[:, 0:n], lhsT=ones_b, rhs=s[:, 0:n], start=(kc == 0), stop=(kc == KC - 1)),
                             [('sq', kc % 2), 'cm'], [bk])
                    tmpf, ktmp = ftile()
                    ts('dve', tmpf[:, 0:n], bank[:, 0:n], float(D * EPS), None, ALU.add, None, [bk], [ktmp])
                    act(tmpf[:, 0:n], tmpf[:, 0:n], AF.Ln, [ktmp], [ktmp])
                    act(rstd[:, c0:c0 + n], tmpf[:, 0:n], AF.Exp, [ktmp], [('rstd', c0)], scale=-0.5)
                    for kc in range(KC):
                        eng = 'dve'
                        if out_f32:
                            stt(eng, x3[:, kc, :], x3[:, kc, :], sc[:, lncol + kc:lncol + kc + 1], rstd[:, c0:c0 + n], ALU.mult, ALU.mult,
                                [('rstd', c0)] + SCK + xkeys, xkeys)
                        else:
                            stt(eng, xnT[:, kc, c0:c0 + n], x3[:, kc, :], sc[:, lncol + kc:lncol + kc + 1], rstd[:, c0:c0 + n], ALU.mult, ALU.mult,
                                [('rstd', c0)] + SCK + xkeys, okeys)

            def xn_keys(c0):
                return [('xn', c0)]

            def proj_fm(wb, wk, fb, c0, n):
                bank, bk = rot()
                wv = wv_f(wb)
                mm(bank[:, 0:n], [(wv[:, fb, kc, :], xnT[:, kc, c0:c0 + n]) for kc in range(KC)],
                   [wk, ('xn', 0), ('xn', 128)], [bk])
                return bank, bk

            def proj_tm(wb, wk, fb0, nfb, c0):
                bank, bk = rot()
                wv = wv_f(wb)
                mm(bank[:, 0:nfb * 128].rearrange("p (f c) -> p f c", f=nfb),
                   [(xnT[:, kc, c0:c0 + 128], wv[:, fb0:fb0 + nfb, kc, :]) for kc in range(KC)],
                   [wk, ('xn', 0), ('xn', 128)], [bk])
                return bank, bk

            def rope(bank, bk, c0, n, out, okeys):
                act(qraw[:, 0:n], bank[:, 0:n], AF.Copy, [bk], K_QRAW)
                stop('rope1')
                b2, bk2 = rot()
                mm(b2[:, 0:n], [(Pm, qraw[:, 0:n])], K_QRAW + ['cm'], [bk2])
                stop('rope2')
                t1, k1 = ftile()
                tt('dve', t1[:, 0:n], bank[:, 0:n], cosT[:, c0:c0 + n], ALU.mult, [bk, 'cos'] + K_QRAW, [k1])
                stop('rope2b')
                t2, k2 = ftile()
                tt('dve', t2[:, 0:n], b2[:, 0:n], sinT[:, c0:c0 + n], ALU.mult, [bk2, 'sin'], [k2])
                stop('rope3')
                tt('pool', out, t1[:, 0:n], t2[:, 0:n], ALU.add, [k1, k2], okeys)

            def gcols(g):
                return slice(g * TG, (g + 1) * TG)

            def load_group(l, g, halo):
                src = xin if l == 0 else xs[l - 1]
                srck = ('xs', l - 1, g)
                S.dma('sp', xT[:], src[:, :, g * TG:(g + 1) * TG].rearrange("k p t -> p k t"), reads=[srck], writes=['xT'], slot='lx')
                if halo and g > 0:
                    S.dma('sp', xh[:], src[:, :, g * TG - 128:g * TG].rearrange("k p t -> p k t"), reads=[('xs', l - 1, g - 1)], writes=['xh'], slot='lh')

            def ring(t):
                return (t % 8) * 128

            def hgrn_prep2(l, h, dirs, zf, zq_bank, zq_k, hs, need_q):
                T = {}
                for d in dirs:
                    T[d] = [ftile(), ftile(), ftile()]
                sca = {}
                for d in dirs:
                    col = l * 8 + d * 4 + h
                    sca[d] = (sc[:, SC_HB + col:SC_HB + col + 1], sc[:, SC_HA + col:SC_HA + col + 1], sc[:, SC_NHB + col:SC_NHB + col + 1])
                for d in dirs:
                    (t0, k0) = T[d][0]
                    act(t0[:, 0:512], zf[d][0][:, :], AF.Tanh, [zf[d][1]], [k0], scale=0.5)
                for d in dirs:
                    (t0, k0), (t1, k1) = T[d][0], T[d][1]
                    hb, ha, nha = sca[d]
                    ts('dve', t1[:, 0:512], t0[:, 0:512], nha, ha, ALU.mult, ALU.add, [k0] + SCK, [k1])
                for d in dirs:
                    (t0, k0) = T[d][0]
                    hb, ha, nha = sca[d]
                    act(t0[:, 0:512], t0[:, 0:512], AF.Ln, [k0] + SCK, [k0], scale=ha, bias=hb)
                views = {}
                for d in dirs:
                    (t0, k0) = T[d][0]
                    LX = Lx[d]
                    S.op('dve', lambda e, LX=LX, t0=t0: e.tensor_tensor_scan(out=LX[:, 1:513], data0=sc[:, SC_ONE:SC_ONE + 1].to_broadcast([128, 512]),
                                                                             data1=t0[:, 0:512], initial=0.0, op0=ALU.mult, op1=ALU.add),
                         [k0] + SCK, [('Lx', d)])
                    views[d] = (LX[:, 0:512].rearrange("p (c t) -> p c t", t=64), LX[:, 1:513].rearrange("p (c t) -> p c t", t=64),
                                LX[:, 64:576].rearrange("p (c t) -> p c t", t=64))
                for d in dirs:
                    (t2, k2) = T[d][2]
                    L0, L1, L64 = views[d]
                    D3 = t2[:, 0:512].rearrange("p (c t) -> p c t", t=64)
                    tt('dve', D3, (L1 if d == 0 else L0), L0[:, :, 32:33].to_broadcast([128, 8, 64]), ALU.subtract, [('Lx', d)], [k2])
                for d in dirs:
                    L0, L1, L64 = views[d]
                    es = esub[:, d, :]
                    tt('dve', es[:, 0:8], L0[:, :, 32], L0[:, :, 0], ALU.subtract, [('Lx', d)], [('esub', d)])
                    tt('dve', es[:, 8:16], L64[:, :, 0], L0[:, :, 0], ALU.subtract, [('Lx', d)], [('esub', d)])
                    tt('dve', es[:, 16:24], L64[:, :, 0], L0[:, :, 32], ALU.subtract, [('Lx', d)], [('esub', d)])
                for d in dirs:
                    (t0, k0), (t2, k2) = T[d][0], T[d][2]
                    act(t0[:, 0:512], t2[:, 0:512], AF.Exp, [k2, ('Lx', d)], [k0])
                    act(t2[:, 0:512], t2[:, 0:512], AF.Exp, [k2], [k2], scale=-1.0)
                    act(ech[hs][:, d, :], esub[:, d, :], AF.Exp, [('esub', d)], [('ech', hs, d)])
                for d in dirs:
                    (t0, k0), (t1, k1), (t2, k2) = T[d]
                    E, Ei = t0, t2
                    Qm, kQ = hv(hs, d)
                    Km, kK = hv(hs, 2 + d)
                    if need_q:
                        tt('dve', Qm, zq_bank[:, :], (E if d == 0 else Ei)[:, 0:512], ALU.mult, [zq_k, k0, k2], [kQ])
                    tt('pool', Km, t1[:, 0:512], (Ei if d == 0 else E)[:, 0:512], ALU.mult, [k1, k0, k2], [kK])
                for d in dirs:
                    Km, kK = hv(hs, 2 + d)
                    KmT, kKT = hv(hs, 4 + d)

                    def tps(e, d=d, Km=Km):
                        for i in range(4):
                            r = e.transpose(TPB[:, d * 512 + i * 128:d * 512 + (i + 1) * 128], Km[:, i * 128:(i + 1) * 128], ident)
                        return r
                    S.op('pe', tps, [kK, 'cm'], [K_TP])
                    S.op('act', lambda e, d=d, KmT=KmT: e.activation(out=KmT, in_=TPB[:, d * 512:(d + 1) * 512], func=AF.Copy), [K_TP], [kKT])

            def escal(hs, d, c):
                e = ech[hs]
                ea, eb, ec = e[:, d, c:c + 1], e[:, d, 8 + c:9 + c], e[:, d, 16 + c:17 + c]
                return (ea, eb, ec) if d == 0 else (ec, eb, ea)

            def ds_compute(d, h, hs, c):
                tile_i, po = c // 2, (c % 2) * 64
                KmT, kKT = hv(hs, 4 + d)
                KmT3 = KmT.rearrange("p (t k) -> p t k", t=4)
                bank, bk = rot()
                mm(bank[:, 0:128], [(KmT3[po:po + 64, tile_i, :], Vt[po:po + 64, tile_i, h * 128:(h + 1) * 128])],
                   [kKT] + K_V, [bk])
                e1, eb, e3 = escal(hs, d, c)
                act(dSb[d][:, c, :], bank[:, 0:128], AF.Identity, [bk, ('ech', hs, d)], [('dS', d, c)], scale=e3)

            def st_src(d, h, k, Sst, skey):
                if k == 0:
                    return Sst[:, h, :], skey
                return Sall[d][:, k - 1, :], ('Sall', d, k - 1)

            def st_dst(d, h, k, Sst, skey):
                if k == 7:
                    return Sst[:, h, :], skey
                return Sall[d][:, k, :], ('Sall', d, k)

            def state_step(d, h, hs, k, c, Sst, skey):
                e1, eb, e3 = escal(hs, d, c)
                s_ap, s_k = st_src(d, h, k, Sst, skey)
                d_ap, d_k = st_dst(d, h, k, Sst, skey)
                stt('dve', d_ap, s_ap, eb, dSb[d][:, c, :], ALU.mult, ALU.add, [s_k, ('ech', hs, d), ('dS', d, c)], [d_k])

            def chunk_sb(d, h, hs, k, c, Sst, skey):
                e1, eb, e3 = escal(hs, d, c)
                i = d * 8 + k
                s_ap, s_k = st_src(d, h, k, Sst, skey)
                act(Sb[i][:], s_ap, AF.Identity, [s_k, ('ech', hs, d)], [('Sb', i)], scale=e1)

            def chunk_at(d, h, hs, k, c):
                tile_i, po = c // 2, (c % 2) * 64
                cs = slice(c * 64, (c + 1) * 64)
                Qm, kQ = hv(hs, d)
                Km, kK = hv(hs, 2 + d)
                i = d * 8 + k
                bank, bk = rot()
                mm(bank[po:po + 64, 0:64], [(Km[:, cs], Qm[:, cs])], [kK, kQ], [bk])
                tt('dve', At[i][po:po + 64, :], bank[po:po + 64, 0:64], MH[d][po:po + 64, :], ALU.mult, [bk, 'cm'], [('At', i)])

            def chunk_fin(d, h, hs, k, c, acc, acck):
                tile_i, po = c // 2, (c % 2) * 64
                cs = slice(c * 64, (c + 1) * 64)
                Qm, kQ = hv(hs, d)
                i = d * 8 + k

                def fn(e_):
                    e_.matmul(acc[:, cs], lhsT=Sb[i][:], rhs=Qm[:, cs], start=True, stop=False)
                    return e_.matmul(acc[:, cs], lhsT=Vt[po:po + 64, tile_i, h * 128:(h + 1) * 128], rhs=At[i][po:po + 64, :], start=False, stop=True)
                S.op('pe', fn, [('Sb', i), ('At', i), kQ] + K_V, [acck])

            def exchange(l, kind):
                if kind == 'S':
                    S.dma('sp', exS_i[l].ap(), S1[:].rearrange("p h v -> p (h v)"), reads=['S1'], writes=[('exSi', l)], slot='ex0')
                    S.coll(lambda e: e.collective_compute("AllGather", ALU.bypass, replica_groups=[[0, 1], [2, 3], [4, 5], [6, 7]],
                                                          ins=[exS_i[l].ap().opt()], outs=[exS_o[l].ap().opt()]),
                           reads=[('exSi', l)], writes=[('exSo', l)], slot='ccS%d' % l)
                    S.dma('sp', exs[:], exS_o[l].ap().rearrange("(r p) n -> p r n", p=128), reads=[('exSo', l)], writes=['exs'], slot='ex1')
                    s2f = S2[:].rearrange("p h v -> p (h v)")
                    ts('dve', s2f, exs[:, 0, :], ppt[:, P_SEL:P_SEL + 1], None, ALU.mult, None, ['exs', 'pp'], ['S2'])
                    stt('dve', s2f, exs[:, 1, :], ppt[:, P_SEL + 1:P_SEL + 2], s2f, ALU.mult, ALU.add, ['exs', 'pp', 'S2'], ['S2'])
                else:
                    tl = NT - 1
                    S.dma('sp', exK_i[l].ap()[:, 0:256].rearrange("p (a b) -> p a b", a=2), KT[:, :, ring(tl):ring(tl) + 128],
                          reads=['KT'], writes=[('exKi', l)], slot='ex2')
                    S.dma('sp', exK_i[l].ap()[:, 256:512], V2[:, tl % 8, :], reads=['V2'], writes=[('exKi', l, 1)], slot='ex3')
                    S.coll(lambda e: e.collective_compute("AllGather", ALU.bypass, replica_groups=[[0, 1], [2, 3], [4, 5], [6, 7]],
                                                          ins=[exK_i[l].ap().opt()], outs=[exK_o[l].ap().opt()]),
                           reads=[('exKi', l), ('exKi', l, 1)], writes=[('exKo', l)], slot='ccK%d' % l)
                    S.dma('sp', exb[:], exK_o[l].ap().rearrange("(r p) n -> p r n", p=128), reads=[('exKo', l)], writes=['exb'], slot='ex4')
                    kdst = KT[:, :, ring(NT):ring(NT) + 128]
                    e0 = exb[:, 0, 0:256].rearrange("p (a b) -> p a b", a=2)
                    e1 = exb[:, 1, 0:256].rearrange("p (a b) -> p a b", a=2)
                    ts('dve', kdst, e0, ppt[:, P_SEL:P_SEL + 1], None, ALU.mult, None, ['exb', 'pp', 'KT'], ['KT'])
                    stt('dve', kdst, e1, ppt[:, P_SEL + 1:P_SEL + 2], kdst, ALU.mult, ALU.add, ['exb', 'pp', 'KT'], ['KT'])
                    vdst = V2[:, NT % 8, :]
                    ts('dve', vdst, exb[:, 0, 256:512], ppt[:, P_SEL:P_SEL + 1], None, ALU.mult, None, ['exb', 'pp', 'V2'], ['V2'])
                    stt('dve', vdst, exb[:, 1, 256:512], ppt[:, P_SEL + 1:P_SEL + 2], vdst, ALU.mult, ALU.add, ['exb', 'pp', 'V2'], ['V2'])

            stop_cnt = {}

            def stop(tag):
                stop_cnt[tag] = stop_cnt.get(tag, 0) + 1
                if DEBUG_STOP[0] == tag or DEBUG_STOP[0] == '%s#%d' % (tag, stop_cnt[tag]):
                    raise _Stop()

            for l in range(NL):
                last = (l == NL - 1)
                S.dma('sp', bsb[:], bs_d[l:l + 1, :].partition_broadcast(128), writes=['bsb'], slot='c2')
                ft, fk = ftile()
                S.dma('sp', ft[:, 0:512], wst_d[l], writes=[fk], slot='c3')
                S.op('dve', lambda e, ft=ft: e.tensor_copy(out=wsT[:], in_=ft[:, 0:512]), [fk], ['wsT'])
                S.dma('sp', esink[:], sink_d[l:l + 1, :].partition_broadcast(128), writes=['esink'], slot='c4')
                act(esink[:], esink[:], AF.Exp, ['esink'], ['esink'])
                es3 = esink[:].rearrange("p (q two) -> p q two", two=2)
                S.op('dve', lambda e, es3=es3: e.tensor_copy(out=esink2[0:64, :], in_=es3[0:64, :, 0]), ['esink'], ['esink2'])
                S.op('dve', lambda e, es3=es3: e.tensor_copy(out=esink2[64:128, :], in_=es3[64:128, :, 1]), ['esink', 'esink2'], ['esink2'])
                S.op('pool', lambda e: e.memset(S1[:], 0.0), writes=['S1'])

                stop('consts')
                for g in range(NG):
                    load_group(l, g, halo=False)
                    rmsnorm([(xT, ['xT'], TG, 128, xn_keys(128))], SC_LN1 + l * 8, l)
                    S.dma('sp', s1s[l, g], S1[:].rearrange("p h v -> p (h v)"), reads=['S1'], writes=[('s1s', l, g)], slot='st1')
                    wb, wk = w_use(l, 'ai')
                    for t in range(4):
                        bank, bk = proj_tm(wb, wk, 0, 4, 128 + t * 128)
                        act(Vt[:, t, :], bank[:, :], AF.Copy, [bk], [K_V[t]])
                    for h in range(4):
                        hs = h % 2
                        wb, wk = w_use(l, 'h%d' % h)
                        zb, zk = proj_fm(wb, wk, 0, 128, TG)
                        hgrn_prep2(l, h, [0], {0: (zb, zk)}, None, None, hs, need_q=False)
                        for c in range(8):
                            ds_compute(0, h, hs, c)
                        for c in range(8):
                            e1_, eb_, e3_ = escal(hs, 0, c)
                            stt('dve', S1[:, h, :], S1[:, h, :], eb_, dSb[0][:, c, :], ALU.mult, ALU.add, ['S1', ('ech', hs, 0), ('dS', 0, c)], ['S1'])
                stop('pass1')
                exchange(l, 'S')
                stop('exS')

                for g in range(NG - 1, -1, -1):
                    has_lo = g > 0
                    has_hi = True
                    load_group(l, g, halo=True)
                    S.dma('sp', S1[:].rearrange("p h v -> p (h v)"), s1s[l, g], reads=[('s1s', l, g)], writes=['S1'], slot='ld1')
                    c_lo = g * TG - (128 if has_lo else 0)
                    ncs = TG + (128 if has_lo else 0)
                    o_lo = 0 if has_lo else 128
                    S.dma('sp', cosT[:, o_lo:640], rope_d[0, :, c_lo:c_lo + ncs], writes=['cos'], slot='lc')
                    S.dma('sp', sinT[:, o_lo:640], rope_d[1, :, c_lo:c_lo + ncs], writes=['sin'], slot='ls')
                    segs = []
                    if has_lo:
                        segs.append((xh, ['xh'], 128, 0, xn_keys(0)))
                    segs.append((xT, ['xT'], TG, 128, xn_keys(128)))
                    stop('p2load')
                    rmsnorm(segs, SC_LN1 + l * 8, l)
                    stop('p2norm')
                    wb, wk = w_use(l, 'ckv')
                    tiles = ([4 * g - 1] if has_lo else []) + [4 * g + i for i in range(4)]
                    for kvf in range(2):
                        if has_lo:
                            bank, bk = proj_fm(wb, wk, kvf, 0, 128)
                            rope(bank, bk, 0, 128, KT[:, kvf, ring(4 * g - 1):ring(4 * g - 1) + 128], ['KT'])
                        bank, bk = proj_fm(wb, wk, kvf, 128, TG)
                        rope(bank, bk, 128, TG, KT[:, kvf, ring(4 * g):ring(4 * g) + TG], ['KT'])
                    stop('krope')
                    for t in tiles:
                        c0 = 128 + (t - 4 * g) * 128
                        bank, bk = proj_tm(wb, wk, 2, 2, c0)
                        act(V2[:, t % 8, :], bank[:, 0:256], AF.Copy, [bk], ['V2'])
                    stop('kv')
                    if g == NG - 1:
                        exchange(l, 'K')
                    stop('exK')
                    wb, wk = w_use(l, 'ai')
                    for t in range(4):
                        bank, bk = proj_tm(wb, wk, 0, 4, 128 + t * 128)
                        act(Vt[:, t, :], bank[:, :], AF.Copy, [bk], [K_V[t]])
                    def hg_prep(h):
                            hs = h % 2
                            wb, wk = w_use(l, 'h%d' % h)
                            zq, zqk = proj_fm(wb, wk, 2, 128, TG)
                            zf = {}
                            for d in range(2):
                                zf[d] = proj_fm(wb, wk, d, 128, TG)
                            hgrn_prep2(l, h, [0, 1], zf, zq, zqk, hs, need_q=True)
                            zg, zgk = proj_fm(wb, wk, 3, 128, TG)
                            thg, kthg = ftile()
                            act(thg[:, 0:512], zg[:, :], AF.Tanh, [zgk], [kthg], scale=0.5)
                            sg, ksg = hv(hs, 6)
                            stt('dve', sg, thg[:, 0:512], 1.0, zg[:, :], ALU.add, ALU.mult, [kthg, zgk], [ksg])
                    def gates_blocks(js):
                        for j in js:
                            wb_, wk_ = w_use(l, 'gz%d' % j)
                            for fb in range(4):
                                bank, bk = proj_fm(wb_, wk_, fb, 128, TG)
                                act(big[:, j * 4 + fb, :], bank[:, :], AF.Sigmoid, [bk], [('big', j * 4 + fb)])

                    def hg_recur(h):
                        hs = h % 2
                        sg, ksg = hv(hs, 6)
                        for d in range(2):
                            for c in range(8):
                                ds_compute(d, h, hs, c)
                        chunk_sb(0, h, hs, 0, 0, S1, 'S1')
                        chunk_sb(1, h, hs, 0, 7, S2, 'S2')
                        for k in range(8):
                            chunk_at(0, h, hs, k, k)
                            chunk_at(1, h, hs, k, 7 - k)
                        for k in range(8):
                            state_step(0, h, hs, k, k, S1, 'S1')
                            state_step(1, h, hs, k, 7 - k, S2, 'S2')
                        if h < 3:
                            gates_blocks([2 * h, 2 * h + 1])
                        for k in range(8):
                            if k > 0:
                                chunk_sb(0, h, hs, k, k, S1, 'S1')
                                chunk_sb(1, h, hs, k, 7 - k, S2, 'S2')
                            chunk_fin(0, h, hs, k, k, ACC[0][0], ACC[0][1])
                            chunk_fin(1, h, hs, k, 7 - k, ACC[1][0], ACC[1][1])
                        o1, ko1 = ftile()
                        act(o1[:, 0:512], ACC[0][0][:, :], AF.Copy, [ACC[0][1]], [ko1])
                        o, ko = ftile()
                        tt('dve', o[:, 0:512], ACC[1][0][:, :], o1[:, 0:512], ALU.add, [ACC[1][1], ko1], [ko])
                        act(osq[:], o[:, 0:512], AF.Square, [ko], ['osq'])
                        bank, bk = rot()
                        mm(bank[:, :], [(ones_b, osq[:])], ['osq', 'cm'], [bk])
                        rs, krs = ftile()
                        rs0, krs0 = ftile()
                        ts('dve', rs0[:, 0:512], bank[:, :], float(128 * EPS), None, ALU.add, None, [bk], [krs0])
                        act(rs0[:, 0:512], rs0[:, 0:512], AF.Ln, [krs0], [krs0])
                        act(rs[:, 0:512], rs0[:, 0:512], AF.Exp, [krs0], [krs], scale=-0.5)
                        t_, kt_ = ftile()
                        tt('dve', t_[:, 0:512], o[:, 0:512], rs[:, 0:512], ALU.mult, [ko, krs], [kt_])
                        stt('dve', yaT[:, h, :], t_[:, 0:512], sc[:, SC_AN + l:SC_AN + l + 1], sg, ALU.mult, ALU.mult, [kt_, ksg] + SCK, [('ya', h)])
                    hg_prep(0)
                    for h in range(4):
                        if h + 1 < 4:
                            hg_prep(h + 1)
                        hg_recur(h)
                    stop('hgrn')
                    wb, wk = w_use(l, 'bu')
                    for fb in range(4):
                        bank, bk = proj_fm(wb, wk, fb, 128, TG)
                        act(uT[:, fb, :], bank[:, :], AF.Gelu_apprx_tanh, [bk], [K_UT[fb]])
                    wb, wk = w_use(l, 'bv')
                    for t in range(4):
                        bank, bk = proj_tm(wb, wk, 0, 4, 128 + t * 128)
                        v, kv = ftile()
                        act(v[:, 0:512], bank[:, :], AF.Gelu_apprx_tanh, [bk], [kv])
                        S.op('dve', lambda e, v=v: e.bn_stats(out=bnst[:, 0:6], in_=v[:, 0:512]), [kv], ['bnst'])
                        S.op('dve', lambda e: e.bn_aggr(out=bnst[:, 8:10], in_=bnst[:, 0:6]), ['bnst'], ['bnst2'])
                        ts('dve', bnst[:, 11:12], bnst[:, 9:10], float(EPS), None, ALU.add, None, ['bnst2'], ['bnst3'])
                        act(bnst[:, 11:12], bnst[:, 11:12], AF.Ln, ['bnst3'], ['bnst3'])
                        act(bnst[:, 10:11], bnst[:, 11:12], AF.Exp, ['bnst3'], ['bnst3'], scale=-0.5)
                        ts('dve', vn[:, t, :], v[:, 0:512], bnst[:, 8:9], bnst[:, 10:11], ALU.subtract, ALU.mult, [kv, 'bnst2', 'bnst3'], [K_VN[t]])
                    wsT3 = wsT[:].rearrange("p (g q) -> p g q", g=4)
                    for t in range(4):
                        bank, bk = rot()

                        def fn(e, bank=bank, t=t):
                            for gg in range(4):
                                r = e.matmul(bank[:, gg * 128:(gg + 1) * 128], lhsT=vn[:, t, gg * 128:(gg + 1) * 128], rhs=wsT3[:, gg, :], start=True, stop=True)
                            return r
                        S.op('pe', fn, [K_VN[t], 'wsT'], [bk])
                        mb, kmb = ftile()
                        tt('dve', mb[:, 0:512], bank[:, :], bsb[:], ALU.add, [bk, 'bsb'], [kmb])
                        tt('pool', ybT[:, :, t * 128:(t + 1) * 128], mb[:, 0:512].rearrange("p (g q) -> p g q", g=4), uT[:, :, t * 128:(t + 1) * 128], ALU.mult,
                           [kmb] + K_UT, [('yb', t)])
                    YBK = [('yb', t) for t in range(4)]
                    stop('gmlp')
                    wb, wk = w_use(l, 'cq')
                    for fb in range(4):
                        bank, bk = proj_fm(wb, wk, fb, 128, TG)
                        rope(bank, bk, 128, TG, QT[:, fb, :], [K_QT[fb]])
                    kts = ([4 * g - 1] if has_lo else []) + [4 * g + i for i in range(4)] + [4 * g + 4]
                    arot_i = [0]

                    def arot():
                        i = arot_i[0] % 4
                        arot_i[0] += 1
                        return psb[i], ('ps', i)
                    OD = [((psb[5], ('ps', 5)), (psb[6], ('ps', 6))), ((psb[7], ('ps', 7)), (psb[4], ('ps', 4)))]

                    def att_scores(qb, hh):
                        kvh = qb // 2
                        pr = slice(hh * 64, hh * 64 + 64)
                        inf = {}
                        for j, kt in enumerate(kts):
                            qlo, qhi = max(kt - 1, 4 * g), min(kt + 1, 4 * g + 3)
                            nq = qhi - qlo + 1
                            qc0 = (qlo - 4 * g) * 128
                            bank, bk = arot()
                            mlist = []
                            for qi in range(nq):
                                rel = (qlo + qi) - kt
                                if kt == NT:
                                    mlist.append((qi, NEGM['c']))
                                elif rel != 0:
                                    mlist.append((qi, NEGM[rel]))

                            def fn(e, bank=bank, pr=pr, kvh=kvh, kt=kt, qb=qb, qc0=qc0, nq=nq, mlist=mlist):
                                r = e.matmul(bank[:, 0:nq * 128], lhsT=KT[pr, kvh, ring(kt):ring(kt) + 128], rhs=QT[pr, qb, qc0:qc0 + nq * 128],
                                             start=True, stop=(len(mlist) == 0))
                                for ii, (qi, ng) in enumerate(mlist):
                                    r = e.matmul(bank[:, qi * 128:(qi + 1) * 128], lhsT=ident, rhs=ng, start=False, stop=(ii == len(mlist) - 1))
                                return r
                            S.op('pe', fn, ['KT', K_QT[qb], 'cm'], [bk])
                            act(PT[hh][j][:, 0:nq * 128], bank[:, 0:nq * 128], AF.Exp, [bk], K_PT[hh][j], scale=0.125)
                            inf[kt] = (j, qlo)
                        return inf

                    def att_pv(qb, hh, inf):
                        kvh = qb // 2
                        (obank, obk), (dbank, dbk) = OD[qb % 2]
                        pr = slice(hh * 64, hh * 64 + 64)
                        for qt in range(4 * g, 4 * g + 4):
                            qcs = slice((qt - 4 * g) * 128, (qt - 4 * g + 1) * 128)
                            use = [kt for kt in (qt - 1, qt, qt + 1) if kt in inf]
                            pairs_o, pairs_d, rk = [], [], []
                            for kt in use:
                                j, qlo = inf[kt]
                                p_ap = PT[hh][j][:, (qt - qlo) * 128:(qt - qlo + 1) * 128]
                                vc = kvh * 128 + hh * 64
                                pairs_o.append((V2[:, kt % 8, vc:vc + 64], p_ap))
                                pairs_d.append((ones_b[:, 0:64], p_ap))
                                rk += K_PT[hh][j]
                            mm(obank[pr, qcs], pairs_o, rk + ['V2'], [obk])
                            mm(dbank[pr, qcs], pairs_d, rk + ['cm'], [dbk])

                    def att_norm(qb):
                        (obank, obk), (dbank, dbk) = OD[qb % 2]
                        rd, krd = ftile()
                        ts('dve', rd[:, 0:512], dbank[:, :], esink2[:, qb:qb + 1], None, ALU.add, None, [dbk, 'esink2'], [krd])
                        S.op('dve', lambda e, rd=rd: e.reciprocal(out=rd[:, 0:512], in_=rd[:, 0:512]), [krd], [krd])
                        tt('dve', ycT[:, qb, :], obank[:, :], rd[:, 0:512], ALU.mult, [obk, krd], [('yc', qb, 0)])

                    units = [(qb, hh) for qb in range(4) for hh in range(2)]
                    infos = {units[0]: att_scores(*units[0])}
                    for ui, u in enumerate(units):
                        if ui + 1 < len(units):
                            infos[units[ui + 1]] = att_scores(*units[ui + 1])
                        att_pv(u[0], u[1], infos[u])
                        if u[1] == 1:
                            att_norm(u[0])
                    YCK = [('yc', qb, 0) for qb in range(4)]
                    YAK = [('ya', h) for h in range(4)]
                    stop('attn')
                    stop('gates')
                    ysrc = [(yaT, YAK), (ybT, YBK), (ycT, YCK)]
                    for j in range(4):
                        wb, wk = w_use(l, 'br%d' % j)
                        wv = wb[:, 0:3072].rearrange("p (b k c) -> p b k c", b=3, k=4)
                        for o2 in range(2):
                            ob = 2 * j + o2
                            tl_ = []
                            for br in range(3):
                                ysb, ykeys = ysrc[br]
                                bank, bk = rot()
                                mm(bank[:, :], [(wv[:, br, kc, o2 * 128:(o2 + 1) * 128], ysb[:, kc, :]) for kc in range(4)], [wk] + ykeys, [bk])
                                tf, ktf = ftile()
                                tt('dve', tf[:, 0:512], bank[:, :], big[:, br * 8 + ob, :], ALU.mult, [bk, ('big', br * 8 + ob)], [ktf])
                                tl_.append((tf, ktf))
                            tt('pool', tl_[0][0][:, 0:512], tl_[0][0][:, 0:512], tl_[1][0][:, 0:512], ALU.add, [tl_[0][1], tl_[1][1]], [tl_[0][1]])
                            tt('pool', big[:, 24 + ob, :], tl_[0][0][:, 0:512], tl_[2][0][:, 0:512], ALU.add, [tl_[0][1], tl_[2][1]], [('big', 24 + ob)])
                    stop('merge')
                    for j in range(2):
                        wb, wk = w_use(l, 'wo%d' % j)
                        wv = wb[:].rearrange("p (k c) -> p k c", k=8)
                        for o4 in range(4):
                            ob = j * 4 + o4
                            bank, bk = rot()
                            mm(bank[:, :], [(wv[:, kc, o4 * 128:(o4 + 1) * 128], big[:, 24 + kc, :]) for kc in range(KC)],
                               [wk] + [('big', 24 + kc) for kc in range(KC)], [bk])
                            tt('dve', xT[:, ob, :], xT[:, ob, :], bank[:, :], ALU.add, [bk, 'xT'], ['xT'])
                    stop('wout')
                    rmsnorm([(xT, ['xT'], TG, 128, xn_keys(128))], SC_LN2 + l * 8, l)
                    ri = 0
                    for j in range(8):
                        wb, wk = w_use(l, 'up%d' % j)
                        for fb in range(4):
                            bank, bk = proj_fm(wb, wk, fb, 128, TG)
                            r = rl[ri % 2]
                            rk_ = ('rl', ri % 2)
                            ri += 1
                            act(r[:], bank[:, :], AF.Relu, [bk], [rk_])
                            tt('pool', big[:, j * 4 + fb, :], r[:], r[:], ALU.mult, [rk_], [('big', j * 4 + fb)])
                    for ob in range(8):
                        wb, wk = w_use(l, 'dn%d' % ob)
                        wv = wb[:].rearrange("p (k c) -> p k c", k=32)
                        bank, bk = rot()
                        mm(bank[:, :], [(wv[:, kc, :], big[:, kc, :]) for kc in range(32)], [wk] + BIGK, [bk])
                        tt('dve', xT[:, ob, :], xT[:, ob, :], bank[:, :], ALU.add, [bk, 'xT'], ['xT'])
                    stop('ffn')
                    if last:
                        rmsnorm([(xT, ['xT'], TG, 128, None)], SC_FN, l, out_f32=True)
                        S.dma('sp', yout[:, :, g * TG:(g + 1) * TG].rearrange("k p t -> p k t"), xT[:], reads=['xT'], writes=[('y', g)], slot='sto')
                    else:
                        S.dma('sp', xs[l][:, :, g * TG:(g + 1) * TG].rearrange("k p t -> p k t"), xT[:], reads=['xT'], writes=[('xs', l, g)], slot='sto')
        try:
            main_body()
        except _Stop:
            S.dma('sp', yout[:, :, 0:TG].rearrange("k p t -> p k t"), xT[:], reads=['xT'], writes=[('y', 0)], slot='sto')
        S.final_wait('sp', ['sto'])
        print("ops:", {e: len(S.lists[e]) for e in S.ENGS})
        with nc.Block() as block:
            S.emit(block)
    return nc


def _consts():
    r = np.arange(128)
    ident = np.eye(128, dtype=np.float32)
    pm = np.zeros((128, 128), np.float32)
    for c in range(128):
        d = c % 64
        if d < 8:
            pm[c + 8, c] = 1.0
        elif d < 16:
            pm[c - 8, c] = 1.0
    m = r[:, None]
    a = r[None, :]
    m3 = np.concatenate([(m <= a), np.ones((128, 128), bool), (m >= a)], axis=1).astype(np.float32)
    mc = ((m + a) >= 127).astype(np.float32)
    s = (r % 64)[:, None]
    t = np.arange(64)[None, :]
    mh1 = (s <= t).astype(np.float32)
    mh2 = (s >= t).astype(np.float32)
    ones = np.ones((128, 128), np.float32)
    NEG = np.float32(-30000.0)
    nb_ = np.where(m <= a, 0.0, NEG).astype(np.float32)
    na_ = np.where(m >= a, 0.0, NEG).astype(np.float32)
    nc_ = np.where((m + a) >= 127, 0.0, NEG).astype(np.float32)
    cm = np.concatenate([ident, pm, m3, mc, mh1, mh2, ones, nb_, na_, nc_], axis=1)
    assert cm.shape[1] == C_END
    return cm.astype(ml_dtypes.bfloat16)


def _rope_tables(pos):
    inv = np.float32(500000.0) ** (-(np.arange(8, dtype=np.float32) * np.float32(2.0 / 16)))
    ang = pos.astype(np.float32)[:, None] * inv[None, :]
    cos = np.cos(ang).astype(np.float32)
    sin = np.sin(ang).astype(np.float32)
    T = pos.shape[0]
    C = np.ones((128, T), np.float32)
    Sn = np.zeros((128, T), np.float32)
    for rr in range(128):
        d = rr % 64
        if d < 8:
            C[rr] = cos[:, d]
            Sn[rr] = -sin[:, d]
        elif d < 16:
            C[rr] = cos[:, d - 8]
            Sn[rr] = sin[:, d - 8]
    return np.stack([C, Sn], 0)


def _wblocks(w_in, w_br, w_out, w_up, w_down, half):
    NL = w_in.shape[0]
    out = np.zeros((NL, NB, 128, 4096), np.float32)

    def fbk(W, cols):
        return W[:, cols].reshape(8, 128, 128).transpose(1, 0, 2)

    def blk(W, colsets):
        return np.stack([fbk(W, c) for c in colsets], 1).reshape(128, 4096)
    ar = np.arange
    for l in range(NL):
        W = w_in[l]
        f1o, f2o = (1024, 1536) if half == 0 else (1536, 1024)
        out[l, BI['ai']] = blk(W, [512 + h * 128 + ar(128) for h in range(4)])
        for h in range(4):
            out[l, BI['h%d' % h]] = blk(W, [f1o + h * 128 + ar(128), f2o + h * 128 + ar(128), h * 128 + ar(128), 2048 + h * 128 + ar(128)])
        out[l, BI['bu']] = blk(W, [2560 + f * 128 + ar(128) for f in range(4)])
        out[l, BI['bv']] = blk(W, [3072 + f * 128 + ar(128) for f in range(4)])
        out[l, BI['cq']] = blk(W, [3584 + f * 128 + ar(128) for f in range(4)])
        k0, k1 = 4096 + ar(64), 4160 + ar(64)
        v0, v1 = 4224 + ar(64), 4288 + ar(64)
        cc = np.concatenate
        out[l, BI['ckv']] = blk(W, [cc([k0, k0]), cc([k1, k1]), cc([v0, v0]), cc([v1, v1])])
        for j in range(6):
            out[l, BI['gz%d' % j]] = blk(W, [4352 + (j * 4 + f) * 128 + ar(128) for f in range(4)])
        for j in range(4):
            a = w_br[l][:, :, j * 256:(j + 1) * 256].reshape(3, 4, 128, 256).transpose(2, 0, 1, 3)
            out[l, BI['br%d' % j], :, 0:3072] = a.reshape(128, 3072)
        for j in range(2):
            a = w_out[l][:, j * 512:(j + 1) * 512].reshape(8, 128, 512).transpose(1, 0, 2)
            out[l, BI['wo%d' % j]] = a.reshape(128, 4096)
        for j in range(8):
            out[l, BI['up%d' % j]] = blk(w_up[l], [j * 512 + f * 128 + ar(128) for f in range(4)])
        for ob in range(8):
            a = w_down[l][:, ob * 128:(ob + 1) * 128].reshape(32, 128, 128).transpose(1, 0, 2)
            out[l, BI['dn%d' % ob]] = a.reshape(128, 4096)
    return out


_PROG_CACHE = {}


def kernel(x, w_in, ln1, lb_logits, a_norm, w_s, b_s, sink, w_br, w_out, ln2, w_up, w_down, final_norm):
    x = np.asarray(x, np.float32)
    f = lambda a: np.asarray(a, np.float32)
    w_in, ln1, lb_logits, a_norm, w_s, b_s, sink = map(f, (w_in, ln1, lb_logits, a_norm, w_s, b_s, sink))
    w_br, w_out, ln2, w_up, w_down, final_norm = map(f, (w_br, w_out, ln2, w_up, w_down, final_norm))
    B, SEQ, _ = x.shape
    NL = w_in.shape[0]
    TOK = SEQ // 2
    assert B * 2 == NCORES and TOK % TG == 0
    key = (TOK, NL)
    if key not in _PROG_CACHE:
        _PROG_CACHE[key] = build_program(TOK, NL, SEQ)
    nc = _PROG_CACHE[key]
    cm = _consts()
    in_maps = []
    variants = {}
    for half in range(2):
        wsrc = _wblocks(w_in, w_br, w_out, w_up, w_down, half)
        pos = np.arange(TOK) if half == 0 else (SEQ - 1 - np.arange(TOK))
        rope = _rope_tables(pos)
        pp = np.zeros((128, P_END), np.float32)
        for l in range(NL):
            pp[:, P_LN1 + l * 8:P_LN1 + l * 8 + 8] = ln1[l].reshape(8, 128).T
            pp[:, P_LN2 + l * 8:P_LN2 + l * 8 + 8] = ln2[l].reshape(8, 128).T
            for d in range(2):
                dirn = d if half == 0 else 1 - d
                pp[:, P_LBL + l * 8 + d * 4:P_LBL + l * 8 + d * 4 + 4] = lb_logits[l, dirn].reshape(4, 128).T
            pp[:, P_AN + l] = a_norm[l]
        pp[:, P_FN:P_FN + 8] = final_norm.reshape(8, 128).T
        pp[:, P_SEL] = 0.0 if half == 0 else 1.0
        pp[:, P_SEL + 1] = 1.0 if half == 0 else 0.0
        if half == 0:
            wst = np.ascontiguousarray(w_s.transpose(0, 3, 1, 2)).reshape(NL, 128, 512)
            bsv = b_s.reshape(NL, 512)
        else:
            wst = np.ascontiguousarray(w_s[:, :, ::-1, ::-1].transpose(0, 3, 1, 2)).reshape(NL, 128, 512)
            bsv = np.ascontiguousarray(b_s[:, :, ::-1]).reshape(NL, 512)
        variants[half] = dict(wsrc=wsrc, pp=pp, cm=cm, bsv=np.ascontiguousarray(bsv), wst=wst,
                              sinkv=np.ascontiguousarray(sink), rope=rope)
    for c in range(NCORES):
        b, half = c // 2, c % 2
        xs_ = x[b, :TOK] if half == 0 else x[b, TOK:][::-1]
        xin = np.ascontiguousarray(xs_.T).reshape(KC, 128, TOK)
        m = dict(variants[half])
        m['xin'] = xin
        in_maps.append(m)
    res = run_bass_kernel_spmd(nc, in_maps, core_ids=list(range(NCORES)))
    y = np.empty((B, SEQ, D), np.float32)
    for c in range(NCORES):
        b, half = c // 2, c % 2
        yt = np.asarray(res.results[c]["yout"]).reshape(D, TOK).T
        if half == 0:
            y[b, :TOK] = yt
        else:
            y[b, TOK:] = yt[::-1]
    return y
```

```python
import contextlib
import numpy as np
import ml_dtypes
import concourse.bass as bass
import concourse.mybir as mybir
from concourse.bass_utils import run_bass_kernel_spmd

F32 = mybir.dt.float32
BF16 = mybir.dt.bfloat16
AF = mybir.ActivationFunctionType
ALU = mybir.AluOpType

D = 1024
KC = 8
TG = 512
EPS = 1e-6
NCORES = 8
NWB = 3
BLK = ['ai', 'h0', 'h1', 'h2', 'h3', 'bu', 'bv', 'cq', 'ckv'] + ['gz%d' % i for i in range(6)] + \
      ['br%d' % i for i in range(4)] + ['wo0', 'wo1'] + ['up%d' % i for i in range(8)] + ['dn%d' % i for i in range(8)]
BI = {n: i for i, n in enumerate(BLK)}
NB = len(BLK)
C_ID, C_PM, C_M3, C_MC, C_MH1, C_MH2, C_ONE, C_NB, C_NA, C_NC, C_END = 0, 128, 256, 640, 768, 832, 896, 1024, 1152, 1280, 1408
P_LN1, P_LN2, P_FN, P_LBL, P_AN, P_SEL, P_END = 0, 16, 32, 40, 56, 58, 60


class Sched:
    ENGS = ('pe', 'act', 'dve', 'pool', 'sp')

    def __init__(self, nc, stack):
        self.nc = nc
        self.stack = stack
        self.lists = {e: [] for e in self.ENGS}
        self.cnt = {}
        self.sem = {}
        self.waited = {e: {} for e in self.ENGS}
        self.lastw = {}
        self.readers = {}
        for e in self.ENGS[:4]:
            self._mksem(e)

    def _mksem(self, name):
        if name not in self.sem:
            self.sem[name] = self.stack.enter_context(self.nc.semaphore("s_" + name))
            self.cnt[name] = 0

    def _deps(self, eng, reads, writes):
        deps = {}
        for b in reads:
            w = self.lastw.get(b)
            if w:
                deps[w[0]] = max(deps.get(w[0], 0), w[1])
            if isinstance(b, tuple) and b[0] == 'ps':
                for s, v in self.readers.get(b, {}).items():
                    if s != eng:
                        deps[s] = max(deps.get(s, 0), v)
        for b in writes:
            w = self.lastw.get(b)
            if w:
                deps[w[0]] = max(deps.get(w[0], 0), w[1])
            for s, v in self.readers.get(b, {}).items():
                deps[s] = max(deps.get(s, 0), v)
        waits = []
        for s, v in deps.items():
            if eng == 'pe' and s == 'pe':
                continue
            if self.waited[eng].get(s, 0) < v:
                waits.append((s, v))
                self.waited[eng][s] = v
        return waits

    def _record(self, semname, val, reads, writes):
        for b in reads:
            r = self.readers.setdefault(b, {})
            r[semname] = max(r.get(semname, 0), val)
        for b in writes:
            self.lastw[b] = (semname, val)
            self.readers[b] = {}

    def op(self, eng, fn, reads=(), writes=()):
        waits = self._deps(eng, reads, writes)
        self.cnt[eng] += 1
        self.lists[eng].append((waits, fn, eng, 1))
        self._record(eng, self.cnt[eng], reads, writes)

    def dma(self, q, out, in_, reads=(), writes=(), slot=None):
        self._mksem(slot)
        waits = self._deps(q, reads, writes)
        self.cnt[slot] += 16
        self.lists[q].append((waits, lambda e: e.dma_start(out=out, in_=in_), slot, 16))
        self._record(slot, self.cnt[slot], reads, writes)

    def coll(self, fn, reads=(), writes=(), slot=None):
        self._mksem(slot)
        assert self.cnt[slot] == 0
        waits = self._deps('pool', reads, writes)
        self.cnt[slot] = 1
        self.lists['pool'].append((waits, fn, slot, None))
        self._record(slot, 1, reads, writes)

    def final_wait(self, q, slots):
        waits = [(s, self.cnt[s]) for s in slots if self.cnt.get(s, 0) > 0]
        self.lists[q].append((waits, None, None, 0))

    def emit(self, block):
        sem = self.sem

        def run(lst):
            def body(e):
                for waits, fn, semname, inc in lst:
                    for s, v in waits:
                        e.wait_ge(sem[s], v)
                    if fn is not None:
                        inst = fn(e)
                        if inc is None:
                            inst.then_inc(sem[semname])
                        else:
                            inst.then_inc(sem[semname], inc)
            return body
        block.tensor(run(self.lists['pe']))
        block.scalar(run(self.lists['act']))
        block.vector(run(self.lists['dve']))
        block.gpsimd(run(self.lists['pool']))
        block.sync(run(self.lists['sp']))


class _Stop(Exception):
    pass


DEBUG_STOP = [None]


def build_program(TOK, NL, S_FULL):
    NG = TOK // TG
    NT = TOK // 128
    nc = bass.Bass("TRN2", target_bir_lowering=False)
    xin = nc.dram_tensor("xin", [KC, 128, TOK], F32, kind="ExternalInput").ap()
    wsrc = nc.dram_tensor("wsrc", [NL, NB, 128, 4096], F32, kind="ExternalInput").ap()
    pp_d = nc.dram_tensor("pp", [128, P_END], F32, kind="ExternalInput").ap()
    cm_d = nc.dram_tensor("cm", [128, C_END], BF16, kind="ExternalInput").ap()
    bs_d = nc.dram_tensor("bsv", [NL, 512], F32, kind="ExternalInput").ap()
    wst_d = nc.dram_tensor("wst", [NL, 128, 512], F32, kind="ExternalInput").ap()
    sink_d = nc.dram_tensor("sinkv", [NL, 8], F32, kind="ExternalInput").ap()
    rope_d = nc.dram_tensor("rope", [2, 128, TOK], F32, kind="ExternalInput").ap()
    yout = nc.dram_tensor("yout", [KC, 128, TOK], F32, kind="ExternalOutput").ap()
    wbf = nc.dram_tensor("wbf", [NL, NB, 128, 4096], BF16).ap()
    xs = [nc.dram_tensor("xs%d" % l, [KC, 128, TOK], F32).ap() for l in range(max(NL - 1, 1))]
    s1s = nc.dram_tensor("s1s", [NL, NG, 128, 512], F32).ap()
    exS_i = [nc.dram_tensor("exSi%d" % l, [128, 512], F32) for l in range(NL)]
    exS_o = [nc.dram_tensor("exSo%d" % l, [256, 512], F32) for l in range(NL)]
    exK_i = [nc.dram_tensor("exKi%d" % l, [128, 512], BF16) for l in range(NL)]
    exK_o = [nc.dram_tensor("exKo%d" % l, [256, 512], BF16) for l in range(NL)]

    st = contextlib.ExitStack()
    with st:
        S = Sched(nc, st)

        def sb(name, shape, dt):
            return st.enter_context(nc.sbuf_tensor(name, shape, dt))

        cm = sb("cm_s", [128, C_END], BF16)
        ppt = sb("pp_s", [128, P_END], F32)
        sc = sb("sc_s", [128, 96], F32)
        bsb = sb("bsb", [128, 512], F32)
        wsT = sb("wsT", [128, 512], BF16)
        esink = sb("esink", [128, 8], F32)
        esink2 = sb("esink2", [128, 4], F32)
        dSb = [sb("dSb%d" % i, [128, 8, 128], F32) for i in range(2)]
        xT = sb("xT", [128, KC, TG], F32)
        xh = sb("xh", [128, KC, 128], F32)
        sq = [sb("sq%d" % i, [128, 640], BF16) for i in range(2)]
        rstd = sb("rstd", [128, 640], F32)
        xnT = sb("xnT", [128, KC, 640], BF16)
        cosT = sb("cosT", [128, 640], F32)
        sinT = sb("sinT", [128, 640], F32)
        KT = sb("KT", [128, 2, 1024], BF16)
        V2 = sb("V2", [128, 8, 256], BF16)
        NF = 9
        Fp = [sb("F%d" % i, [128, 640], F32) for i in range(NF)]
        Lx = [sb("Lx%d" % i, [128, 576], F32) for i in range(2)]
        esub = sb("esub", [128, 2, 24], F32)
        ech = [sb("ech%d" % i, [128, 2, 24], F32) for i in range(2)]
        TRW = 9216
        TR = sb("TR", [128, TRW], BF16)
        S1 = sb("S1", [128, 4, 128], F32)
        S2 = sb("S2", [128, 4, 128], F32)
        Sb = [sb("Sb%d" % i, [128, 128], BF16) for i in range(16)]
        At = [sb("At%d" % i, [128, 64], BF16) for i in range(16)]
        Sall = [sb("Sall%d" % i, [128, 7, 128], F32) for i in range(2)]
        osq = sb("osq", [128, 512], BF16)
        yaT = sb("yaT", [128, 4, TG], BF16)
        ybT = sb("ybT", [128, 4, TG], BF16)
        ycT = sb("ycT", [128, 4, TG], BF16)
        exb = sb("exb", [128, 2, 512], BF16)
        exs = sb("exs", [128, 2, 512], F32)
        cst = [sb("cst%d" % i, [128, 512], BF16) for i in range(2)]
        big = sb("big", [128, 32, TG], BF16)
        rl = [sb("rl%d" % i, [128, 512], BF16) for i in range(2)]
        bnst = sb("bnst", [128, 16], F32)
        wbuf = [sb("wb%d" % i, [128, 4096], BF16) for i in range(NWB)]
        psb = [st.enter_context(nc.psum_tensor("ps%d" % i, [128, 512], F32)) for i in range(8)]
        print("sbuf bytes remaining:", nc.sbuf_bytes_remaining)

        def trk(i):
            return ('TR', i)
        Vt = TR[:, 0:2048].rearrange("p (t c) -> p t c", t=4)
        def hv(s, j):
            o = 2048 + (s * 7 + j) * 512
            return TR[:, o:o + 512], trk(4 + s * 7 + j)
        uT = TR[:, 0:2048].rearrange("p (f t) -> p f t", f=4)
        vn = TR[:, 2048:4096].rearrange("p (t c) -> p t c", t=4)
        QT = TR[:, 0:2048].rearrange("p (f t) -> p f t", f=4)
        qraw = TR[:, 2048:2688]
        PTt = [TR[:, 3072 + i * 512:3072 + i * 512 + 384] for i in range(2)]
        PT = [[TR[:, 4096 + (hh_ * 6 + i) * 384:4096 + (hh_ * 6 + i + 1) * 384] for i in range(6)] for hh_ in range(2)]
        K_V = [trk(i) for i in range(4)]
        K_UT = [trk(i) for i in range(4)]
        K_VN = [trk(4 + i) for i in range(4)]
        K_QT = [trk(i) for i in range(4)]
        K_QRAW = [trk(4), trk(5)]
        K_PTT = [trk(6), trk(7)]
        K_PT = [[[trk(8 + ((hh_ * 6 + i) * 384) // 512), trk(8 + ((hh_ * 6 + i + 1) * 384 - 1) // 512)] for i in range(6)] for hh_ in range(2)]

        ident = cm[:, C_ID:C_ID + 128]
        Pm = cm[:, C_PM:C_PM + 128]
        M3 = cm[:, C_M3:C_M3 + 384]
        MC = cm[:, C_MC:C_MC + 128]
        MH = [cm[:, C_MH1:C_MH1 + 64], cm[:, C_MH2:C_MH2 + 64]]
        ones_b = cm[:, C_ONE:C_ONE + 128]
        NEGM = {-1: cm[:, C_NB:C_NB + 128], 1: cm[:, C_NA:C_NA + 128], 'c': cm[:, C_NC:C_NC + 128]}
        SC_LN1, SC_LN2, SC_FN, SC_AN = 0, 16, 32, 40
        SC_HB, SC_HA, SC_NHB = 44, 60, 76
        SC_ONE, SC_ZERO = 92, 93

        rot_i = [0]

        def rot():
            i = rot_i[0] % 5
            rot_i[0] += 1
            return psb[i], ('ps', i)
        ACC = [(psb[5], ('ps', 5)), (psb[6], ('ps', 6))]
        TPB = psb[7][:].bitcast(BF16)
        K_TP = ('ps', 7)
        f_i = [0]

        def ftile():
            i = f_i[0] % NF
            f_i[0] += 1
            return Fp[i], ('F', i)

        def mm(out, pairs, reads, writes):
            def fn(e, out=out, pairs=pairs):
                n = len(pairs)
                for i, (l, r) in enumerate(pairs):
                    inst = e.matmul(out, lhsT=l, rhs=r, start=(i == 0), stop=(i == n - 1))
                return inst
            S.op('pe', fn, reads, writes)

        def act(out, in_, func, reads, writes, **kw):
            S.op('act', lambda e: e.activation(out=out, in_=in_, func=func, **kw), reads, writes)

        def tt(eng, out, in0, in1, op, reads, writes):
            S.op(eng, lambda e: e.tensor_tensor(out=out, in0=in0, in1=in1, op=op), reads, writes)

        def ts(eng, out, in0, s1, s2, op0, op1, reads, writes):
            if s2 is None:
                S.op(eng, lambda e: e.tensor_scalar(out=out, in0=in0, scalar1=s1, scalar2=None, op0=op0), reads, writes)
            else:
                S.op(eng, lambda e: e.tensor_scalar(out=out, in0=in0, scalar1=s1, scalar2=s2, op0=op0, op1=op1), reads, writes)

        def stt(eng, out, in0, scalar, in1, op0, op1, reads, writes):
            S.op(eng, lambda e: e.scalar_tensor_tensor(out=out, in0=in0, scalar=scalar, in1=in1, op0=op0, op1=op1), reads, writes)

        S.dma('sp', cm[:], cm_d[:, :], writes=['cm'], slot='c0')
        S.dma('sp', ppt[:], pp_d[:, :], writes=['pp'], slot='c1')
        S.op('pool', lambda e: e.memset(sc[:, SC_ONE:SC_ONE + 1], 1.0), writes=['sc1'])
        S.op('pool', lambda e: e.memset(sc[:, SC_ZERO:SC_ZERO + 1], 0.0), writes=['sc1'])
        for i in range(2):
            S.op('pool', lambda e, i=i: e.memset(Lx[i][:], 0.0), writes=[('Lx', i)])
        sD = float(np.sqrt(D))
        ts('dve', sc[:, SC_LN1:SC_LN1 + 16], ppt[:, P_LN1:P_LN1 + 16], sD, None, ALU.mult, None, ['pp'], ['sc'])
        ts('dve', sc[:, SC_LN2:SC_LN2 + 16], ppt[:, P_LN2:P_LN2 + 16], sD, None, ALU.mult, None, ['pp'], ['sc'])
        ts('dve', sc[:, SC_FN:SC_FN + 8], ppt[:, P_FN:P_FN + 8], sD, None, ALU.mult, None, ['pp'], ['sc'])
        ts('dve', sc[:, SC_AN:SC_AN + 2], ppt[:, P_AN:P_AN + 2], float(np.sqrt(128.0) * 0.5), None, ALU.mult, None, ['pp'], ['sc'])
        lbt = sc[:, SC_HB:SC_HB + 16]
        S.op('pool', lambda e: e.memset(sc[:, SC_HB:SC_HB + 16], 0.0), writes=['sc2'])
        if NL > 1:
            tt('dve', sc[:, SC_HA + 8:SC_HA + 16], ppt[:, P_LBL + 8:P_LBL + 16], ppt[:, P_LBL:P_LBL + 8], ALU.subtract, ['pp'], ['sc3'])
            act(sc[:, SC_HB + 8:SC_HB + 16], sc[:, SC_HA + 8:SC_HA + 16], AF.Sigmoid, ['sc3', 'sc2'], ['sc2'])
        ts('dve', sc[:, SC_HA:SC_HA + 16], sc[:, SC_HB:SC_HB + 16], -0.5, 0.5, ALU.mult, ALU.add, ['sc2', 'sc3'], ['sc3'])
        ts('dve', sc[:, SC_NHB:SC_NHB + 16], sc[:, SC_HA:SC_HA + 16], -1.0, None, ALU.mult, None, ['sc3'], ['sc4'])
        ts('dve', sc[:, SC_HB:SC_HB + 16], sc[:, SC_HB:SC_HB + 16], 0.5, 0.5, ALU.mult, ALU.add, ['sc2', 'sc3'], ['sc2'])
        SCK = ['sc', 'sc1', 'sc2', 'sc3', 'sc4']

        stage = big[:].rearrange("p a b -> p (a b)").bitcast(F32)
        cast_engs = ['act', 'dve', 'dve']
        ci = 0
        def main_body():
            nonlocal ci
            for l in (range(1) if NL == 2 else range(NL)):
                for bi in range(NB):
                    s2 = ci % 2
                    w = ci % NWB
                    S.dma('sp', stage[:, s2 * 4096:(s2 + 1) * 4096], wsrc[l, bi], writes=[('stg', s2)], slot='pg%d' % s2)
                    eng = cast_engs[ci % 3]
                    if eng == 'act':
                        act(wbuf[w][:], stage[:, s2 * 4096:(s2 + 1) * 4096], AF.Copy, [('stg', s2)], [('wb', w)])
                    else:
                        S.op(eng, lambda e, w=w, s2=s2: e.tensor_copy(out=wbuf[w][:], in_=stage[:, s2 * 4096:(s2 + 1) * 4096]),
                             [('stg', s2)], [('wb', w)])
                    S.dma('sp', wbf[l, bi], wbuf[w][:], reads=[('wb', w)], writes=[('wbf', l, bi)], slot='pw%d' % w)
                    ci += 1
            BIGK = [('big', i) for i in range(32)]
            S.op('pool', lambda e: e.memset(sc[:, SC_ZERO:SC_ZERO + 1], 0.0), writes=BIGK + ['sc1', ('stg', 0), ('stg', 1)])

            cast1_steps = [(1, bi, e8) for bi in range(NB) for e8 in range(8)] if NL == 2 else []
            c1 = {'in': 0, 'done': 0, 'on': False}

            def cast1_in():
                i = c1['in']
                if i >= len(cast1_steps):
                    return
                l1, bi, e8 = cast1_steps[i]
                s_ = i % 2
                S.dma('pool', exs[:, s_, :], wsrc[l1, bi, :, e8 * 512:(e8 + 1) * 512], writes=[('exs', s_)], slot='cg%d' % s_)
                c1['in'] += 1

            def cast1_step():
                i = c1['done']
                if i >= len(cast1_steps):
                    return
                if c1['in'] == i:
                    cast1_in()
                cast1_in()
                l1, bi, e8 = cast1_steps[i]
                s_ = i % 2
                S.op('pool', lambda e, s_=s_: e.tensor_copy(out=cst[s_][:], in_=exs[:, s_, :]), [('exs', s_)], [('cst', s_)])
                S.dma('pool', wbf[l1, bi, :, e8 * 512:(e8 + 1) * 512], cst[s_][:], reads=[('cst', s_)], writes=[('wbfp', l1, bi, e8)], slot='co%d' % s_)
                c1['done'] += 1

            def wbf_keys(l, bi, part):
                if NL == 2 and l == 1:
                    return [('wbfp', l, bi, e8) for e8 in range(2 if part else 8)]
                return [('wbf', l, bi)]

            seq = []
            for l in range(NL):
                for g in range(NG):
                    seq += [(l, 'ai', False)] + [(l, 'h%d' % h, True) for h in range(4)]
                for g in range(NG - 1, -1, -1):
                    seq += [(l, n, False) for n in ['ckv', 'ai', 'h0', 'h1', 'gz0', 'gz1', 'h2', 'gz2', 'gz3', 'h3', 'gz4', 'gz5', 'bu', 'bv', 'cq'] +
                            ['br%d' % i for i in range(4)] + ['wo0', 'wo1'] +
                            ['up%d' % i for i in range(8)] + ['dn%d' % i for i in range(8)]]
            ws = {'issued': 0, 'used': 0}

            def w_issue():
                k = ws['issued']
                if k >= len(seq):
                    return
                l, n, part = seq[k]
                slot = k % NWB
                bi = BI[n]
                if part:
                    S.dma('sp', wbuf[slot][:, 0:1024], wbf[l, bi, :, 0:1024], reads=wbf_keys(l, bi, True), writes=[('wb', slot)], slot='w%d' % slot)
                else:
                    S.dma('sp', wbuf[slot][:], wbf[l, bi], reads=wbf_keys(l, bi, False), writes=[('wb', slot)], slot='w%d' % slot)
                ws['issued'] += 1

            def w_use(l, n):
                k = ws['used']
                assert seq[k][0] == l and seq[k][1] == n, (seq[k], l, n)
                while ws['issued'] < min(k + NWB, len(seq)):
                    w_issue()
                ws['used'] += 1
                slot = k % NWB
                if c1['on']:
                    cast1_step()
                return wbuf[slot], ('wb', slot)

            def wv_f(wb):
                return wb[:].rearrange("p (f k c) -> p f k c", f=4, k=8)

            def rmsnorm(segs, lncol, l, out_f32=False):
                for (x3, xkeys, n, c0, okeys) in segs:
                    bank, bk = rot()
                    for kc in range(KC):
                        s = sq[kc % 2]
                        act(s[:, 0:n], x3[:, kc, :], AF.Square, xkeys, [('sq', kc % 2)])
                        S.op('pe', lambda e, s=s, n=n, kc=kc, bank=bank: e.matmul(bank[:, 0:n], lhsT=ones_b, rhs=s[:, 0:n], start=(kc == 0), stop=(kc == KC - 1)),
                             [('sq', kc % 2), 'cm'], [bk])
                    tmpf, ktmp = ftile()
                    ts('dve', tmpf[:, 0:n], bank[:, 0:n], float(D * EPS), None, ALU.add, None, [bk], [ktmp])
                    act(tmpf[:, 0:n], tmpf[:, 0:n], AF.Ln, [ktmp], [ktmp])
                    act(rstd[:, c0:c0 + n], tmpf[:, 0:n], AF.Exp, [ktmp], [('rstd', c0)], scale=-0.5)
                    for kc in range(KC):
                        eng = 'dve'
                        if out_f32:
                            stt(eng, x3[:, kc, :], x3[:, kc, :], sc[:, lncol + kc:lncol + kc + 1], rstd[:, c0:c0 + n], ALU.mult, ALU.mult,
                                [('rstd', c0)] + SCK + xkeys, xkeys)
                        else:
                            stt(eng, xnT[:, kc, c0:c0 + n], x3[:, kc, :], sc[:, lncol + kc:lncol + kc + 1], rstd[:, c0:c0 + n], ALU.mult, ALU.mult,
                                [('rstd', c0)] + SCK + xkeys, okeys)

            def xn_keys(c0):
                return [('xn', c0)]

            def proj_fm(wb, wk, fb, c0, n):
                bank, bk = rot()
                wv = wv_f(wb)
                mm(bank[:, 0:n], [(wv[:, fb, kc, :], xnT[:, kc, c0:c0 + n]) for kc in range(KC)],
                   [wk, ('xn', 0), ('xn', 128)], [bk])
                return bank, bk

            def proj_tm(wb, wk, fb0, nfb, c0):
                bank, bk = rot()
                wv = wv_f(wb)
                mm(bank[:, 0:nfb * 128].rearrange("p (f c) -> p f c", f=nfb),
                   [(xnT[:, kc, c0:c0 + 128], wv[:, fb0:fb0 + nfb, kc, :]) for kc in range(KC)],
                   [wk, ('xn', 0), ('xn', 128)], [bk])
                return bank, bk

            def rope(bank, bk, c0, n, out, okeys):
                act(qraw[:, 0:n], bank[:, 0:n], AF.Copy, [bk], K_QRAW)
                stop('rope1')
                b2, bk2 = rot()
                mm(b2[:, 0:n], [(Pm, qraw[:, 0:n])], K_QRAW + ['cm'], [bk2])
                stop('rope2')
                t1, k1 = ftile()
                tt('dve', t1[:, 0:n], bank[:, 0:n], cosT[:, c0:c0 + n], ALU.mult, [bk, 'cos'] + K_QRAW, [k1])
                stop('rope2b')
                t2, k2 = ftile()
                tt('dve', t2[:, 0:n], b2[:, 0:n], sinT[:, c0:c0 + n], ALU.mult, [bk2, 'sin'], [k2])
                stop('rope3')
                tt('pool', out, t1[:, 0:n], t2[:, 0:n], ALU.add, [k1, k2], okeys)

            def gcols(g):
                return slice(g * TG, (g + 1) * TG)

            def load_group(l, g, halo):
                src = xin if l == 0 else xs[l - 1]
                srck = ('xs', l - 1, g)
                S.dma('sp', xT[:], src[:, :, g * TG:(g + 1) * TG].rearrange("k p t -> p k t"), reads=[srck], writes=['xT'], slot='lx')
                if halo and g > 0:
                    S.dma('sp', xh[:], src[:, :, g * TG - 128:g * TG].rearrange("k p t -> p k t"), reads=[('xs', l - 1, g - 1)], writes=['xh'], slot='lh')

            def ring(t):
                return (t % 8) * 128

            def hgrn_prep2(l, h, dirs, zf, zq_bank, zq_k, hs, need_q):
                T = {}
                for d in dirs:
                    T[d] = [ftile(), ftile(), ftile()]
                sca = {}
                for d in dirs:
                    col = l * 8 + d * 4 + h
                    sca[d] = (sc[:, SC_HB + col:SC_HB + col + 1], sc[:, SC_HA + col:SC_HA + col + 1], sc[:, SC_NHB + col:SC_NHB + col + 1])
                for d in dirs:
                    (t0, k0) = T[d][0]
                    act(t0[:, 0:512], zf[d][0][:, :], AF.Tanh, [zf[d][1]], [k0], scale=0.5)
                for d in dirs:
                    (t0, k0), (t1, k1) = T[d][0], T[d][1]
                    hb, ha, nha = sca[d]
                    ts('dve', t1[:, 0:512], t0[:, 0:512], nha, ha, ALU.mult, ALU.add, [k0] + SCK, [k1])
                for d in dirs:
                    (t0, k0) = T[d][0]
                    hb, ha, nha = sca[d]
                    act(t0[:, 0:512], t0[:, 0:512], AF.Ln, [k0] + SCK, [k0], scale=ha, bias=hb)
                views = {}
                for d in dirs:
                    (t0, k0) = T[d][0]
                    LX = Lx[d]
                    S.op('dve', lambda e, LX=LX, t0=t0: e.tensor_tensor_scan(out=LX[:, 1:513], data0=sc[:, SC_ONE:SC_ONE + 1].to_broadcast([128, 512]),
                                                                             data1=t0[:, 0:512], initial=0.0, op0=ALU.mult, op1=ALU.add),
                         [k0] + SCK, [('Lx', d)])
                    views[d] = (LX[:, 0:512].rearrange("p (c t) -> p c t", t=64), LX[:, 1:513].rearrange("p (c t) -> p c t", t=64),
                                LX[:, 64:576].rearrange("p (c t) -> p c t", t=64))
                for d in dirs:
                    (t2, k2) = T[d][2]
                    L0, L1, L64 = views[d]
                    D3 = t2[:, 0:512].rearrange("p (c t) -> p c t", t=64)
                    tt('dve', D3, (L1 if d == 0 else L0), L0[:, :, 32:33].to_broadcast([128, 8, 64]), ALU.subtract, [('Lx', d)], [k2])
                for d in dirs:
                    L0, L1, L64 = views[d]
                    es = esub[:, d, :]
                    tt('dve', es[:, 0:8], L0[:, :, 32], L0[:, :, 0], ALU.subtract, [('Lx', d)], [('esub', d)])
                    tt('dve', es[:, 8:16], L64[:, :, 0], L0[:, :, 0], ALU.subtract, [('Lx', d)], [('esub', d)])
                    tt('dve', es[:, 16:24], L64[:, :, 0], L0[:, :, 32], ALU.subtract, [('Lx', d)], [('esub', d)])
                for d in dirs:
                    (t0, k0), (t2, k2) = T[d][0], T[d][2]
                    act(t0[:, 0:512], t2[:, 0:512], AF.Exp, [k2, ('Lx', d)], [k0])
                    act(t2[:, 0:512], t2[:, 0:512], AF.Exp, [k2], [k2], scale=-1.0)
                    act(ech[hs][:, d, :], esub[:, d, :], AF.Exp, [('esub', d)], [('ech', hs, d)])
                for d in dirs:
                    (t0, k0), (t1, k1), (t2, k2) = T[d]
                    E, Ei = t0, t2
                    Qm, kQ = hv(hs, d)
                    Km, kK = hv(hs, 2 + d)
                    if need_q:
                        tt('dve', Qm, zq_bank[:, :], (E if d == 0 else Ei)[:, 0:512], ALU.mult, [zq_k, k0, k2], [kQ])
                    tt('pool', Km, t1[:, 0:512], (Ei if d == 0 else E)[:, 0:512], ALU.mult, [k1, k0, k2], [kK])
                for d in dirs:
                    Km, kK = hv(hs, 2 + d)
                    KmT, kKT = hv(hs, 4 + d)

                    def tps(e, d=d, Km=Km):
                        for i in range(4):
                            r = e.transpose(TPB[:, d * 512 + i * 128:d * 512 + (i + 1) * 128], Km[:, i * 128:(i + 1) * 128], ident)
                        return r
                    S.op('pe', tps, [kK, 'cm'], [K_TP])
                    S.op('act', lambda e, d=d, KmT=KmT: e.activation(out=KmT, in_=TPB[:, d * 512:(d + 1) * 512], func=AF.Copy), [K_TP], [kKT])

            def escal(hs, d, c):
                e = ech[hs]
                ea, eb, ec = e[:, d, c:c + 1], e[:, d, 8 + c:9 + c], e[:, d, 16 + c:17 + c]
                return (ea, eb, ec) if d == 0 else (ec, eb, ea)

            def ds_compute(d, h, hs, c):
                tile_i, po = c // 2, (c % 2) * 64
                KmT, kKT = hv(hs, 4 + d)
                KmT3 = KmT.rearrange("p (t k) -> p t k", t=4)
                bank, bk = rot()
                mm(bank[:, 0:128], [(KmT3[po:po + 64, tile_i, :], Vt[po:po + 64, tile_i, h * 128:(h + 1) * 128])],
                   [kKT] + K_V, [bk])
                e1, eb, e3 = escal(hs, d, c)
                act(dSb[d][:, c, :], bank[:, 0:128], AF.Identity, [bk, ('ech', hs, d)], [('dS', d, c)], scale=e3)

            def st_src(d, h, k, Sst, skey):
                if k == 0:
                    return Sst[:, h, :], skey
                return Sall[d][:, k - 1, :], ('Sall', d, k - 1)

            def st_dst(d, h, k, Sst, skey):
                if k == 7:
                    return Sst[:, h, :], skey
                return Sall[d][:, k, :], ('Sall', d, k)

            def state_step(d, h, hs, k, c, Sst, skey):
                e1, eb, e3 = escal(hs, d, c)
                s_ap, s_k = st_src(d, h, k, Sst, skey)
                d_ap, d_k = st_dst(d, h, k, Sst, skey)
                stt('dve', d_ap, s_ap, eb, dSb[d][:, c, :], ALU.mult, ALU.add, [s_k, ('ech', hs, d), ('dS', d, c)], [d_k])

            def chunk_sb(d, h, hs, k, c, Sst, skey):
                e1, eb, e3 = escal(hs, d, c)
                i = d * 8 + k
                s_ap, s_k = st_src(d, h, k, Sst, skey)
                act(Sb[i][:], s_ap, AF.Identity, [s_k, ('ech', hs, d)], [('Sb', i)], scale=e1)

            def chunk_at(d, h, hs, k, c):
                tile_i, po = c // 2, (c % 2) * 64
                cs = slice(c * 64, (c + 1) * 64)
                Qm, kQ = hv(hs, d)
                Km, kK = hv(hs, 2 + d)
                i = d * 8 + k
                bank, bk = rot()
                mm(bank[po:po + 64, 0:64], [(Km[:, cs], Qm[:, cs])], [kK, kQ], [bk])
                tt('dve', At[i][po:po + 64, :], bank[po:po + 64, 0:64], MH[d][po:po + 64, :], ALU.mult, [bk, 'cm'], [('At', i)])

            def chunk_fin(d, h, hs, k, c, acc, acck):
                tile_i, po = c // 2, (c % 2) * 64
                cs = slice(c * 64, (c + 1) * 64)
                Qm, kQ = hv(hs, d)
                i = d * 8 + k

                def fn(e_):
                    e_.matmul(acc[:, cs], lhsT=Sb[i][:], rhs=Qm[:, cs], start=True, stop=False)
                    return e_.matmul(acc[:, cs], lhsT=Vt[po:po + 64, tile_i, h * 128:(h + 1) * 128], rhs=At[i][po:po + 64, :], start=False, stop=True)
                S.op('pe', fn, [('Sb', i), ('At', i), kQ] + K_V, [acck])

            def exchange(l, kind):
                if kind == 'S':
                    S.dma('sp', exS_i[l].ap(), S1[:].rearrange("p h v -> p (h v)"), reads=['S1'], writes=[('exSi', l)], slot='ex0')
                    S.coll(lambda e: e.collective_compute("AllGather", ALU.bypass, replica_groups=[[0, 1], [2, 3], [4, 5], [6, 7]],
                                                          ins=[exS_i[l].ap().opt()], outs=[exS_o[l].ap().opt()]),
                           reads=[('exSi', l)], writes=[('exSo', l)], slot='ccS%d' % l)
                    S.dma('sp', exs[:], exS_o[l].ap().rearrange("(r p) n -> p r n", p=128), reads=[('exSo', l)], writes=[('exs', 0), ('exs', 1)], slot='ex1')
                    s2f = S2[:].rearrange("p h v -> p (h v)")
                    ts('dve', s2f, exs[:, 0, :], ppt[:, P_SEL:P_SEL + 1], None, ALU.mult, None, [('exs', 0), ('exs', 1), 'pp'], ['S2'])
                    stt('dve', s2f, exs[:, 1, :], ppt[:, P_SEL + 1:P_SEL + 2], s2f, ALU.mult, ALU.add, [('exs', 0), ('exs', 1), 'pp', 'S2'], ['S2'])
                else:
                    tl = NT - 1
                    S.dma('sp', exK_i[l].ap()[:, 0:256].rearrange("p (a b) -> p a b", a=2), KT[:, :, ring(tl):ring(tl) + 128],
                          reads=['KT'], writes=[('exKi', l)], slot='ex2')
                    S.dma('sp', exK_i[l].ap()[:, 256:512], V2[:, tl % 8, :], reads=['V2'], writes=[('exKi', l, 1)], slot='ex3')
                    S.coll(lambda e: e.collective_compute("AllGather", ALU.bypass, replica_groups=[[0, 1], [2, 3], [4, 5], [6, 7]],
                                                          ins=[exK_i[l].ap().opt()], outs=[exK_o[l].ap().opt()]),
                           reads=[('exKi', l), ('exKi', l, 1)], writes=[('exKo', l)], slot='ccK%d' % l)
                    S.dma('sp', exb[:], exK_o[l].ap().rearrange("(r p) n -> p r n", p=128), reads=[('exKo', l)], writes=['exb'], slot='ex4')
                    kdst = KT[:, :, ring(NT):ring(NT) + 128]
                    e0 = exb[:, 0, 0:256].rearrange("p (a b) -> p a b", a=2)
                    e1 = exb[:, 1, 0:256].rearrange("p (a b) -> p a b", a=2)
                    ts('dve', kdst, e0, ppt[:, P_SEL:P_SEL + 1], None, ALU.mult, None, ['exb', 'pp', 'KT'], ['KT'])
                    stt('dve', kdst, e1, ppt[:, P_SEL + 1:P_SEL + 2], kdst, ALU.mult, ALU.add, ['exb', 'pp', 'KT'], ['KT'])
                    vdst = V2[:, NT % 8, :]
                    ts('dve', vdst, exb[:, 0, 256:512], ppt[:, P_SEL:P_SEL + 1], None, ALU.mult, None, ['exb', 'pp', 'V2'], ['V2'])
                    stt('dve', vdst, exb[:, 1, 256:512], ppt[:, P_SEL + 1:P_SEL + 2], vdst, ALU.mult, ALU.add, ['exb', 'pp', 'V2'], ['V2'])

            stop_cnt = {}

            def stop(tag):
                stop_cnt[tag] = stop_cnt.get(tag, 0) + 1
                if DEBUG_STOP[0] == tag or DEBUG_STOP[0] == '%s#%d' % (tag, stop_cnt[tag]):
                    raise _Stop()

            for l in range(NL):
                last = (l == NL - 1)
                S.dma('sp', bsb[:], bs_d[l:l + 1, :].partition_broadcast(128), writes=['bsb'], slot='c2')
                ft, fk = ftile()
                S.dma('sp', ft[:, 0:512], wst_d[l], writes=[fk], slot='c3')
                S.op('dve', lambda e, ft=ft: e.tensor_copy(out=wsT[:], in_=ft[:, 0:512]), [fk], ['wsT'])
                S.dma('sp', esink[:], sink_d[l:l + 1, :].partition_broadcast(128), writes=['esink'], slot='c4')
                act(esink[:], esink[:], AF.Exp, ['esink'], ['esink'])
                es3 = esink[:].rearrange("p (q two) -> p q two", two=2)
                S.op('dve', lambda e, es3=es3: e.tensor_copy(out=esink2[0:64, :], in_=es3[0:64, :, 0]), ['esink'], ['esink2'])
                S.op('dve', lambda e, es3=es3: e.tensor_copy(out=esink2[64:128, :], in_=es3[64:128, :, 1]), ['esink', 'esink2'], ['esink2'])
                S.op('pool', lambda e: e.memset(S1[:], 0.0), writes=['S1'])

                stop('consts')
                for g in range(NG):
                    load_group(l, g, halo=False)
                    rmsnorm([(xT, ['xT'], TG, 128, xn_keys(128))], SC_LN1 + l * 8, l)
                    S.dma('sp', s1s[l, g], S1[:].rearrange("p h v -> p (h v)"), reads=['S1'], writes=[('s1s', l, g)], slot='st1')
                    wb, wk = w_use(l, 'ai')
                    for t in range(4):
                        bank, bk = proj_tm(wb, wk, 0, 4, 128 + t * 128)
                        act(Vt[:, t, :], bank[:, :], AF.Copy, [bk], [K_V[t]])
                    for h in range(4):
                        hs = h % 2
                        wb, wk = w_use(l, 'h%d' % h)
                        zb, zk = proj_fm(wb, wk, 0, 128, TG)
                        hgrn_prep2(l, h, [0], {0: (zb, zk)}, None, None, hs, need_q=False)
                        for c in range(8):
                            ds_compute(0, h, hs, c)
                        for c in range(8):
                            e1_, eb_, e3_ = escal(hs, 0, c)
                            stt('dve', S1[:, h, :], S1[:, h, :], eb_, dSb[0][:, c, :], ALU.mult, ALU.add, ['S1', ('ech', hs, 0), ('dS', 0, c)], ['S1'])
                stop('pass1')
                exchange(l, 'S')
                stop('exS')

                c1['on'] = (l == 0 and NL == 2)
                for g in range(NG - 1, -1, -1):
                    has_lo = g > 0
                    has_hi = True
                    load_group(l, g, halo=True)
                    S.dma('sp', S1[:].rearrange("p h v -> p (h v)"), s1s[l, g], reads=[('s1s', l, g)], writes=['S1'], slot='ld1')
                    c_lo = g * TG - (128 if has_lo else 0)
                    ncs = TG + (128 if has_lo else 0)
                    o_lo = 0 if has_lo else 128
                    S.dma('sp', cosT[:, o_lo:640], rope_d[0, :, c_lo:c_lo + ncs], writes=['cos'], slot='lc')
                    S.dma('sp', sinT[:, o_lo:640], rope_d[1, :, c_lo:c_lo + ncs], writes=['sin'], slot='ls')
                    segs = []
                    if has_lo:
                        segs.append((xh, ['xh'], 128, 0, xn_keys(0)))
                    segs.append((xT, ['xT'], TG, 128, xn_keys(128)))
                    stop('p2load')
                    rmsnorm(segs, SC_LN1 + l * 8, l)
                    stop('p2norm')
                    wb, wk = w_use(l, 'ckv')
                    tiles = ([4 * g - 1] if has_lo else []) + [4 * g + i for i in range(4)]
                    for kvf in range(2):
                        if has_lo:
                            bank, bk = proj_fm(wb, wk, kvf, 0, 128)
                            rope(bank, bk, 0, 128, KT[:, kvf, ring(4 * g - 1):ring(4 * g - 1) + 128], ['KT'])
                        bank, bk = proj_fm(wb, wk, kvf, 128, TG)
                        rope(bank, bk, 128, TG, KT[:, kvf, ring(4 * g):ring(4 * g) + TG], ['KT'])
                    stop('krope')
                    for t in tiles:
                        c0 = 128 + (t - 4 * g) * 128
                        bank, bk = proj_tm(wb, wk, 2, 2, c0)
                        act(V2[:, t % 8, :], bank[:, 0:256], AF.Copy, [bk], ['V2'])
                    stop('kv')
                    if g == NG - 1:
                        exchange(l, 'K')
                    stop('exK')
                    wb, wk = w_use(l, 'ai')
                    for t in range(4):
                        bank, bk = proj_tm(wb, wk, 0, 4, 128 + t * 128)
                        act(Vt[:, t, :], bank[:, :], AF.Copy, [bk], [K_V[t]])
                    def hg_prep(h):
                            hs = h % 2
                            wb, wk = w_use(l, 'h%d' % h)
                            zq, zqk = proj_fm(wb, wk, 2, 128, TG)
                            zf = {}
                            for d in range(2):
                                zf[d] = proj_fm(wb, wk, d, 128, TG)
                            hgrn_prep2(l, h, [0, 1], zf, zq, zqk, hs, need_q=True)
                            zg, zgk = proj_fm(wb, wk, 3, 128, TG)
                            thg, kthg = ftile()
                            act(thg[:, 0:512], zg[:, :], AF.Tanh, [zgk], [kthg], scale=0.5)
                            sg, ksg = hv(hs, 6)
                            stt('dve', sg, thg[:, 0:512], 1.0, zg[:, :], ALU.add, ALU.mult, [kthg, zgk], [ksg])
                    def gates_blocks(js):
                        for j in js:
                            wb_, wk_ = w_use(l, 'gz%d' % j)
                            for fb in range(4):
                                bank, bk = proj_fm(wb_, wk_, fb, 128, TG)
                                act(big[:, j * 4 + fb, :], bank[:, :], AF.Sigmoid, [bk], [('big', j * 4 + fb)])

                    def hg_recur(h):
                        hs = h % 2
                        sg, ksg = hv(hs, 6)
                        for d in range(2):
                            for c in range(8):
                                ds_compute(d, h, hs, c)
                        chunk_sb(0, h, hs, 0, 0, S1, 'S1')
                        chunk_sb(1, h, hs, 0, 7, S2, 'S2')
                        for k in range(8):
                            chunk_at(0, h, hs, k, k)
                            chunk_at(1, h, hs, k, 7 - k)
                        for k in range(8):
                            state_step(0, h, hs, k, k, S1, 'S1')
                            state_step(1, h, hs, k, 7 - k, S2, 'S2')
                        if h < 3:
                            gates_blocks([2 * h, 2 * h + 1])
                        for k in range(8):
                            if k > 0:
                                chunk_sb(0, h, hs, k, k, S1, 'S1')
                                chunk_sb(1, h, hs, k, 7 - k, S2, 'S2')
                            chunk_fin(0, h, hs, k, k, ACC[0][0], ACC[0][1])
                            chunk_fin(1, h, hs, k, 7 - k, ACC[1][0], ACC[1][1])
                        o1, ko1 = ftile()
                        act(o1[:, 0:512], ACC[0][0][:, :], AF.Copy, [ACC[0][1]], [ko1])
                        o, ko = ftile()
                        tt('dve', o[:, 0:512], ACC[1][0][:, :], o1[:, 0:512], ALU.add, [ACC[1][1], ko1], [ko])
                        act(osq[:], o[:, 0:512], AF.Square, [ko], ['osq'])
                        bank, bk = rot()
                        mm(bank[:, :], [(ones_b, osq[:])], ['osq', 'cm'], [bk])
                        rs, krs = ftile()
                        rs0, krs0 = ftile()
                        ts('dve', rs0[:, 0:512], bank[:, :], float(128 * EPS), None, ALU.add, None, [bk], [krs0])
                        act(rs0[:, 0:512], rs0[:, 0:512], AF.Ln, [krs0], [krs0])
                        act(rs[:, 0:512], rs0[:, 0:512], AF.Exp, [krs0], [krs], scale=-0.5)
                        t_, kt_ = ftile()
                        tt('dve', t_[:, 0:512], o[:, 0:512], rs[:, 0:512], ALU.mult, [ko, krs], [kt_])
                        stt('dve', yaT[:, h, :], t_[:, 0:512], sc[:, SC_AN + l:SC_AN + l + 1], sg, ALU.mult, ALU.mult, [kt_, ksg] + SCK, [('ya', h)])
                    hg_prep(0)
                    for h in range(4):
                        if h + 1 < 4:
                            hg_prep(h + 1)
                        hg_recur(h)
                    stop('hgrn')
                    wb, wk = w_use(l, 'bu')
                    for fb in range(4):
                        bank, bk = proj_fm(wb, wk, fb, 128, TG)
                        act(uT[:, fb, :], bank[:, :], AF.Gelu_apprx_tanh, [bk], [K_UT[fb]])
                    wb, wk = w_use(l, 'bv')
                    for t in range(4):
                        bank, bk = proj_tm(wb, wk, 0, 4, 128 + t * 128)
                        v, kv = ftile()
                        act(v[:, 0:512], bank[:, :], AF.Gelu_apprx_tanh, [bk], [kv])
                        S.op('dve', lambda e, v=v: e.bn_stats(out=bnst[:, 0:6], in_=v[:, 0:512]), [kv], ['bnst'])
                        S.op('dve', lambda e: e.bn_aggr(out=bnst[:, 8:10], in_=bnst[:, 0:6]), ['bnst'], ['bnst2'])
                        ts('dve', bnst[:, 11:12], bnst[:, 9:10], float(EPS), None, ALU.add, None, ['bnst2'], ['bnst3'])
                        act(bnst[:, 11:12], bnst[:, 11:12], AF.Ln, ['bnst3'], ['bnst3'])
                        act(bnst[:, 10:11], bnst[:, 11:12], AF.Exp, ['bnst3'], ['bnst3'], scale=-0.5)
                        ts('dve', vn[:, t, :], v[:, 0:512], bnst[:, 8:9], bnst[:, 10:11], ALU.subtract, ALU.mult, [kv, 'bnst2', 'bnst3'], [K_VN[t]])
                    wsT3 = wsT[:].rearrange("p (g q) -> p g q", g=4)
                    for t in range(4):
                        bank, bk = rot()

                        def fn(e, bank=bank, t=t):
                            for gg in range(4):
                                r = e.matmul(bank[:, gg * 128:(gg + 1) * 128], lhsT=vn[:, t, gg * 128:(gg + 1) * 128], rhs=wsT3[:, gg, :], start=True, stop=True)
                            return r
                        S.op('pe', fn, [K_VN[t], 'wsT'], [bk])
                        mb, kmb = ftile()
                        tt('dve', mb[:, 0:512], bank[:, :], bsb[:], ALU.add, [bk, 'bsb'], [kmb])
                        tt('pool', ybT[:, :, t * 128:(t + 1) * 128], mb[:, 0:512].rearrange("p (g q) -> p g q", g=4), uT[:, :, t * 128:(t + 1) * 128], ALU.mult,
                           [kmb] + K_UT, [('yb', t)])
                    YBK = [('yb', t) for t in range(4)]
                    stop('gmlp')
                    wb, wk = w_use(l, 'cq')
                    for fb in range(4):
                        bank, bk = proj_fm(wb, wk, fb, 128, TG)
                        rope(bank, bk, 128, TG, QT[:, fb, :], [K_QT[fb]])
                    kts = ([4 * g - 1] if has_lo else []) + [4 * g + i for i in range(4)] + [4 * g + 4]
                    arot_i = [0]

                    def arot():
                        i = arot_i[0] % 4
                        arot_i[0] += 1
                        return psb[i], ('ps', i)
                    OD = [((psb[5], ('ps', 5)), (psb[6], ('ps', 6))), ((psb[7], ('ps', 7)), (psb[4], ('ps', 4)))]

                    def att_scores(qb, hh):
                        kvh = qb // 2
                        pr = slice(hh * 64, hh * 64 + 64)
                        inf = {}
                        for j, kt in enumerate(kts):
                            qlo, qhi = max(kt - 1, 4 * g), min(kt + 1, 4 * g + 3)
                            nq = qhi - qlo + 1
                            qc0 = (qlo - 4 * g) * 128
                            bank, bk = arot()
                            mlist = []
                            for qi in range(nq):
                                rel = (qlo + qi) - kt
                                if kt == NT:
                                    mlist.append((qi, NEGM['c']))
                                elif rel != 0:
                                    mlist.append((qi, NEGM[rel]))

                            def fn(e, bank=bank, pr=pr, kvh=kvh, kt=kt, qb=qb, qc0=qc0, nq=nq, mlist=mlist):
                                r = e.matmul(bank[:, 0:nq * 128], lhsT=KT[pr, kvh, ring(kt):ring(kt) + 128], rhs=QT[pr, qb, qc0:qc0 + nq * 128],
                                             start=True, stop=(len(mlist) == 0))
                                for ii, (qi, ng) in enumerate(mlist):
                                    r = e.matmul(bank[:, qi * 128:(qi + 1) * 128], lhsT=ident, rhs=ng, start=False, stop=(ii == len(mlist) - 1))
                                return r
                            S.op('pe', fn, ['KT', K_QT[qb], 'cm'], [bk])
                            act(PT[hh][j][:, 0:nq * 128], bank[:, 0:nq * 128], AF.Exp, [bk], K_PT[hh][j], scale=0.125)
                            inf[kt] = (j, qlo)
                        return inf

                    def att_pv(qb, hh, inf):
                        kvh = qb // 2
                        (obank, obk), (dbank, dbk) = OD[qb % 2]
                        pr = slice(hh * 64, hh * 64 + 64)
                        for qt in range(4 * g, 4 * g + 4):
                            qcs = slice((qt - 4 * g) * 128, (qt - 4 * g + 1) * 128)
                            use = [kt for kt in (qt - 1, qt, qt + 1) if kt in inf]
                            pairs_o, pairs_d, rk = [], [], []
                            for kt in use:
                                j, qlo = inf[kt]
                                p_ap = PT[hh][j][:, (qt - qlo) * 128:(qt - qlo + 1) * 128]
                                vc = kvh * 128 + hh * 64
                                pairs_o.append((V2[:, kt % 8, vc:vc + 64], p_ap))
                                pairs_d.append((ones_b[:, 0:64], p_ap))
                                rk += K_PT[hh][j]
                            mm(obank[pr, qcs], pairs_o, rk + ['V2'], [obk])
                            mm(dbank[pr, qcs], pairs_d, rk + ['cm'], [dbk])

                    def att_norm(qb):
                        (obank, obk), (dbank, dbk) = OD[qb % 2]
                        rd, krd = ftile()
                        ts('dve', rd[:, 0:512], dbank[:, :], esink2[:, qb:qb + 1], None, ALU.add, None, [dbk, 'esink2'], [krd])
                        S.op('dve', lambda e, rd=rd: e.reciprocal(out=rd[:, 0:512], in_=rd[:, 0:512]), [krd], [krd])
                        tt('dve', ycT[:, qb, :], obank[:, :], rd[:, 0:512], ALU.mult, [obk, krd], [('yc', qb, 0)])

                    units = [(qb, hh) for qb in range(4) for hh in range(2)]
                    infos = {units[0]: att_scores(*units[0])}
                    for ui, u in enumerate(units):
                        if ui + 1 < len(units):
                            infos[units[ui + 1]] = att_scores(*units[ui + 1])
                        att_pv(u[0], u[1], infos[u])
                        if u[1] == 1:
                            att_norm(u[0])
                    YCK = [('yc', qb, 0) for qb in range(4)]
                    YAK = [('ya', h) for h in range(4)]
                    stop('attn')
                    stop('gates')
                    ysrc = [(yaT, YAK), (ybT, YBK), (ycT, YCK)]
                    for j in range(4):
                        wb, wk = w_use(l, 'br%d' % j)
                        wv = wb[:, 0:3072].rearrange("p (b k c) -> p b k c", b=3, k=4)
                        for o2 in range(2):
                            ob = 2 * j + o2
                            tl_ = []
                            for br in range(3):
                                ysb, ykeys = ysrc[br]
                                bank, bk = rot()
                                mm(bank[:, :], [(wv[:, br, kc, o2 * 128:(o2 + 1) * 128], ysb[:, kc, :]) for kc in range(4)], [wk] + ykeys, [bk])
                                tf, ktf = ftile()
                                tt('dve', tf[:, 0:512], bank[:, :], big[:, br * 8 + ob, :], ALU.mult, [bk, ('big', br * 8 + ob)], [ktf])
                                tl_.append((tf, ktf))
                            tt('pool', tl_[0][0][:, 0:512], tl_[0][0][:, 0:512], tl_[1][0][:, 0:512], ALU.add, [tl_[0][1], tl_[1][1]], [tl_[0][1]])
                            tt('pool', big[:, 24 + ob, :], tl_[0][0][:, 0:512], tl_[2][0][:, 0:512], ALU.add, [tl_[0][1], tl_[2][1]], [('big', 24 + ob)])
                    stop('merge')
                    for j in range(2):
                        wb, wk = w_use(l, 'wo%d' % j)
                        wv = wb[:].rearrange("p (k c) -> p k c", k=8)
                        for o4 in range(4):
                            ob = j * 4 + o4
                            bank, bk = rot()
                            mm(bank[:, :], [(wv[:, kc, o4 * 128:(o4 + 1) * 128], big[:, 24 + kc, :]) for kc in range(KC)],
                               [wk] + [('big', 24 + kc) for kc in range(KC)], [bk])
                            tt('dve', xT[:, ob, :], xT[:, ob, :], bank[:, :], ALU.add, [bk, 'xT'], ['xT'])
                    stop('wout')
                    rmsnorm([(xT, ['xT'], TG, 128, xn_keys(128))], SC_LN2 + l * 8, l)
                    ri = 0
                    for j in range(8):
                        wb, wk = w_use(l, 'up%d' % j)
                        for fb in range(4):
                            bank, bk = proj_fm(wb, wk, fb, 128, TG)
                            r = rl[ri % 2]
                            rk_ = ('rl', ri % 2)
                            ri += 1
                            act(r[:], bank[:, :], AF.Relu, [bk], [rk_])
                            tt('pool', big[:, j * 4 + fb, :], r[:], r[:], ALU.mult, [rk_], [('big', j * 4 + fb)])
                    for ob in range(8):
                        wb, wk = w_use(l, 'dn%d' % ob)
                        wv = wb[:].rearrange("p (k c) -> p k c", k=32)
                        bank, bk = rot()
                        mm(bank[:, :], [(wv[:, kc, :], big[:, kc, :]) for kc in range(32)], [wk] + BIGK, [bk])
                        tt('dve', xT[:, ob, :], xT[:, ob, :], bank[:, :], ALU.add, [bk, 'xT'], ['xT'])
                    stop('ffn')
                    if last:
                        rmsnorm([(xT, ['xT'], TG, 128, None)], SC_FN, l, out_f32=True)
                        S.dma('sp', yout[:, :, g * TG:(g + 1) * TG].rearrange("k p t -> p k t"), xT[:], reads=['xT'], writes=[('y', g)], slot='sto')
                    else:
                        S.dma('sp', xs[l][:, :, g * TG:(g + 1) * TG].rearrange("k p t -> p k t"), xT[:], reads=['xT'], writes=[('xs', l, g)], slot='sto')
                if c1['on']:
                    while c1['done'] < len(cast1_steps):
                        cast1_step()
                    c1['on'] = False
        try:
            main_body()
        except _Stop:
            S.dma('sp', yout[:, :, 0:TG].rearrange("k p t -> p k t"), xT[:], reads=['xT'], writes=[('y', 0)], slot='sto')
        S.final_wait('sp', ['sto'])
        print("ops:", {e: len(S.lists[e]) for e in S.ENGS})
        with nc.Block() as block:
            S.emit(block)
    return nc


def _consts():
    r = np.arange(128)
    ident = np.eye(128, dtype=np.float32)
    pm = np.zeros((128, 128), np.float32)
    for c in range(128):
        d = c % 64
        if d < 8:
            pm[c + 8, c] = 1.0
        elif d < 16:
            pm[c - 8, c] = 1.0
    m = r[:, None]
    a = r[None, :]
    m3 = np.concatenate([(m <= a), np.ones((128, 128), bool), (m >= a)], axis=1).astype(np.float32)
    mc = ((m + a) >= 127).astype(np.float32)
    s = (r % 64)[:, None]
    t = np.arange(64)[None, :]
    mh1 = (s <= t).astype(np.float32)
    mh2 = (s >= t).astype(np.float32)
    ones = np.ones((128, 128), np.float32)
    NEG = np.float32(-30000.0)
    nb_ = np.where(m <= a, 0.0, NEG).astype(np.float32)
    na_ = np.where(m >= a, 0.0, NEG).astype(np.float32)
    nc_ = np.where((m + a) >= 127, 0.0, NEG).astype(np.float32)
    cm = np.concatenate([ident, pm, m3, mc, mh1, mh2, ones, nb_, na_, nc_], axis=1)
    assert cm.shape[1] == C_END
    return cm.astype(ml_dtypes.bfloat16)


def _rope_tables(pos):
    inv = np.float32(500000.0) ** (-(np.arange(8, dtype=np.float32) * np.float32(2.0 / 16)))
    ang = pos.astype(np.float32)[:, None] * inv[None, :]
    cos = np.cos(ang).astype(np.float32)
    sin = np.sin(ang).astype(np.float32)
    T = pos.shape[0]
    C = np.ones((128, T), np.float32)
    Sn = np.zeros((128, T), np.float32)
    for rr in range(128):
        d = rr % 64
        if d < 8:
            C[rr] = cos[:, d]
            Sn[rr] = -sin[:, d]
        elif d < 16:
            C[rr] = cos[:, d - 8]
            Sn[rr] = sin[:, d - 8]
    return np.stack([C, Sn], 0)


def _wblocks(w_in, w_br, w_out, w_up, w_down, half):
    NL = w_in.shape[0]
    out = np.zeros((NL, NB, 128, 4096), np.float32)

    def fbk(W, cols):
        return W[:, cols].reshape(8, 128, 128).transpose(1, 0, 2)

    def blk(W, colsets):
        return np.stack([fbk(W, c) for c in colsets], 1).reshape(128, 4096)
    ar = np.arange
    for l in range(NL):
        W = w_in[l]
        f1o, f2o = (1024, 1536) if half == 0 else (1536, 1024)
        out[l, BI['ai']] = blk(W, [512 + h * 128 + ar(128) for h in range(4)])
        for h in range(4):
            out[l, BI['h%d' % h]] = blk(W, [f1o + h * 128 + ar(128), f2o + h * 128 + ar(128), h * 128 + ar(128), 2048 + h * 128 + ar(128)])
        out[l, BI['bu']] = blk(W, [2560 + f * 128 + ar(128) for f in range(4)])
        out[l, BI['bv']] = blk(W, [3072 + f * 128 + ar(128) for f in range(4)])
        out[l, BI['cq']] = blk(W, [3584 + f * 128 + ar(128) for f in range(4)])
        k0, k1 = 4096 + ar(64), 4160 + ar(64)
        v0, v1 = 4224 + ar(64), 4288 + ar(64)
        cc = np.concatenate
        out[l, BI['ckv']] = blk(W, [cc([k0, k0]), cc([k1, k1]), cc([v0, v0]), cc([v1, v1])])
        for j in range(6):
            out[l, BI['gz%d' % j]] = blk(W, [4352 + (j * 4 + f) * 128 + ar(128) for f in range(4)])
        for j in range(4):
            a = w_br[l][:, :, j * 256:(j + 1) * 256].reshape(3, 4, 128, 256).transpose(2, 0, 1, 3)
            out[l, BI['br%d' % j], :, 0:3072] = a.reshape(128, 3072)
        for j in range(2):
            a = w_out[l][:, j * 512:(j + 1) * 512].reshape(8, 128, 512).transpose(1, 0, 2)
            out[l, BI['wo%d' % j]] = a.reshape(128, 4096)
        for j in range(8):
            out[l, BI['up%d' % j]] = blk(w_up[l], [j * 512 + f * 128 + ar(128) for f in range(4)])
        for ob in range(8):
            a = w_down[l][:, ob * 128:(ob + 1) * 128].reshape(32, 128, 128).transpose(1, 0, 2)
            out[l, BI['dn%d' % ob]] = a.reshape(128, 4096)
    return out


_PROG_CACHE = {}


def kernel(x, w_in, ln1, lb_logits, a_norm, w_s, b_s, sink, w_br, w_out, ln2, w_up, w_down, final_norm):
    x = np.asarray(x, np.float32)
    f = lambda a: np.asarray(a, np.float32)
    w_in, ln1, lb_logits, a_norm, w_s, b_s, sink = map(f, (w_in, ln1, lb_logits, a_norm, w_s, b_s, sink))
    w_br, w_out, ln2, w_up, w_down, final_norm = map(f, (w_br, w_out, ln2, w_up, w_down, final_norm))
    B, SEQ, _ = x.shape
    NL = w_in.shape[0]
    TOK = SEQ // 2
    assert B * 2 == NCORES and TOK % TG == 0
    key = (TOK, NL)
    if key not in _PROG_CACHE:
        _PROG_CACHE[key] = build_program(TOK, NL, SEQ)
    nc = _PROG_CACHE[key]
    cm = _consts()
    in_maps = []
    variants = {}
    for half in range(2):
        wsrc = _wblocks(w_in, w_br, w_out, w_up, w_down, half)
        pos = np.arange(TOK) if half == 0 else (SEQ - 1 - np.arange(TOK))
        rope = _rope_tables(pos)
        pp = np.zeros((128, P_END), np.float32)
        for l in range(NL):
            pp[:, P_LN1 + l * 8:P_LN1 + l * 8 + 8] = ln1[l].reshape(8, 128).T
            pp[:, P_LN2 + l * 8:P_LN2 + l * 8 + 8] = ln2[l].reshape(8, 128).T
            for d in range(2):
                dirn = d if half == 0 else 1 - d
                pp[:, P_LBL + l * 8 + d * 4:P_LBL + l * 8 + d * 4 + 4] = lb_logits[l, dirn].reshape(4, 128).T
            pp[:, P_AN + l] = a_norm[l]
        pp[:, P_FN:P_FN + 8] = final_norm.reshape(8, 128).T
        pp[:, P_SEL] = 0.0 if half == 0 else 1.0
        pp[:, P_SEL + 1] = 1.0 if half == 0 else 0.0
        if half == 0:
            wst = np.ascontiguousarray(w_s.transpose(0, 3, 1, 2)).reshape(NL, 128, 512)
            bsv = b_s.reshape(NL, 512)
        else:
            wst = np.ascontiguousarray(w_s[:, :, ::-1, ::-1].transpose(0, 3, 1, 2)).reshape(NL, 128, 512)
            bsv = np.ascontiguousarray(b_s[:, :, ::-1]).reshape(NL, 512)
        variants[half] = dict(wsrc=wsrc, pp=pp, cm=cm, bsv=np.ascontiguousarray(bsv), wst=wst,
                              sinkv=np.ascontiguousarray(sink), rope=rope)
    for c in range(NCORES):
        b, half = c // 2, c % 2
        xs_ = x[b, :TOK] if half == 0 else x[b, TOK:][::-1]
        xin = np.ascontiguousarray(xs_.T).reshape(KC, 128, TOK)
        m = dict(variants[half])
        m['xin'] = xin
        in_maps.append(m)
    res = run_bass_kernel_spmd(nc, in_maps, core_ids=list(range(NCORES)))
    y = np.empty((B, SEQ, D), np.float32)
    for c in range(NCORES):
        b, half = c // 2, c % 2
        yt = np.asarray(res.results[c]["yout"]).reshape(D, TOK).T
        if half == 0:
            y[b, :TOK] = yt
        else:
            y[b, TOK:] = yt[::-1]
    return y
```

```python
import contextlib
import numpy as np
import ml_dtypes
import concourse.bass as bass
import concourse.mybir as mybir
from concourse.bass_utils import run_bass_kernel_spmd

F32 = mybir.dt.float32
BF16 = mybir.dt.bfloat16
AF = mybir.ActivationFunctionType
ALU = mybir.AluOpType

D = 1024
KC = 8
TG = 512
EPS = 1e-6
NCORES = 8
NWB = 3
BLK = ['ai', 'h0', 'h1', 'h2', 'h3', 'bu', 'bv', 'cq', 'ckv'] + ['gz%d' % i for i in range(6)] + \
      ['br%d' % i for i in range(4)] + ['wo0', 'wo1'] + ['up%d' % i for i in range(8)] + ['dn%d' % i for i in range(8)]
BI = {n: i for i, n in enumerate(BLK)}
NB = len(BLK)
C_ID, C_PM, C_M3, C_MC, C_MH1, C_MH2, C_ONE, C_NB, C_NA, C_NC, C_END = 0, 128, 256, 640, 768, 832, 896, 1024, 1152, 1280, 1408
P_LN1, P_LN2, P_FN, P_LBL, P_AN, P_SEL, P_END = 0, 16, 32, 40, 56, 58, 60


class Sched:
    ENGS = ('pe', 'act', 'dve', 'pool', 'sp')

    def __init__(self, nc, stack):
        self.nc = nc
        self.stack = stack
        self.lists = {e: [] for e in self.ENGS}
        self.cnt = {}
        self.sem = {}
        self.waited = {e: {} for e in self.ENGS}
        self.lastw = {}
        self.readers = {}
        for e in self.ENGS[:4]:
            self._mksem(e)

    def _mksem(self, name):
        if name not in self.sem:
            self.sem[name] = self.stack.enter_context(self.nc.semaphore("s_" + name))
            self.cnt[name] = 0

    def _deps(self, eng, reads, writes):
        deps = {}
        for b in reads:
            w = self.lastw.get(b)
            if w:
                deps[w[0]] = max(deps.get(w[0], 0), w[1])
            if isinstance(b, tuple) and b[0] == 'ps':
                for s, v in self.readers.get(b, {}).items():
                    if s != eng:
                        deps[s] = max(deps.get(s, 0), v)
        for b in writes:
            w = self.lastw.get(b)
            if w:
                deps[w[0]] = max(deps.get(w[0], 0), w[1])
            for s, v in self.readers.get(b, {}).items():
                deps[s] = max(deps.get(s, 0), v)
        waits = []
        for s, v in deps.items():
            if eng == 'pe' and s == 'pe':
                continue
            if self.waited[eng].get(s, 0) < v:
                waits.append((s, v))
                self.waited[eng][s] = v
        return waits

    def _record(self, semname, val, reads, writes):
        for b in reads:
            r = self.readers.setdefault(b, {})
            r[semname] = max(r.get(semname, 0), val)
        for b in writes:
            self.lastw[b] = (semname, val)
            self.readers[b] = {}

    def op(self, eng, fn, reads=(), writes=()):
        waits = self._deps(eng, reads, writes)
        self.cnt[eng] += 1
        self.lists[eng].append((waits, fn, eng, 1))
        self._record(eng, self.cnt[eng], reads, writes)

    def dma(self, q, out, in_, reads=(), writes=(), slot=None):
        self._mksem(slot)
        waits = self._deps(q, reads, writes)
        self.cnt[slot] += 16
        self.lists[q].append((waits, lambda e: e.dma_start(out=out, in_=in_), slot, 16))
        self._record(slot, self.cnt[slot], reads, writes)

    def coll(self, fn, reads=(), writes=(), slot=None):
        self._mksem(slot)
        assert self.cnt[slot] == 0
        waits = self._deps('pool', reads, writes)
        self.cnt[slot] = 1
        self.lists['pool'].append((waits, fn, slot, None))
        self._record(slot, 1, reads, writes)

    def final_wait(self, q, slots):
        waits = [(s, self.cnt[s]) for s in slots if self.cnt.get(s, 0) > 0]
        self.lists[q].append((waits, None, None, 0))

    def emit(self, block):
        sem = self.sem

        def run(lst):
            def body(e):
                for waits, fn, semname, inc in lst:
                    for s, v in waits:
                        e.wait_ge(sem[s], v)
                    if fn is not None:
                        inst = fn(e)
                        if inc is None:
                            inst.then_inc(sem[semname])
                        else:
                            inst.then_inc(sem[semname], inc)
            return body
        block.tensor(run(self.lists['pe']))
        block.scalar(run(self.lists['act']))
        block.vector(run(self.lists['dve']))
        block.gpsimd(run(self.lists['pool']))
        block.sync(run(self.lists['sp']))


class _Stop(Exception):
    pass


DEBUG_STOP = [None]


def build_program(TOK, NL, S_FULL):
    NG = TOK // TG
    NT = TOK // 128
    nc = bass.Bass("TRN2", target_bir_lowering=False)
    xin = nc.dram_tensor("xin", [KC, 128, TOK], F32, kind="ExternalInput").ap()
    wsrc = nc.dram_tensor("wsrc", [NL, NB, 128, 4096], F32, kind="ExternalInput").ap()
    pp_d = nc.dram_tensor("pp", [128, P_END], F32, kind="ExternalInput").ap()
    cm_d = nc.dram_tensor("cm", [128, C_END], BF16, kind="ExternalInput").ap()
    bs_d = nc.dram_tensor("bsv", [NL, 512], F32, kind="ExternalInput").ap()
    wst_d = nc.dram_tensor("wst", [NL, 128, 512], F32, kind="ExternalInput").ap()
    sink_d = nc.dram_tensor("sinkv", [NL, 8], F32, kind="ExternalInput").ap()
    rope_d = nc.dram_tensor("rope", [2, 128, TOK], F32, kind="ExternalInput").ap()
    yout = nc.dram_tensor("yout", [KC, 128, TOK], F32, kind="ExternalOutput").ap()
    wbf = nc.dram_tensor("wbf", [NL, NB, 128, 4096], BF16).ap()
    xs = [nc.dram_tensor("xs%d" % l, [KC, 128, TOK], F32).ap() for l in range(max(NL - 1, 1))]
    s1s = nc.dram_tensor("s1s", [NL, NG, 128, 512], F32).ap()
    exS_i = [nc.dram_tensor("exSi%d" % l, [128, 512], F32) for l in range(NL)]
    exS_o = [nc.dram_tensor("exSo%d" % l, [256, 512], F32) for l in range(NL)]
    exK_i = [nc.dram_tensor("exKi%d" % l, [128, 512], BF16) for l in range(NL)]
    exK_o = [nc.dram_tensor("exKo%d" % l, [256, 512], BF16) for l in range(NL)]

    st = contextlib.ExitStack()
    with st:
        S = Sched(nc, st)

        def sb(name, shape, dt):
            return st.enter_context(nc.sbuf_tensor(name, shape, dt))

        cm = sb("cm_s", [128, C_END], BF16)
        ppt = sb("pp_s", [128, P_END], F32)
        sc = sb("sc_s", [128, 96], F32)
        bsb = sb("bsb", [128, 512], F32)
        wsT = sb("wsT", [128, 512], BF16)
        esink = sb("esink", [128, 8], F32)
        esink2 = sb("esink2", [128, 4], F32)
        dSb = [sb("dSb%d" % i, [128, 8, 128], F32) for i in range(2)]
        xT = sb("xT", [128, KC, TG], F32)
        xh = sb("xh", [128, KC, 128], F32)
        sq = [sb("sq%d" % i, [128, 640], BF16) for i in range(2)]
        rstd = sb("rstd", [128, 640], F32)
        xnT = sb("xnT", [128, KC, 640], BF16)
        cosT = sb("cosT", [128, 640], F32)
        sinT = sb("sinT", [128, 640], F32)
        KT = sb("KT", [128, 2, 1024], BF16)
        V2 = sb("V2", [128, 8, 256], BF16)
        NF = 9
        Fp = [sb("F%d" % i, [128, 640], F32) for i in range(NF)]
        Lx = [sb("Lx%d" % i, [128, 576], F32) for i in range(2)]
        esub = sb("esub", [128, 2, 24], F32)
        ech = [sb("ech%d" % i, [128, 2, 24], F32) for i in range(2)]
        TRW = 9216
        TR = sb("TR", [128, TRW], BF16)
        S1 = sb("S1", [128, 4, 128], F32)
        S2 = sb("S2", [128, 4, 128], F32)
        Sb = [sb("Sb%d" % i, [128, 128], BF16) for i in range(16)]
        At = [sb("At%d" % i, [128, 64], BF16) for i in range(16)]
        Sall = [sb("Sall%d" % i, [128, 7, 128], F32) for i in range(2)]
        osq = sb("osq", [128, 512], BF16)
        yaT = sb("yaT", [128, 4, TG], BF16)
        ybT = sb("ybT", [128, 4, TG], BF16)
        ycT = sb("ycT", [128, 4, TG], BF16)
        exb = sb("exb", [128, 2, 512], BF16)
        exs = sb("exs", [128, 2, 512], F32)
        cst = [sb("cst%d" % i, [128, 512], BF16) for i in range(2)]
        big = sb("big", [128, 32, TG], BF16)
        rl = [sb("rl%d" % i, [128, 512], BF16) for i in range(2)]
        bnst = sb("bnst", [128, 16], F32)
        wbuf = [sb("wb%d" % i, [128, 4096], BF16) for i in range(NWB)]
        psb = [st.enter_context(nc.psum_tensor("ps%d" % i, [128, 512], F32)) for i in range(8)]
        print("sbuf bytes remaining:", nc.sbuf_bytes_remaining)

        def trk(i):
            return ('TR', i)
        Vt = TR[:, 0:2048].rearrange("p (t c) -> p t c", t=4)
        def hv(s, j):
            o = 2048 + (s * 7 + j) * 512
            return TR[:, o:o + 512], trk(4 + s * 7 + j)
        uT = TR[:, 0:2048].rearrange("p (f t) -> p f t", f=4)
        vn = TR[:, 2048:4096].rearrange("p (t c) -> p t c", t=4)
        QT = TR[:, 0:2048].rearrange("p (f t) -> p f t", f=4)
        qraw = TR[:, 2048:2688]
        PTt = [TR[:, 3072 + i * 512:3072 + i * 512 + 384] for i in range(2)]
        PT = [[TR[:, 4096 + (hh_ * 6 + i) * 384:4096 + (hh_ * 6 + i + 1) * 384] for i in range(6)] for hh_ in range(2)]
        K_V = [trk(i) for i in range(4)]
        K_UT = [trk(i) for i in range(4)]
        K_VN = [trk(4 + i) for i in range(4)]
        K_QT = [trk(i) for i in range(4)]
        K_QRAW = [trk(4), trk(5)]
        qraw2 = TR[:, 3072:3712]
        K_QRAW2 = [trk(6), trk(7)]
        K_PTT = [trk(6), trk(7)]
        K_PT = [[[trk(8 + ((hh_ * 6 + i) * 384) // 512), trk(8 + ((hh_ * 6 + i + 1) * 384 - 1) // 512)] for i in range(6)] for hh_ in range(2)]

        ident = cm[:, C_ID:C_ID + 128]
        Pm = cm[:, C_PM:C_PM + 128]
        M3 = cm[:, C_M3:C_M3 + 384]
        MC = cm[:, C_MC:C_MC + 128]
        MH = [cm[:, C_MH1:C_MH1 + 64], cm[:, C_MH2:C_MH2 + 64]]
        ones_b = cm[:, C_ONE:C_ONE + 128]
        NEGM = {-1: cm[:, C_NB:C_NB + 128], 1: cm[:, C_NA:C_NA + 128], 'c': cm[:, C_NC:C_NC + 128]}
        SC_LN1, SC_LN2, SC_FN, SC_AN = 0, 16, 32, 40
        SC_HB, SC_HA, SC_NHB = 44, 60, 76
        SC_ONE, SC_ZERO = 92, 93

        rot_i = [0]

        def rot():
            i = rot_i[0] % 5
            rot_i[0] += 1
            return psb[i], ('ps', i)
        ACC = [(psb[5], ('ps', 5)), (psb[6], ('ps', 6))]
        TPB = psb[7][:].bitcast(BF16)
        K_TP = ('ps', 7)
        f_i = [0]

        def ftile():
            i = f_i[0] % NF
            f_i[0] += 1
            return Fp[i], ('F', i)

        def mm(out, pairs, reads, writes):
            def fn(e, out=out, pairs=pairs):
                n = len(pairs)
                for i, (l, r) in enumerate(pairs):
                    inst = e.matmul(out, lhsT=l, rhs=r, start=(i == 0), stop=(i == n - 1))
                return inst
            S.op('pe', fn, reads, writes)

        def act(out, in_, func, reads, writes, **kw):
            S.op('act', lambda e: e.activation(out=out, in_=in_, func=func, **kw), reads, writes)

        def tt(eng, out, in0, in1, op, reads, writes):
            S.op(eng, lambda e: e.tensor_tensor(out=out, in0=in0, in1=in1, op=op), reads, writes)

        def ts(eng, out, in0, s1, s2, op0, op1, reads, writes):
            if s2 is None:
                S.op(eng, lambda e: e.tensor_scalar(out=out, in0=in0, scalar1=s1, scalar2=None, op0=op0), reads, writes)
            else:
                S.op(eng, lambda e: e.tensor_scalar(out=out, in0=in0, scalar1=s1, scalar2=s2, op0=op0, op1=op1), reads, writes)

        def stt(eng, out, in0, scalar, in1, op0, op1, reads, writes):
            S.op(eng, lambda e: e.scalar_tensor_tensor(out=out, in0=in0, scalar=scalar, in1=in1, op0=op0, op1=op1), reads, writes)

        S.dma('sp', cm[:], cm_d[:, :], writes=['cm'], slot='c0')
        S.dma('sp', ppt[:], pp_d[:, :], writes=['pp'], slot='c1')
        S.op('pool', lambda e: e.memset(sc[:, SC_ONE:SC_ONE + 1], 1.0), writes=['sc1'])
        S.op('pool', lambda e: e.memset(sc[:, SC_ZERO:SC_ZERO + 1], 0.0), writes=['sc1'])
        for i in range(2):
            S.op('pool', lambda e, i=i: e.memset(Lx[i][:], 0.0), writes=[('Lx', i)])
        sD = float(np.sqrt(D))
        ts('dve', sc[:, SC_LN1:SC_LN1 + 16], ppt[:, P_LN1:P_LN1 + 16], sD, None, ALU.mult, None, ['pp'], ['sc'])
        ts('dve', sc[:, SC_LN2:SC_LN2 + 16], ppt[:, P_LN2:P_LN2 + 16], sD, None, ALU.mult, None, ['pp'], ['sc'])
        ts('dve', sc[:, SC_FN:SC_FN + 8], ppt[:, P_FN:P_FN + 8], sD, None, ALU.mult, None, ['pp'], ['sc'])
        ts('dve', sc[:, SC_AN:SC_AN + 2], ppt[:, P_AN:P_AN + 2], float(np.sqrt(128.0) * 0.5), None, ALU.mult, None, ['pp'], ['sc'])
        lbt = sc[:, SC_HB:SC_HB + 16]
        S.op('pool', lambda e: e.memset(sc[:, SC_HB:SC_HB + 16], 0.0), writes=['sc2'])
        if NL > 1:
            tt('dve', sc[:, SC_HA + 8:SC_HA + 16], ppt[:, P_LBL + 8:P_LBL + 16], ppt[:, P_LBL:P_LBL + 8], ALU.subtract, ['pp'], ['sc3'])
            act(sc[:, SC_HB + 8:SC_HB + 16], sc[:, SC_HA + 8:SC_HA + 16], AF.Sigmoid, ['sc3', 'sc2'], ['sc2'])
        ts('dve', sc[:, SC_HA:SC_HA + 16], sc[:, SC_HB:SC_HB + 16], -0.5, 0.5, ALU.mult, ALU.add, ['sc2', 'sc3'], ['sc3'])
        ts('dve', sc[:, SC_NHB:SC_NHB + 16], sc[:, SC_HA:SC_HA + 16], -1.0, None, ALU.mult, None, ['sc3'], ['sc4'])
        ts('dve', sc[:, SC_HB:SC_HB + 16], sc[:, SC_HB:SC_HB + 16], 0.5, 0.5, ALU.mult, ALU.add, ['sc2', 'sc3'], ['sc2'])
        SCK = ['sc', 'sc1', 'sc2', 'sc3', 'sc4']

        stage = big[:].rearrange("p a b -> p (a b)").bitcast(F32)
        cast_engs = ['act', 'dve', 'dve']
        ci = 0
        def main_body():
            nonlocal ci
            for l in (range(1) if NL == 2 else range(NL)):
                for bi in range(NB):
                    s2 = ci % 2
                    w = ci % NWB
                    S.dma('sp', stage[:, s2 * 4096:(s2 + 1) * 4096], wsrc[l, bi], writes=[('stg', s2)], slot='pg%d' % s2)
                    eng = cast_engs[ci % 3]
                    if eng == 'act':
                        act(wbuf[w][:], stage[:, s2 * 4096:(s2 + 1) * 4096], AF.Copy, [('stg', s2)], [('wb', w)])
                    else:
                        S.op(eng, lambda e, w=w, s2=s2: e.tensor_copy(out=wbuf[w][:], in_=stage[:, s2 * 4096:(s2 + 1) * 4096]),
                             [('stg', s2)], [('wb', w)])
                    S.dma('sp', wbf[l, bi], wbuf[w][:], reads=[('wb', w)], writes=[('wbf', l, bi)], slot='pw%d' % w)
                    ci += 1
            BIGK = [('big', i) for i in range(32)]
            S.op('pool', lambda e: e.memset(sc[:, SC_ZERO:SC_ZERO + 1], 0.0), writes=BIGK + ['sc1', ('stg', 0), ('stg', 1)])

            cast1_steps = [(1, bi, e8) for bi in range(NB) for e8 in range(8)] if NL == 2 else []
            c1 = {'in': 0, 'done': 0, 'on': False}

            def cast1_in():
                i = c1['in']
                if i >= len(cast1_steps):
                    return
                l1, bi, e8 = cast1_steps[i]
                s_ = i % 2
                S.dma('pool', exs[:, s_, :], wsrc[l1, bi, :, e8 * 512:(e8 + 1) * 512], writes=[('exs', s_)], slot='cg%d' % s_)
                c1['in'] += 1

            def cast1_step():
                i = c1['done']
                if i >= len(cast1_steps):
                    return
                if c1['in'] == i:
                    cast1_in()
                cast1_in()
                l1, bi, e8 = cast1_steps[i]
                s_ = i % 2
                S.op('pool', lambda e, s_=s_: e.tensor_copy(out=cst[s_][:], in_=exs[:, s_, :]), [('exs', s_)], [('cst', s_)])
                S.dma('pool', wbf[l1, bi, :, e8 * 512:(e8 + 1) * 512], cst[s_][:], reads=[('cst', s_)], writes=[('wbfp', l1, bi, e8)], slot='co%d' % s_)
                c1['done'] += 1

            def wbf_keys(l, bi, part):
                if NL == 2 and l == 1:
                    return [('wbfp', l, bi, e8) for e8 in range(2 if part else 8)]
                return [('wbf', l, bi)]

            seq = []
            for l in range(NL):
                for g in range(NG):
                    seq += [(l, 'ai', False)] + [(l, 'h%d' % h, True) for h in range(4)]
                for g in range(NG - 1, -1, -1):
                    seq += [(l, n, False) for n in ['ckv', 'ai', 'h0', 'h1', 'gz0', 'gz1', 'h2', 'gz2', 'gz3', 'h3', 'gz4', 'gz5', 'bu', 'bv', 'cq'] +
                            ['br%d' % i for i in range(4)] + ['wo0', 'wo1'] +
                            ['up%d' % i for i in range(8)] + ['dn%d' % i for i in range(8)]]
            ws = {'issued': 0, 'used': 0}

            def w_issue():
                k = ws['issued']
                if k >= len(seq):
                    return
                l, n, part = seq[k]
                slot = k % NWB
                bi = BI[n]
                if part:
                    S.dma('sp', wbuf[slot][:, 0:1024], wbf[l, bi, :, 0:1024], reads=wbf_keys(l, bi, True), writes=[('wb', slot)], slot='w%d' % slot)
                else:
                    S.dma('sp', wbuf[slot][:], wbf[l, bi], reads=wbf_keys(l, bi, False), writes=[('wb', slot)], slot='w%d' % slot)
                ws['issued'] += 1

            def w_use(l, n):
                k = ws['used']
                assert seq[k][0] == l and seq[k][1] == n, (seq[k], l, n)
                while ws['issued'] < min(k + NWB, len(seq)):
                    w_issue()
                ws['used'] += 1
                slot = k % NWB
                if c1['on']:
                    cast1_step()
                return wbuf[slot], ('wb', slot)

            def wv_f(wb):
                return wb[:].rearrange("p (f k c) -> p f k c", f=4, k=8)

            def rmsnorm(segs, lncol, l, out_f32=False):
                for (x3, xkeys, n, c0, okeys) in segs:
                    bank, bk = rot()
                    for kc in range(KC):
                        s = sq[kc % 2]
                        act(s[:, 0:n], x3[:, kc, :], AF.Square, xkeys, [('sq', kc % 2)])
                        S.op('pe', lambda e, s=s, n=n, kc=kc, bank=bank: e.matmul(bank[:, 0:n], lhsT=ones_b, rhs=s[:, 0:n], start=(kc == 0), stop=(kc == KC - 1)),
                             [('sq', kc % 2), 'cm'], [bk])
                    tmpf, ktmp = ftile()
                    ts('dve', tmpf[:, 0:n], bank[:, 0:n], float(D * EPS), None, ALU.add, None, [bk], [ktmp])
                    act(tmpf[:, 0:n], tmpf[:, 0:n], AF.Ln, [ktmp], [ktmp])
                    act(rstd[:, c0:c0 + n], tmpf[:, 0:n], AF.Exp, [ktmp], [('rstd', c0)], scale=-0.5)
                    for kc in range(KC):
                        eng = 'dve'
                        if out_f32:
                            stt(eng, x3[:, kc, :], x3[:, kc, :], sc[:, lncol + kc:lncol + kc + 1], rstd[:, c0:c0 + n], ALU.mult, ALU.mult,
                                [('rstd', c0)] + SCK + xkeys, xkeys)
                        else:
                            stt(eng, xnT[:, kc, c0:c0 + n], x3[:, kc, :], sc[:, lncol + kc:lncol + kc + 1], rstd[:, c0:c0 + n], ALU.mult, ALU.mult,
                                [('rstd', c0)] + SCK + xkeys, okeys)

            def xn_keys(c0):
                return [('xn', c0)]

            def proj_fm(wb, wk, fb, c0, n):
                bank, bk = rot()
                wv = wv_f(wb)
                mm(bank[:, 0:n], [(wv[:, fb, kc, :], xnT[:, kc, c0:c0 + n]) for kc in range(KC)],
                   [wk, ('xn', 0), ('xn', 128)], [bk])
                return bank, bk

            def proj_tm(wb, wk, fb0, nfb, c0):
                bank, bk = rot()
                wv = wv_f(wb)
                mm(bank[:, 0:nfb * 128].rearrange("p (f c) -> p f c", f=nfb),
                   [(xnT[:, kc, c0:c0 + 128], wv[:, fb0:fb0 + nfb, kc, :]) for kc in range(KC)],
                   [wk, ('xn', 0), ('xn', 128)], [bk])
                return bank, bk

            def rope_multi(calls):
                qr = [(qraw, K_QRAW), (qraw2, K_QRAW2)]
                pend = None
                st_ = []
                for i, (pf, c0, n, out, okeys) in enumerate(calls):
                    bank, bk = pf()
                    q_, qk_ = qr[i % 2]
                    act(q_[:, 0:n], bank[:, 0:n], AF.Copy, [bk], qk_)
                    cur = (bank, bk, q_, qk_, c0, n, out, okeys)
                    if pend is not None:
                        rope_finish(*pend)
                    pend = cur
                if pend is not None:
                    rope_finish(*pend)

            def rope_finish(bank, bk, q_, qk_, c0, n, out, okeys):
                b2, bk2 = rot()
                mm(b2[:, 0:n], [(Pm, q_[:, 0:n])], qk_ + ['cm'], [bk2])
                t1, k1 = ftile()
                tt('dve', t1[:, 0:n], bank[:, 0:n], cosT[:, c0:c0 + n], ALU.mult, [bk, 'cos'] + qk_, [k1])
                t2, k2 = ftile()
                tt('dve', t2[:, 0:n], b2[:, 0:n], sinT[:, c0:c0 + n], ALU.mult, [bk2, 'sin'], [k2])
                tt('pool', out, t1[:, 0:n], t2[:, 0:n], ALU.add, [k1, k2], okeys)

            def gcols(g):
                return slice(g * TG, (g + 1) * TG)

            def load_group(l, g, halo):
                src = xin if l == 0 else xs[l - 1]
                srck = ('xs', l - 1, g)
                S.dma('sp', xT[:], src[:, :, g * TG:(g + 1) * TG].rearrange("k p t -> p k t"), reads=[srck], writes=['xT'], slot='lx')
                if halo and g > 0:
                    S.dma('sp', xh[:], src[:, :, g * TG - 128:g * TG].rearrange("k p t -> p k t"), reads=[('xs', l - 1, g - 1)], writes=['xh'], slot='lh')

            def ring(t):
                return (t % 8) * 128

            def hgrn_prep2(l, h, dirs, zf, zq_bank, zq_k, hs, need_q):
                T = {}
                for d in dirs:
                    T[d] = [ftile(), ftile(), ftile()]
                sca = {}
                for d in dirs:
                    col = l * 8 + d * 4 + h
                    sca[d] = (sc[:, SC_HB + col:SC_HB + col + 1], sc[:, SC_HA + col:SC_HA + col + 1], sc[:, SC_NHB + col:SC_NHB + col + 1])
                for d in dirs:
                    (t0, k0) = T[d][0]
                    act(t0[:, 0:512], zf[d][0][:, :], AF.Tanh, [zf[d][1]], [k0], scale=0.5)
                for d in dirs:
                    (t0, k0), (t1, k1) = T[d][0], T[d][1]
                    hb, ha, nha = sca[d]
                    ts('dve', t1[:, 0:512], t0[:, 0:512], nha, ha, ALU.mult, ALU.add, [k0] + SCK, [k1])
                for d in dirs:
                    (t0, k0) = T[d][0]
                    hb, ha, nha = sca[d]
                    act(t0[:, 0:512], t0[:, 0:512], AF.Ln, [k0] + SCK, [k0], scale=ha, bias=hb)
                views = {}
                for d in dirs:
                    (t0, k0) = T[d][0]
                    LX = Lx[d]
                    S.op('dve', lambda e, LX=LX, t0=t0: e.tensor_tensor_scan(out=LX[:, 1:513], data0=sc[:, SC_ONE:SC_ONE + 1].to_broadcast([128, 512]),
                                                                             data1=t0[:, 0:512], initial=0.0, op0=ALU.mult, op1=ALU.add),
                         [k0] + SCK, [('Lx', d)])
                    views[d] = (LX[:, 0:512].rearrange("p (c t) -> p c t", t=64), LX[:, 1:513].rearrange("p (c t) -> p c t", t=64),
                                LX[:, 64:576].rearrange("p (c t) -> p c t", t=64))
                for d in dirs:
                    (t2, k2) = T[d][2]
                    L0, L1, L64 = views[d]
                    D3 = t2[:, 0:512].rearrange("p (c t) -> p c t", t=64)
                    tt('dve', D3, (L1 if d == 0 else L0), L0[:, :, 32:33].to_broadcast([128, 8, 64]), ALU.subtract, [('Lx', d)], [k2])
                for d in dirs:
                    L0, L1, L64 = views[d]
                    es = esub[:, d, :]
                    tt('dve', es[:, 0:8], L0[:, :, 32], L0[:, :, 0], ALU.subtract, [('Lx', d)], [('esub', d)])
                    tt('dve', es[:, 8:16], L64[:, :, 0], L0[:, :, 0], ALU.subtract, [('Lx', d)], [('esub', d)])
                    tt('dve', es[:, 16:24], L64[:, :, 0], L0[:, :, 32], ALU.subtract, [('Lx', d)], [('esub', d)])
                for d in dirs:
                    (t0, k0), (t2, k2) = T[d][0], T[d][2]
                    act(t0[:, 0:512], t2[:, 0:512], AF.Exp, [k2, ('Lx', d)], [k0])
                    act(t2[:, 0:512], t2[:, 0:512], AF.Exp, [k2], [k2], scale=-1.0)
                    act(ech[hs][:, d, :], esub[:, d, :], AF.Exp, [('esub', d)], [('ech', hs, d)])
                for d in dirs:
                    (t0, k0), (t1, k1), (t2, k2) = T[d]
                    E, Ei = t0, t2
                    Qm, kQ = hv(hs, d)
                    Km, kK = hv(hs, 2 + d)
                    if need_q:
                        tt('dve', Qm, zq_bank[:, :], (E if d == 0 else Ei)[:, 0:512], ALU.mult, [zq_k, k0, k2], [kQ])
                    tt('pool', Km, t1[:, 0:512], (Ei if d == 0 else E)[:, 0:512], ALU.mult, [k1, k0, k2], [kK])
                for d in dirs:
                    Km, kK = hv(hs, 2 + d)
                    KmT, kKT = hv(hs, 4 + d)

                    def tps(e, d=d, Km=Km):
                        for i in range(4):
                            r = e.transpose(TPB[:, d * 512 + i * 128:d * 512 + (i + 1) * 128], Km[:, i * 128:(i + 1) * 128], ident)
                        return r
                    S.op('pe', tps, [kK, 'cm'], [K_TP])
                    S.op('act', lambda e, d=d, KmT=KmT: e.activation(out=KmT, in_=TPB[:, d * 512:(d + 1) * 512], func=AF.Copy), [K_TP], [kKT])

            def escal(hs, d, c):
                e = ech[hs]
                ea, eb, ec = e[:, d, c:c + 1], e[:, d, 8 + c:9 + c], e[:, d, 16 + c:17 + c]
                return (ea, eb, ec) if d == 0 else (ec, eb, ea)

            def ds_compute(d, h, hs, c):
                tile_i, po = c // 2, (c % 2) * 64
                KmT, kKT = hv(hs, 4 + d)
                KmT3 = KmT.rearrange("p (t k) -> p t k", t=4)
                bank, bk = rot()
                mm(bank[:, 0:128], [(KmT3[po:po + 64, tile_i, :], Vt[po:po + 64, tile_i, h * 128:(h + 1) * 128])],
                   [kKT] + K_V, [bk])
                e1, eb, e3 = escal(hs, d, c)
                act(dSb[d][:, c, :], bank[:, 0:128], AF.Identity, [bk, ('ech', hs, d)], [('dS', d, c)], scale=e3)

            def st_src(d, h, k, Sst, skey):
                if k == 0:
                    return Sst[:, h, :], skey
                return Sall[d][:, k - 1, :], ('Sall', d, k - 1)

            def st_dst(d, h, k, Sst, skey):
                if k == 7:
                    return Sst[:, h, :], skey
                return Sall[d][:, k, :], ('Sall', d, k)

            def state_step(d, h, hs, k, c, Sst, skey):
                e1, eb, e3 = escal(hs, d, c)
                s_ap, s_k = st_src(d, h, k, Sst, skey)
                d_ap, d_k = st_dst(d, h, k, Sst, skey)
                stt('dve', d_ap, s_ap, eb, dSb[d][:, c, :], ALU.mult, ALU.add, [s_k, ('ech', hs, d), ('dS', d, c)], [d_k])

            def chunk_sb(d, h, hs, k, c, Sst, skey):
                e1, eb, e3 = escal(hs, d, c)
                i = d * 8 + k
                s_ap, s_k = st_src(d, h, k, Sst, skey)
                act(Sb[i][:], s_ap, AF.Identity, [s_k, ('ech', hs, d)], [('Sb', i)], scale=e1)

            def chunk_at(d, h, hs, k, c):
                tile_i, po = c // 2, (c % 2) * 64
                cs = slice(c * 64, (c + 1) * 64)
                Qm, kQ = hv(hs, d)
                Km, kK = hv(hs, 2 + d)
                i = d * 8 + k
                bank, bk = rot()
                mm(bank[po:po + 64, 0:64], [(Km[:, cs], Qm[:, cs])], [kK, kQ], [bk])
                tt('dve', At[i][po:po + 64, :], bank[po:po + 64, 0:64], MH[d][po:po + 64, :], ALU.mult, [bk, 'cm'], [('At', i)])

            def chunk_fin(d, h, hs, k, c, acc, acck):
                tile_i, po = c // 2, (c % 2) * 64
                cs = slice(c * 64, (c + 1) * 64)
                Qm, kQ = hv(hs, d)
                i = d * 8 + k

                def fn(e_):
                    e_.matmul(acc[:, cs], lhsT=Sb[i][:], rhs=Qm[:, cs], start=True, stop=False)
                    return e_.matmul(acc[:, cs], lhsT=Vt[po:po + 64, tile_i, h * 128:(h + 1) * 128], rhs=At[i][po:po + 64, :], start=False, stop=True)
                S.op('pe', fn, [('Sb', i), ('At', i), kQ] + K_V, [acck])

            def exchange(l, kind):
                if kind == 'S':
                    S.dma('sp', exS_i[l].ap(), S1[:].rearrange("p h v -> p (h v)"), reads=['S1'], writes=[('exSi', l)], slot='ex0')
                    S.coll(lambda e: e.collective_compute("AllGather", ALU.bypass, replica_groups=[[0, 1], [2, 3], [4, 5], [6, 7]],
                                                          ins=[exS_i[l].ap().opt()], outs=[exS_o[l].ap().opt()]),
                           reads=[('exSi', l)], writes=[('exSo', l)], slot='ccS%d' % l)
                    S.dma('sp', exs[:], exS_o[l].ap().rearrange("(r p) n -> p r n", p=128), reads=[('exSo', l)], writes=[('exs', 0), ('exs', 1)], slot='ex1')
                    s2f = S2[:].rearrange("p h v -> p (h v)")
                    ts('dve', s2f, exs[:, 0, :], ppt[:, P_SEL:P_SEL + 1], None, ALU.mult, None, [('exs', 0), ('exs', 1), 'pp'], ['S2'])
                    stt('dve', s2f, exs[:, 1, :], ppt[:, P_SEL + 1:P_SEL + 2], s2f, ALU.mult, ALU.add, [('exs', 0), ('exs', 1), 'pp', 'S2'], ['S2'])
                else:
                    tl = NT - 1
                    S.dma('sp', exK_i[l].ap()[:, 0:256].rearrange("p (a b) -> p a b", a=2), KT[:, :, ring(tl):ring(tl) + 128],
                          reads=['KT'], writes=[('exKi', l)], slot='ex2')
                    S.dma('sp', exK_i[l].ap()[:, 256:512], V2[:, tl % 8, :], reads=['V2'], writes=[('exKi', l, 1)], slot='ex3')
                    S.coll(lambda e: e.collective_compute("AllGather", ALU.bypass, replica_groups=[[0, 1], [2, 3], [4, 5], [6, 7]],
                                                          ins=[exK_i[l].ap().opt()], outs=[exK_o[l].ap().opt()]),
                           reads=[('exKi', l), ('exKi', l, 1)], writes=[('exKo', l)], slot='ccK%d' % l)
                    S.dma('sp', exb[:], exK_o[l].ap().rearrange("(r p) n -> p r n", p=128), reads=[('exKo', l)], writes=['exb'], slot='ex4')
                    kdst = KT[:, :, ring(NT):ring(NT) + 128]
                    e0 = exb[:, 0, 0:256].rearrange("p (a b) -> p a b", a=2)
                    e1 = exb[:, 1, 0:256].rearrange("p (a b) -> p a b", a=2)
                    ts('dve', kdst, e0, ppt[:, P_SEL:P_SEL + 1], None, ALU.mult, None, ['exb', 'pp', 'KT'], ['KT'])
                    stt('dve', kdst, e1, ppt[:, P_SEL + 1:P_SEL + 2], kdst, ALU.mult, ALU.add, ['exb', 'pp', 'KT'], ['KT'])
                    vdst = V2[:, NT % 8, :]
                    ts('dve', vdst, exb[:, 0, 256:512], ppt[:, P_SEL:P_SEL + 1], None, ALU.mult, None, ['exb', 'pp', 'V2'], ['V2'])
                    stt('dve', vdst, exb[:, 1, 256:512], ppt[:, P_SEL + 1:P_SEL + 2], vdst, ALU.mult, ALU.add, ['exb', 'pp', 'V2'], ['V2'])

            stop_cnt = {}

            def stop(tag):
                stop_cnt[tag] = stop_cnt.get(tag, 0) + 1
                if DEBUG_STOP[0] == tag or DEBUG_STOP[0] == '%s#%d' % (tag, stop_cnt[tag]):
                    raise _Stop()

            for l in range(NL):
                last = (l == NL - 1)
                S.dma('sp', bsb[:], bs_d[l:l + 1, :].partition_broadcast(128), writes=['bsb'], slot='c2')
                ft, fk = ftile()
                S.dma('sp', ft[:, 0:512], wst_d[l], writes=[fk], slot='c3')
                S.op('dve', lambda e, ft=ft: e.tensor_copy(out=wsT[:], in_=ft[:, 0:512]), [fk], ['wsT'])
                S.dma('sp', esink[:], sink_d[l:l + 1, :].partition_broadcast(128), writes=['esink'], slot='c4')
                act(esink[:], esink[:], AF.Exp, ['esink'], ['esink'])
                es3 = esink[:].rearrange("p (q two) -> p q two", two=2)
                S.op('dve', lambda e, es3=es3: e.tensor_copy(out=esink2[0:64, :], in_=es3[0:64, :, 0]), ['esink'], ['esink2'])
                S.op('dve', lambda e, es3=es3: e.tensor_copy(out=esink2[64:128, :], in_=es3[64:128, :, 1]), ['esink', 'esink2'], ['esink2'])
                S.op('pool', lambda e: e.memset(S1[:], 0.0), writes=['S1'])

                stop('consts')
                for g in range(NG):
                    load_group(l, g, halo=False)
                    rmsnorm([(xT, ['xT'], TG, 128, xn_keys(128))], SC_LN1 + l * 8, l)
                    S.dma('sp', s1s[l, g], S1[:].rearrange("p h v -> p (h v)"), reads=['S1'], writes=[('s1s', l, g)], slot='st1')
                    wb, wk = w_use(l, 'ai')
                    for t in range(4):
                        bank, bk = proj_tm(wb, wk, 0, 4, 128 + t * 128)
                        act(Vt[:, t, :], bank[:, :], AF.Copy, [bk], [K_V[t]])
                    for h in range(4):
                        hs = h % 2
                        wb, wk = w_use(l, 'h%d' % h)
                        zb, zk = proj_fm(wb, wk, 0, 128, TG)
                        hgrn_prep2(l, h, [0], {0: (zb, zk)}, None, None, hs, need_q=False)
                        for c in range(8):
                            ds_compute(0, h, hs, c)
                        for c in range(8):
                            e1_, eb_, e3_ = escal(hs, 0, c)
                            stt('dve', S1[:, h, :], S1[:, h, :], eb_, dSb[0][:, c, :], ALU.mult, ALU.add, ['S1', ('ech', hs, 0), ('dS', 0, c)], ['S1'])
                stop('pass1')
                exchange(l, 'S')
                stop('exS')

                c1['on'] = (l == 0 and NL == 2)
                for g in range(NG - 1, -1, -1):
                    has_lo = g > 0
                    has_hi = True
                    load_group(l, g, halo=True)
                    S.dma('sp', S1[:].rearrange("p h v -> p (h v)"), s1s[l, g], reads=[('s1s', l, g)], writes=['S1'], slot='ld1')
                    c_lo = g * TG - (128 if has_lo else 0)
                    ncs = TG + (128 if has_lo else 0)
                    o_lo = 0 if has_lo else 128
                    S.dma('sp', cosT[:, o_lo:640], rope_d[0, :, c_lo:c_lo + ncs], writes=['cos'], slot='lc')
                    S.dma('sp', sinT[:, o_lo:640], rope_d[1, :, c_lo:c_lo + ncs], writes=['sin'], slot='ls')
                    segs = []
                    if has_lo:
                        segs.append((xh, ['xh'], 128, 0, xn_keys(0)))
                    segs.append((xT, ['xT'], TG, 128, xn_keys(128)))
                    stop('p2load')
                    rmsnorm(segs, SC_LN1 + l * 8, l)
                    stop('p2norm')
                    wb, wk = w_use(l, 'ckv')
                    tiles = ([4 * g - 1] if has_lo else []) + [4 * g + i for i in range(4)]
                    kcalls = []
                    for kvf in range(2):
                        if has_lo:
                            kcalls.append((lambda kvf=kvf: proj_fm(wb, wk, kvf, 0, 128), 0, 128,
                                           KT[:, kvf, ring(4 * g - 1):ring(4 * g - 1) + 128], ['KT']))
                        kcalls.append((lambda kvf=kvf: proj_fm(wb, wk, kvf, 128, TG), 128, TG,
                                       KT[:, kvf, ring(4 * g):ring(4 * g) + TG], ['KT']))
                    rope_multi(kcalls)
                    stop('krope')
                    for t in tiles:
                        c0 = 128 + (t - 4 * g) * 128
                        bank, bk = proj_tm(wb, wk, 2, 2, c0)
                        act(V2[:, t % 8, :], bank[:, 0:256], AF.Copy, [bk], ['V2'])
                    stop('kv')
                    if g == NG - 1:
                        exchange(l, 'K')
                    stop('exK')
                    wb, wk = w_use(l, 'ai')
                    for t in range(4):
                        bank, bk = proj_tm(wb, wk, 0, 4, 128 + t * 128)
                        act(Vt[:, t, :], bank[:, :], AF.Copy, [bk], [K_V[t]])
                    def hg_prep(h):
                            hs = h % 2
                            wb, wk = w_use(l, 'h%d' % h)
                            zq, zqk = proj_fm(wb, wk, 2, 128, TG)
                            zf = {}
                            for d in range(2):
                                zf[d] = proj_fm(wb, wk, d, 128, TG)
                            hgrn_prep2(l, h, [0, 1], zf, zq, zqk, hs, need_q=True)
                            zg, zgk = proj_fm(wb, wk, 3, 128, TG)
                            thg, kthg = ftile()
                            act(thg[:, 0:512], zg[:, :], AF.Tanh, [zgk], [kthg], scale=0.5)
                            sg, ksg = hv(hs, 6)
                            stt('dve', sg, thg[:, 0:512], 1.0, zg[:, :], ALU.add, ALU.mult, [kthg, zgk], [ksg])
                    def gates_blocks(js):
                        for j in js:
                            wb_, wk_ = w_use(l, 'gz%d' % j)
                            for fb in range(4):
                                bank, bk = proj_fm(wb_, wk_, fb, 128, TG)
                                act(big[:, j * 4 + fb, :], bank[:, :], AF.Sigmoid, [bk], [('big', j * 4 + fb)])

                    def hg_recur(h):
                        hs = h % 2
                        sg, ksg = hv(hs, 6)
                        for d in range(2):
                            for c in range(8):
                                ds_compute(d, h, hs, c)
                        chunk_sb(0, h, hs, 0, 0, S1, 'S1')
                        chunk_sb(1, h, hs, 0, 7, S2, 'S2')
                        for k in range(8):
                            chunk_at(0, h, hs, k, k)
                            chunk_at(1, h, hs, k, 7 - k)
                        for k in range(8):
                            state_step(0, h, hs, k, k, S1, 'S1')
                            state_step(1, h, hs, k, 7 - k, S2, 'S2')
                        if h < 3:
                            gates_blocks([2 * h, 2 * h + 1])
                        for k in range(8):
                            if k > 0:
                                chunk_sb(0, h, hs, k, k, S1, 'S1')
                                chunk_sb(1, h, hs, k, 7 - k, S2, 'S2')
                            chunk_fin(0, h, hs, k, k, ACC[0][0], ACC[0][1])
                            chunk_fin(1, h, hs, k, 7 - k, ACC[1][0], ACC[1][1])
                        o1, ko1 = ftile()
                        act(o1[:, 0:512], ACC[0][0][:, :], AF.Copy, [ACC[0][1]], [ko1])
                        o, ko = ftile()
                        tt('dve', o[:, 0:512], ACC[1][0][:, :], o1[:, 0:512], ALU.add, [ACC[1][1], ko1], [ko])
                        act(osq[:], o[:, 0:512], AF.Square, [ko], ['osq'])
                        bank, bk = rot()
                        mm(bank[:, :], [(ones_b, osq[:])], ['osq', 'cm'], [bk])
                        rs, krs = ftile()
                        rs0, krs0 = ftile()
                        ts('dve', rs0[:, 0:512], bank[:, :], float(128 * EPS), None, ALU.add, None, [bk], [krs0])
                        act(rs0[:, 0:512], rs0[:, 0:512], AF.Ln, [krs0], [krs0])
                        act(rs[:, 0:512], rs0[:, 0:512], AF.Exp, [krs0], [krs], scale=-0.5)
                        t_, kt_ = ftile()
                        tt('dve', t_[:, 0:512], o[:, 0:512], rs[:, 0:512], ALU.mult, [ko, krs], [kt_])
                        stt('dve', yaT[:, h, :], t_[:, 0:512], sc[:, SC_AN + l:SC_AN + l + 1], sg, ALU.mult, ALU.mult, [kt_, ksg] + SCK, [('ya', h)])
                    hg_prep(0)
                    for h in range(4):
                        if h + 1 < 4:
                            hg_prep(h + 1)
                        hg_recur(h)
                    stop('hgrn')
                    wb, wk = w_use(l, 'bu')
                    for fb in range(4):
                        bank, bk = proj_fm(wb, wk, fb, 128, TG)
                        act(uT[:, fb, :], bank[:, :], AF.Gelu_apprx_tanh, [bk], [K_UT[fb]])
                    wb, wk = w_use(l, 'bv')
                    for t in range(4):
                        bank, bk = proj_tm(wb, wk, 0, 4, 128 + t * 128)
                        v, kv = ftile()
                        act(v[:, 0:512], bank[:, :], AF.Gelu_apprx_tanh, [bk], [kv])
                        S.op('dve', lambda e, v=v: e.bn_stats(out=bnst[:, 0:6], in_=v[:, 0:512]), [kv], ['bnst'])
                        S.op('dve', lambda e: e.bn_aggr(out=bnst[:, 8:10], in_=bnst[:, 0:6]), ['bnst'], ['bnst2'])
                        ts('dve', bnst[:, 11:12], bnst[:, 9:10], float(EPS), None, ALU.add, None, ['bnst2'], ['bnst3'])
                        act(bnst[:, 11:12], bnst[:, 11:12], AF.Ln, ['bnst3'], ['bnst3'])
                        act(bnst[:, 10:11], bnst[:, 11:12], AF.Exp, ['bnst3'], ['bnst3'], scale=-0.5)
                        ts('dve', vn[:, t, :], v[:, 0:512], bnst[:, 8:9], bnst[:, 10:11], ALU.subtract, ALU.mult, [kv, 'bnst2', 'bnst3'], [K_VN[t]])
                    wsT3 = wsT[:].rearrange("p (g q) -> p g q", g=4)
                    for t in range(4):
                        bank, bk = rot()

                        def fn(e, bank=bank, t=t):
                            for gg in range(4):
                                r = e.matmul(bank[:, gg * 128:(gg + 1) * 128], lhsT=vn[:, t, gg * 128:(gg + 1) * 128], rhs=wsT3[:, gg, :], start=True, stop=True)
                            return r
                        S.op('pe', fn, [K_VN[t], 'wsT'], [bk])
                        mb, kmb = ftile()
                        tt('dve', mb[:, 0:512], bank[:, :], bsb[:], ALU.add, [bk, 'bsb'], [kmb])
                        tt('pool', ybT[:, :, t * 128:(t + 1) * 128], mb[:, 0:512].rearrange("p (g q) -> p g q", g=4), uT[:, :, t * 128:(t + 1) * 128], ALU.mult,
                           [kmb] + K_UT, [('yb', t)])
                    YBK = [('yb', t) for t in range(4)]
                    stop('gmlp')
                    wb, wk = w_use(l, 'cq')
                    rope_multi([(lambda fb=fb: proj_fm(wb, wk, fb, 128, TG), 128, TG, QT[:, fb, :], [K_QT[fb]]) for fb in range(4)])
                    kts = ([4 * g - 1] if has_lo else []) + [4 * g + i for i in range(4)] + [4 * g + 4]
                    arot_i = [0]

                    def arot():
                        i = arot_i[0] % 4
                        arot_i[0] += 1
                        return psb[i], ('ps', i)
                    OD = [((psb[5], ('ps', 5)), (psb[6], ('ps', 6))), ((psb[7], ('ps', 7)), (psb[4], ('ps', 4)))]

                    def att_scores(qb, hh):
                        kvh = qb // 2
                        pr = slice(hh * 64, hh * 64 + 64)
                        inf = {}
                        for j, kt in enumerate(kts):
                            qlo, qhi = max(kt - 1, 4 * g), min(kt + 1, 4 * g + 3)
                            nq = qhi - qlo + 1
                            qc0 = (qlo - 4 * g) * 128
                            bank, bk = arot()
                            mlist = []
                            for qi in range(nq):
                                rel = (qlo + qi) - kt
                                if kt == NT:
                                    mlist.append((qi, NEGM['c']))
                                elif rel != 0:
                                    mlist.append((qi, NEGM[rel]))

                            def fn(e, bank=bank, pr=pr, kvh=kvh, kt=kt, qb=qb, qc0=qc0, nq=nq, mlist=mlist):
                                r = e.matmul(bank[:, 0:nq * 128], lhsT=KT[pr, kvh, ring(kt):ring(kt) + 128], rhs=QT[pr, qb, qc0:qc0 + nq * 128],
                                             start=True, stop=(len(mlist) == 0))
                                for ii, (qi, ng) in enumerate(mlist):
                                    r = e.matmul(bank[:, qi * 128:(qi + 1) * 128], lhsT=ident, rhs=ng, start=False, stop=(ii == len(mlist) - 1))
                                return r
                            S.op('pe', fn, ['KT', K_QT[qb], 'cm'], [bk])
                            act(PT[hh][j][:, 0:nq * 128], bank[:, 0:nq * 128], AF.Exp, [bk], K_PT[hh][j], scale=0.125)
                            inf[kt] = (j, qlo)
                        return inf

                    def att_pv(qb, hh, inf):
                        kvh = qb // 2
                        (obank, obk), (dbank, dbk) = OD[qb % 2]
                        pr = slice(hh * 64, hh * 64 + 64)
                        for qt in range(4 * g, 4 * g + 4):
                            qcs = slice((qt - 4 * g) * 128, (qt - 4 * g + 1) * 128)
                            use = [kt for kt in (qt - 1, qt, qt + 1) if kt in inf]
                            pairs_o, pairs_d, rk = [], [], []
                            for kt in use:
                                j, qlo = inf[kt]
                                p_ap = PT[hh][j][:, (qt - qlo) * 128:(qt - qlo + 1) * 128]
                                vc = kvh * 128 + hh * 64
                                pairs_o.append((V2[:, kt % 8, vc:vc + 64], p_ap))
                                pairs_d.append((ones_b[:, 0:64], p_ap))
                                rk += K_PT[hh][j]
                            mm(obank[pr, qcs], pairs_o, rk + ['V2'], [obk])
                            mm(dbank[pr, qcs], pairs_d, rk + ['cm'], [dbk])

                    def att_norm(qb):
                        (obank, obk), (dbank, dbk) = OD[qb % 2]
                        rd, krd = ftile()
                        ts('dve', rd[:, 0:512], dbank[:, :], esink2[:, qb:qb + 1], None, ALU.add, None, [dbk, 'esink2'], [krd])
                        S.op('dve', lambda e, rd=rd: e.reciprocal(out=rd[:, 0:512], in_=rd[:, 0:512]), [krd], [krd])
                        tt('dve', ycT[:, qb, :], obank[:, :], rd[:, 0:512], ALU.mult, [obk, krd], [('yc', qb, 0)])

                    units = [(qb, hh) for qb in range(4) for hh in range(2)]
                    infos = {units[0]: att_scores(*units[0])}
                    for ui, u in enumerate(units):
                        if ui + 1 < len(units):
                            infos[units[ui + 1]] = att_scores(*units[ui + 1])
                        att_pv(u[0], u[1], infos[u])
                        if u[1] == 1:
                            att_norm(u[0])
                    YCK = [('yc', qb, 0) for qb in range(4)]
                    YAK = [('ya', h) for h in range(4)]
                    stop('attn')
                    stop('gates')
                    ysrc = [(yaT, YAK), (ybT, YBK), (ycT, YCK)]
                    for j in range(4):
                        wb, wk = w_use(l, 'br%d' % j)
                        wv = wb[:, 0:3072].rearrange("p (b k c) -> p b k c", b=3, k=4)
                        for o2 in range(2):
                            ob = 2 * j + o2
                            tl_ = []
                            for br in range(3):
                                ysb, ykeys = ysrc[br]
                                bank, bk = rot()
                                mm(bank[:, :], [(wv[:, br, kc, o2 * 128:(o2 + 1) * 128], ysb[:, kc, :]) for kc in range(4)], [wk] + ykeys, [bk])
                                tf, ktf = ftile()
                                tt('dve', tf[:, 0:512], bank[:, :], big[:, br * 8 + ob, :], ALU.mult, [bk, ('big', br * 8 + ob)], [ktf])
                                tl_.append((tf, ktf))
                            tt('pool', tl_[0][0][:, 0:512], tl_[0][0][:, 0:512], tl_[1][0][:, 0:512], ALU.add, [tl_[0][1], tl_[1][1]], [tl_[0][1]])
                            tt('pool', big[:, 24 + ob, :], tl_[0][0][:, 0:512], tl_[2][0][:, 0:512], ALU.add, [tl_[0][1], tl_[2][1]], [('big', 24 + ob)])
                    stop('merge')
                    for j in range(2):
                        wb, wk = w_use(l, 'wo%d' % j)
                        wv = wb[:].rearrange("p (k c) -> p k c", k=8)
                        for o4 in range(4):
                            ob = j * 4 + o4
                            bank, bk = rot()
                            mm(bank[:, :], [(wv[:, kc, o4 * 128:(o4 + 1) * 128], big[:, 24 + kc, :]) for kc in range(KC)],
                               [wk] + [('big', 24 + kc) for kc in range(KC)], [bk])
                            tt('dve', xT[:, ob, :], xT[:, ob, :], bank[:, :], ALU.add, [bk, 'xT'], ['xT'])
                    stop('wout')
                    rmsnorm([(xT, ['xT'], TG, 128, xn_keys(128))], SC_LN2 + l * 8, l)
                    ri = 0
                    for j in range(8):
                        wb, wk = w_use(l, 'up%d' % j)
                        for fb in range(4):
                            bank, bk = proj_fm(wb, wk, fb, 128, TG)
                            r = rl[ri % 2]
                            rk_ = ('rl', ri % 2)
                            ri += 1
                            act(r[:], bank[:, :], AF.Relu, [bk], [rk_])
                            tt('pool', big[:, j * 4 + fb, :], r[:], r[:], ALU.mult, [rk_], [('big', j * 4 + fb)])
                    for ob in range(8):
                        wb, wk = w_use(l, 'dn%d' % ob)
                        wv = wb[:].rearrange("p (k c) -> p k c", k=32)
                        bank, bk = rot()
                        mm(bank[:, :], [(wv[:, kc, :], big[:, kc, :]) for kc in range(32)], [wk] + BIGK, [bk])
                        tt('dve', xT[:, ob, :], xT[:, ob, :], bank[:, :], ALU.add, [bk, 'xT'], ['xT'])
                    stop('ffn')
                    if last:
                        rmsnorm([(xT, ['xT'], TG, 128, None)], SC_FN, l, out_f32=True)
                        S.dma('sp', yout[:, :, g * TG:(g + 1) * TG].rearrange("k p t -> p k t"), xT[:], reads=['xT'], writes=[('y', g)], slot='sto')
                    else:
                        S.dma('sp', xs[l][:, :, g * TG:(g + 1) * TG].rearrange("k p t -> p k t"), xT[:], reads=['xT'], writes=[('xs', l, g)], slot='sto')
                if c1['on']:
                    while c1['done'] < len(cast1_steps):
                        cast1_step()
                    c1['on'] = False
        try:
            main_body()
        except _Stop:
            S.dma('sp', yout[:, :, 0:TG].rearrange("k p t -> p k t"), xT[:], reads=['xT'], writes=[('y', 0)], slot='sto')
        S.final_wait('sp', ['sto'])
        print("ops:", {e: len(S.lists[e]) for e in S.ENGS})
        with nc.Block() as block:
            S.emit(block)
    return nc


def _consts():
    r = np.arange(128)
    ident = np.eye(128, dtype=np.float32)
    pm = np.zeros((128, 128), np.float32)
    for c in range(128):
        d = c % 64
        if d < 8:
            pm[c + 8, c] = 1.0
        elif d < 16:
            pm[c - 8, c] = 1.0
    m = r[:, None]
    a = r[None, :]
    m3 = np.concatenate([(m <= a), np.ones((128, 128), bool), (m >= a)], axis=1).astype(np.float32)
    mc = ((m + a) >= 127).astype(np.float32)
    s = (r % 64)[:, None]
    t = np.arange(64)[None, :]
    mh1 = (s <= t).astype(np.float32)
    mh2 = (s >= t).astype(np.float32)
    ones = np.ones((128, 128), np.float32)
    NEG = np.float32(-30000.0)
    nb_ = np.where(m <= a, 0.0, NEG).astype(np.float32)
    na_ = np.where(m >= a, 0.0, NEG).astype(np.float32)
    nc_ = np.where((m + a) >= 127, 0.0, NEG).astype(np.float32)
    cm = np.concatenate([ident, pm, m3, mc, mh1, mh2, ones, nb_, na_, nc_], axis=1)
    assert cm.shape[1] == C_END
    return cm.astype(ml_dtypes.bfloat16)


def _rope_tables(pos):
    inv = np.float32(500000.0) ** (-(np.arange(8, dtype=np.float32) * np.float32(2.0 / 16)))
    ang = pos.astype(np.float32)[:, None] * inv[None, :]
    cos = np.cos(ang).astype(np.float32)
    sin = np.sin(ang).astype(np.float32)
    T = pos.shape[0]
    C = np.ones((128, T), np.float32)
    Sn = np.zeros((128, T), np.float32)
    for rr in range(128):
        d = rr % 64
        if d < 8:
            C[rr] = cos[:, d]
            Sn[rr] = -sin[:, d]
        elif d < 16:
            C[rr] = cos[:, d - 8]
            Sn[rr] = sin[:, d - 8]
    return np.stack([C, Sn], 0)


def _wblocks(w_in, w_br, w_out, w_up, w_down, half):
    NL = w_in.shape[0]
    out = np.zeros((NL, NB, 128, 4096), np.float32)

    def fbk(W, cols):
        return W[:, cols].reshape(8, 128, 128).transpose(1, 0, 2)

    def blk(W, colsets):
        return np.stack([fbk(W, c) for c in colsets], 1).reshape(128, 4096)
    ar = np.arange
    for l in range(NL):
        W = w_in[l]
        f1o, f2o = (1024, 1536) if half == 0 else (1536, 1024)
        out[l, BI['ai']] = blk(W, [512 + h * 128 + ar(128) for h in range(4)])
        for h in range(4):
            out[l, BI['h%d' % h]] = blk(W, [f1o + h * 128 + ar(128), f2o + h * 128 + ar(128), h * 128 + ar(128), 2048 + h * 128 + ar(128)])
        out[l, BI['bu']] = blk(W, [2560 + f * 128 + ar(128) for f in range(4)])
        out[l, BI['bv']] = blk(W, [3072 + f * 128 + ar(128) for f in range(4)])
        out[l, BI['cq']] = blk(W, [3584 + f * 128 + ar(128) for f in range(4)])
        k0, k1 = 4096 + ar(64), 4160 + ar(64)
        v0, v1 = 4224 + ar(64), 4288 + ar(64)
        cc = np.concatenate
        out[l, BI['ckv']] = blk(W, [cc([k0, k0]), cc([k1, k1]), cc([v0, v0]), cc([v1, v1])])
        for j in range(6):
            out[l, BI['gz%d' % j]] = blk(W, [4352 + (j * 4 + f) * 128 + ar(128) for f in range(4)])
        for j in range(4):
            a = w_br[l][:, :, j * 256:(j + 1) * 256].reshape(3, 4, 128, 256).transpose(2, 0, 1, 3)
            out[l, BI['br%d' % j], :, 0:3072] = a.reshape(128, 3072)
        for j in range(2):
            a = w_out[l][:, j * 512:(j + 1) * 512].reshape(8, 128, 512).transpose(1, 0, 2)
            out[l, BI['wo%d' % j]] = a.reshape(128, 4096)
        for j in range(8):
            out[l, BI['up%d' % j]] = blk(w_up[l], [j * 512 + f * 128 + ar(128) for f in range(4)])
        for ob in range(8):
            a = w_down[l][:, ob * 128:(ob + 1) * 128].reshape(32, 128, 128).transpose(1, 0, 2)
            out[l, BI['dn%d' % ob]] = a.reshape(128, 4096)
    return out


_PROG_CACHE = {}


def kernel(x, w_in, ln1, lb_logits, a_norm, w_s, b_s, sink, w_br, w_out, ln2, w_up, w_down, final_norm):
    x = np.asarray(x, np.float32)
    f = lambda a: np.asarray(a, np.float32)
    w_in, ln1, lb_logits, a_norm, w_s, b_s, sink = map(f, (w_in, ln1, lb_logits, a_norm, w_s, b_s, sink))
    w_br, w_out, ln2, w_up, w_down, final_norm = map(f, (w_br, w_out, ln2, w_up, w_down, final_norm))
    B, SEQ, _ = x.shape
    NL = w_in.shape[0]
    TOK = SEQ // 2
    assert B * 2 == NCORES and TOK % TG == 0
    key = (TOK, NL)
    if key not in _PROG_CACHE:
        _PROG_CACHE[key] = build_program(TOK, NL, SEQ)
    nc = _PROG_CACHE[key]
    cm = _consts()
    in_maps = []
    variants = {}
    for half in range(2):
        wsrc = _wblocks(w_in, w_br, w_out, w_up, w_down, half)
        pos = np.arange(TOK) if half == 0 else (SEQ - 1 - np.arange(TOK))
        rope = _rope_tables(pos)
        pp = np.zeros((128, P_END), np.float32)
        for l in range(NL):
            pp[:, P_LN1 + l * 8:P_LN1 + l * 8 + 8] = ln1[l].reshape(8, 128).T
            pp[:, P_LN2 + l * 8:P_LN2 + l * 8 + 8] = ln2[l].reshape(8, 128).T
            for d in range(2):
                dirn = d if half == 0 else 1 - d
                pp[:, P_LBL + l * 8 + d * 4:P_LBL + l * 8 + d * 4 + 4] = lb_logits[l, dirn].reshape(4, 128).T
            pp[:, P_AN + l] = a_norm[l]
        pp[:, P_FN:P_FN + 8] = final_norm.reshape(8, 128).T
        pp[:, P_SEL] = 0.0 if half == 0 else 1.0
        pp[:, P_SEL + 1] = 1.0 if half == 0 else 0.0
        if half == 0:
            wst = np.ascontiguousarray(w_s.transpose(0, 3, 1, 2)).reshape(NL, 128, 512)
            bsv = b_s.reshape(NL, 512)
        else:
            wst = np.ascontiguousarray(w_s[:, :, ::-1, ::-1].transpose(0, 3, 1, 2)).reshape(NL, 128, 512)
            bsv = np.ascontiguousarray(b_s[:, :, ::-1]).reshape(NL, 512)
        variants[half] = dict(wsrc=wsrc, pp=pp, cm=cm, bsv=np.ascontiguousarray(bsv), wst=wst,
                              sinkv=np.ascontiguousarray(sink), rope=rope)
    for c in range(NCORES):
        b, half = c // 2, c % 2
        xs_ = x[b, :TOK] if half == 0 else x[b, TOK:][::-1]
        xin = np.ascontiguousarray(xs_.T).reshape(KC, 128, TOK)
        m = dict(variants[half])
        m['xin'] = xin
        in_maps.append(m)
    res = run_bass_kernel_spmd(nc, in_maps, core_ids=list(range(NCORES)))
    y = np.empty((B, SEQ, D), np.float32)
    for c in range(NCORES):
        b, half = c // 2, c % 2
        yt = np.asarray(res.results[c]["yout"]).reshape(D, TOK).T
        if half == 0:
            y[b, :TOK] = yt
        else:
            y[b, TOK:] = yt[::-1]
    return y
```

```python
import contextlib
import numpy as np
import ml_dtypes
import concourse.bass as bass
import concourse.mybir as mybir
from concourse.bass_utils import run_bass_kernel_spmd

F32 = mybir.dt.float32
BF16 = mybir.dt.bfloat16
AF = mybir.ActivationFunctionType
ALU = mybir.AluOpType

D = 1024
KC = 8
TG = 512
EPS = 1e-6
NCORES = 8
NWB = 3
BLK = ['ai', 'h0', 'h1', 'h2', 'h3', 'bu', 'bv', 'cq', 'ckv'] + ['gz%d' % i for i in range(6)] + \
      ['br%d' % i for i in range(4)] + ['wo0', 'wo1'] + ['up%d' % i for i in range(8)] + ['dn%d' % i for i in range(8)]
BI = {n: i for i, n in enumerate(BLK)}
NB = len(BLK)
C_ID, C_PM, C_M3, C_MC, C_MH1, C_MH2, C_ONE, C_NB, C_NA, C_NC, C_END = 0, 128, 256, 640, 768, 832, 896, 1024, 1152, 1280, 1408
P_LN1, P_LN2, P_FN, P_LBL, P_AN, P_SEL, P_END = 0, 16, 32, 40, 56, 58, 60


class Sched:
    ENGS = ('pe', 'act', 'dve', 'pool', 'sp')

    def __init__(self, nc, stack):
        self.nc = nc
        self.stack = stack
        self.lists = {e: [] for e in self.ENGS}
        self.cnt = {}
        self.sem = {}
        self.waited = {e: {} for e in self.ENGS}
        self.lastw = {}
        self.readers = {}
        for e in self.ENGS[:4]:
            self._mksem(e)

    def _mksem(self, name):
        if name not in self.sem:
            self.sem[name] = self.stack.enter_context(self.nc.semaphore("s_" + name))
            self.cnt[name] = 0

    def _deps(self, eng, reads, writes):
        deps = {}
        for b in reads:
            w = self.lastw.get(b)
            if w:
                deps[w[0]] = max(deps.get(w[0], 0), w[1])
            if isinstance(b, tuple) and b[0] == 'ps':
                for s, v in self.readers.get(b, {}).items():
                    if s != eng:
                        deps[s] = max(deps.get(s, 0), v)
        for b in writes:
            w = self.lastw.get(b)
            if w:
                deps[w[0]] = max(deps.get(w[0], 0), w[1])
            for s, v in self.readers.get(b, {}).items():
                deps[s] = max(deps.get(s, 0), v)
        waits = []
        for s, v in deps.items():
            if eng == 'pe' and s == 'pe':
                continue
            if self.waited[eng].get(s, 0) < v:
                waits.append((s, v))
                self.waited[eng][s] = v
        return waits

    def _record(self, semname, val, reads, writes):
        for b in reads:
            r = self.readers.setdefault(b, {})
            r[semname] = max(r.get(semname, 0), val)
        for b in writes:
            self.lastw[b] = (semname, val)
            self.readers[b] = {}

    def op(self, eng, fn, reads=(), writes=()):
        waits = self._deps(eng, reads, writes)
        self.cnt[eng] += 1
        self.lists[eng].append((waits, fn, eng, 1))
        self._record(eng, self.cnt[eng], reads, writes)

    def dma(self, q, out, in_, reads=(), writes=(), slot=None):
        self._mksem(slot)
        waits = self._deps(q, reads, writes)
        self.cnt[slot] += 16
        self.lists[q].append((waits, lambda e: e.dma_start(out=out, in_=in_), slot, 16))
        self._record(slot, self.cnt[slot], reads, writes)

    def coll(self, fn, reads=(), writes=(), slot=None):
        self._mksem(slot)
        assert self.cnt[slot] == 0
        waits = self._deps('pool', reads, writes)
        self.cnt[slot] = 1
        self.lists['pool'].append((waits, fn, slot, None))
        self._record(slot, 1, reads, writes)

    def final_wait(self, q, slots):
        waits = [(s, self.cnt[s]) for s in slots if self.cnt.get(s, 0) > 0]
        self.lists[q].append((waits, None, None, 0))

    def emit(self, block):
        sem = self.sem

        def run(lst):
            def body(e):
                for waits, fn, semname, inc in lst:
                    for s, v in waits:
                        e.wait_ge(sem[s], v)
                    if fn is not None:
                        inst = fn(e)
                        if inc is None:
                            inst.then_inc(sem[semname])
                        else:
                            inst.then_inc(sem[semname], inc)
            return body
        block.tensor(run(self.lists['pe']))
        block.scalar(run(self.lists['act']))
        block.vector(run(self.lists['dve']))
        block.gpsimd(run(self.lists['pool']))
        block.sync(run(self.lists['sp']))


class _Stop(Exception):
    pass


DEBUG_STOP = [None]


def build_program(TOK, NL, S_FULL):
    NG = TOK // TG
    NT = TOK // 128
    nc = bass.Bass("TRN2", target_bir_lowering=False)
    xin = nc.dram_tensor("xin", [KC, 128, TOK], F32, kind="ExternalInput").ap()
    wsrc = nc.dram_tensor("wsrc", [NL, NB, 128, 4096], F32, kind="ExternalInput").ap()
    pp_d = nc.dram_tensor("pp", [128, P_END], F32, kind="ExternalInput").ap()
    cm_d = nc.dram_tensor("cm", [128, C_END], BF16, kind="ExternalInput").ap()
    bs_d = nc.dram_tensor("bsv", [NL, 512], F32, kind="ExternalInput").ap()
    wst_d = nc.dram_tensor("wst", [NL, 128, 512], F32, kind="ExternalInput").ap()
    sink_d = nc.dram_tensor("sinkv", [NL, 8], F32, kind="ExternalInput").ap()
    rope_d = nc.dram_tensor("rope", [2, 128, TOK], F32, kind="ExternalInput").ap()
    yout = nc.dram_tensor("yout", [KC, 128, TOK], F32, kind="ExternalOutput").ap()
    wbf = nc.dram_tensor("wbf", [NL, NB, 128, 4096], BF16).ap()
    xs = [nc.dram_tensor("xs%d" % l, [KC, 128, TOK], F32).ap() for l in range(max(NL - 1, 1))]
    s1s = nc.dram_tensor("s1s", [NL, NG, 128, 512], F32).ap()
    exS_i = [nc.dram_tensor("exSi%d" % l, [128, 512], F32) for l in range(NL)]
    exS_o = [nc.dram_tensor("exSo%d" % l, [256, 512], F32) for l in range(NL)]
    exK_i = [nc.dram_tensor("exKi%d" % l, [128, 512], BF16) for l in range(NL)]
    exK_o = [nc.dram_tensor("exKo%d" % l, [256, 512], BF16) for l in range(NL)]

    st = contextlib.ExitStack()
    with st:
        S = Sched(nc, st)

        def sb(name, shape, dt):
            return st.enter_context(nc.sbuf_tensor(name, shape, dt))

        cm = sb("cm_s", [128, C_END], BF16)
        ppt = sb("pp_s", [128, P_END], F32)
        sc = sb("sc_s", [128, 96], F32)
        bsb = sb("bsb", [128, 512], F32)
        wsT = sb("wsT", [128, 512], BF16)
        esink = sb("esink", [128, 8], F32)
        esink2 = sb("esink2", [128, 4], F32)
        dSb = [sb("dSb%d" % i, [128, 8, 128], F32) for i in range(2)]
        xT = sb("xT", [128, KC, TG], F32)
        xh = sb("xh", [128, KC, 128], F32)
        sq = [sb("sq%d" % i, [128, 640], BF16) for i in range(2)]
        rstd = sb("rstd", [128, 640], F32)
        xnT = sb("xnT", [128, KC, 640], BF16)
        cosT = sb("cosT", [128, 640], F32)
        sinT = sb("sinT", [128, 640], F32)
        KT = sb("KT", [128, 2, 1024], BF16)
        V2 = sb("V2", [128, 8, 256], BF16)
        NF = 9
        Fp = [sb("F%d" % i, [128, 640], F32) for i in range(NF)]
        Lx = [sb("Lx%d" % i, [128, 576], F32) for i in range(2)]
        esub = sb("esub", [128, 2, 24], F32)
        ech = [sb("ech%d" % i, [128, 2, 24], F32) for i in range(2)]
        TRW = 9216
        TR = sb("TR", [128, TRW], BF16)
        S1 = sb("S1", [128, 4, 128], F32)
        S2 = sb("S2", [128, 4, 128], F32)
        Sb = [sb("Sb%d" % i, [128, 128], BF16) for i in range(16)]
        At = [sb("At%d" % i, [128, 64], BF16) for i in range(16)]
        Sall = [sb("Sall%d" % i, [128, 7, 128], F32) for i in range(2)]
        osq = sb("osq", [128, 512], BF16)
        yaT = sb("yaT", [128, 4, TG], BF16)
        ybT = sb("ybT", [128, 4, TG], BF16)
        ycT = sb("ycT", [128, 4, TG], BF16)
        exb = sb("exb", [128, 2, 512], BF16)
        exs = sb("exs", [128, 2, 512], F32)
        cst = [sb("cst%d" % i, [128, 512], BF16) for i in range(2)]
        big = sb("big", [128, 32, TG], BF16)
        rl = [sb("rl%d" % i, [128, 512], BF16) for i in range(2)]
        bnst = sb("bnst", [128, 48], F32)
        wbuf = [sb("wb%d" % i, [128, 4096], BF16) for i in range(NWB)]
        psb = [st.enter_context(nc.psum_tensor("ps%d" % i, [128, 512], F32)) for i in range(8)]
        print("sbuf bytes remaining:", nc.sbuf_bytes_remaining)

        def trk(i):
            return ('TR', i)
        Vt = TR[:, 0:2048].rearrange("p (t c) -> p t c", t=4)
        def hv(s, j):
            o = 2048 + (s * 7 + j) * 512
            return TR[:, o:o + 512], trk(4 + s * 7 + j)
        uT = TR[:, 0:2048].rearrange("p (f t) -> p f t", f=4)
        vn = TR[:, 2048:4096].rearrange("p (t c) -> p t c", t=4)
        QT = TR[:, 0:2048].rearrange("p (f t) -> p f t", f=4)
        qraw = TR[:, 2048:2688]
        PTt = [TR[:, 3072 + i * 512:3072 + i * 512 + 384] for i in range(2)]
        PT = [[TR[:, 4096 + (hh_ * 6 + i) * 384:4096 + (hh_ * 6 + i + 1) * 384] for i in range(6)] for hh_ in range(2)]
        K_V = [trk(i) for i in range(4)]
        K_UT = [trk(i) for i in range(4)]
        K_VN = [trk(4 + i) for i in range(4)]
        K_QT = [trk(i) for i in range(4)]
        K_QRAW = [trk(4), trk(5)]
        qraw2 = TR[:, 3072:3712]
        K_QRAW2 = [trk(6), trk(7)]
        K_PTT = [trk(6), trk(7)]
        K_PT = [[[trk(8 + ((hh_ * 6 + i) * 384) // 512), trk(8 + ((hh_ * 6 + i + 1) * 384 - 1) // 512)] for i in range(6)] for hh_ in range(2)]

        ident = cm[:, C_ID:C_ID + 128]
        Pm = cm[:, C_PM:C_PM + 128]
        M3 = cm[:, C_M3:C_M3 + 384]
        MC = cm[:, C_MC:C_MC + 128]
        MH = [cm[:, C_MH1:C_MH1 + 64], cm[:, C_MH2:C_MH2 + 64]]
        ones_b = cm[:, C_ONE:C_ONE + 128]
        NEGM = {-1: cm[:, C_NB:C_NB + 128], 1: cm[:, C_NA:C_NA + 128], 'c': cm[:, C_NC:C_NC + 128]}
        SC_LN1, SC_LN2, SC_FN, SC_AN = 0, 16, 32, 40
        SC_HB, SC_HA, SC_NHB = 44, 60, 76
        SC_ONE, SC_ZERO = 92, 93

        rot_i = [0]

        def rot():
            i = rot_i[0] % 5
            rot_i[0] += 1
            return psb[i], ('ps', i)
        ACC = [(psb[5], ('ps', 5)), (psb[6], ('ps', 6))]
        TPB = psb[7][:].bitcast(BF16)
        K_TP = ('ps', 7)
        f_i = [0]

        def ftile():
            i = f_i[0] % NF
            f_i[0] += 1
            return Fp[i], ('F', i)

        def mm(out, pairs, reads, writes):
            def fn(e, out=out, pairs=pairs):
                n = len(pairs)
                for i, (l, r) in enumerate(pairs):
                    inst = e.matmul(out, lhsT=l, rhs=r, start=(i == 0), stop=(i == n - 1))
                return inst
            S.op('pe', fn, reads, writes)

        def act(out, in_, func, reads, writes, **kw):
            S.op('act', lambda e: e.activation(out=out, in_=in_, func=func, **kw), reads, writes)

        def tt(eng, out, in0, in1, op, reads, writes):
            S.op(eng, lambda e: e.tensor_tensor(out=out, in0=in0, in1=in1, op=op), reads, writes)

        def ts(eng, out, in0, s1, s2, op0, op1, reads, writes):
            if s2 is None:
                S.op(eng, lambda e: e.tensor_scalar(out=out, in0=in0, scalar1=s1, scalar2=None, op0=op0), reads, writes)
            else:
                S.op(eng, lambda e: e.tensor_scalar(out=out, in0=in0, scalar1=s1, scalar2=s2, op0=op0, op1=op1), reads, writes)

        def stt(eng, out, in0, scalar, in1, op0, op1, reads, writes):
            S.op(eng, lambda e: e.scalar_tensor_tensor(out=out, in0=in0, scalar=scalar, in1=in1, op0=op0, op1=op1), reads, writes)

        S.dma('sp', cm[:], cm_d[:, :], writes=['cm'], slot='c0')
        S.dma('sp', ppt[:], pp_d[:, :], writes=['pp'], slot='c1')
        S.op('pool', lambda e: e.memset(sc[:, SC_ONE:SC_ONE + 1], 1.0), writes=['sc1'])
        S.op('pool', lambda e: e.memset(sc[:, SC_ZERO:SC_ZERO + 1], 0.0), writes=['sc1'])
        for i in range(2):
            S.op('pool', lambda e, i=i: e.memset(Lx[i][:], 0.0), writes=[('Lx', i)])
        sD = float(np.sqrt(D))
        ts('dve', sc[:, SC_LN1:SC_LN1 + 16], ppt[:, P_LN1:P_LN1 + 16], sD, None, ALU.mult, None, ['pp'], ['sc'])
        ts('dve', sc[:, SC_LN2:SC_LN2 + 16], ppt[:, P_LN2:P_LN2 + 16], sD, None, ALU.mult, None, ['pp'], ['sc'])
        ts('dve', sc[:, SC_FN:SC_FN + 8], ppt[:, P_FN:P_FN + 8], sD, None, ALU.mult, None, ['pp'], ['sc'])
        ts('dve', sc[:, SC_AN:SC_AN + 2], ppt[:, P_AN:P_AN + 2], float(np.sqrt(128.0) * 0.5), None, ALU.mult, None, ['pp'], ['sc'])
        lbt = sc[:, SC_HB:SC_HB + 16]
        S.op('pool', lambda e: e.memset(sc[:, SC_HB:SC_HB + 16], 0.0), writes=['sc2'])
        if NL > 1:
            tt('dve', sc[:, SC_HA + 8:SC_HA + 16], ppt[:, P_LBL + 8:P_LBL + 16], ppt[:, P_LBL:P_LBL + 8], ALU.subtract, ['pp'], ['sc3'])
            act(sc[:, SC_HB + 8:SC_HB + 16], sc[:, SC_HA + 8:SC_HA + 16], AF.Sigmoid, ['sc3', 'sc2'], ['sc2'])
        ts('dve', sc[:, SC_HA:SC_HA + 16], sc[:, SC_HB:SC_HB + 16], -0.5, 0.5, ALU.mult, ALU.add, ['sc2', 'sc3'], ['sc3'])
        ts('dve', sc[:, SC_NHB:SC_NHB + 16], sc[:, SC_HA:SC_HA + 16], -1.0, None, ALU.mult, None, ['sc3'], ['sc4'])
        ts('dve', sc[:, SC_HB:SC_HB + 16], sc[:, SC_HB:SC_HB + 16], 0.5, 0.5, ALU.mult, ALU.add, ['sc2', 'sc3'], ['sc2'])
        SCK = ['sc', 'sc1', 'sc2', 'sc3', 'sc4']

        stage = big[:].rearrange("p a b -> p (a b)").bitcast(F32)
        cast_engs = ['act', 'dve', 'dve']
        ci = 0
        def main_body():
            nonlocal ci
            for l in (range(1) if NL == 2 else range(NL)):
                for bi in range(NB):
                    s2 = ci % 2
                    w = ci % NWB
                    S.dma('sp', stage[:, s2 * 4096:(s2 + 1) * 4096], wsrc[l, bi], writes=[('stg', s2)], slot='pg%d' % s2)
                    eng = cast_engs[ci % 3]
                    if eng == 'act':
                        act(wbuf[w][:], stage[:, s2 * 4096:(s2 + 1) * 4096], AF.Copy, [('stg', s2)], [('wb', w)])
                    else:
                        S.op(eng, lambda e, w=w, s2=s2: e.tensor_copy(out=wbuf[w][:], in_=stage[:, s2 * 4096:(s2 + 1) * 4096]),
                             [('stg', s2)], [('wb', w)])
                    S.dma('sp', wbf[l, bi], wbuf[w][:], reads=[('wb', w)], writes=[('wbf', l, bi)], slot='pw%d' % w)
                    ci += 1
            BIGK = [('big', i) for i in range(32)]
            S.op('pool', lambda e: e.memset(sc[:, SC_ZERO:SC_ZERO + 1], 0.0), writes=BIGK + ['sc1', ('stg', 0), ('stg', 1)])

            cast1_steps = [(1, bi, e8) for bi in range(NB) for e8 in range(8)] if NL == 2 else []
            c1 = {'in': 0, 'done': 0, 'on': False}

            def cast1_in():
                i = c1['in']
                if i >= len(cast1_steps):
                    return
                l1, bi, e8 = cast1_steps[i]
                s_ = i % 2
                S.dma('pool', exs[:, s_, :], wsrc[l1, bi, :, e8 * 512:(e8 + 1) * 512], writes=[('exs', s_)], slot='cg%d' % s_)
                c1['in'] += 1

            def cast1_step():
                i = c1['done']
                if i >= len(cast1_steps):
                    return
                if c1['in'] == i:
                    cast1_in()
                cast1_in()
                l1, bi, e8 = cast1_steps[i]
                s_ = i % 2
                S.op('pool', lambda e, s_=s_: e.tensor_copy(out=cst[s_][:], in_=exs[:, s_, :]), [('exs', s_)], [('cst', s_)])
                S.dma('pool', wbf[l1, bi, :, e8 * 512:(e8 + 1) * 512], cst[s_][:], reads=[('cst', s_)], writes=[('wbfp', l1, bi, e8)], slot='co%d' % s_)
                c1['done'] += 1

            def wbf_keys(l, bi, part):
                if NL == 2 and l == 1:
                    return [('wbfp', l, bi, e8) for e8 in range(2 if part else 8)]
                return [('wbf', l, bi)]

            seq = []
            for l in range(NL):
                for g in range(NG):
                    seq += [(l, 'ai', False)] + [(l, 'h%d' % h, True) for h in range(4)]
                for g in range(NG - 1, -1, -1):
                    seq += [(l, n, False) for n in ['ckv', 'ai', 'h0', 'h1', 'gz0', 'gz1', 'h2', 'gz2', 'gz3', 'h3', 'gz4', 'gz5', 'bu', 'bv', 'cq'] +
                            ['br%d' % i for i in range(4)] + ['wo0', 'wo1'] +
                            ['up%d' % i for i in range(8)] + ['dn%d' % i for i in range(8)]]
            ws = {'issued': 0, 'used': 0}

            def w_issue():
                k = ws['issued']
                if k >= len(seq):
                    return
                l, n, part = seq[k]
                slot = k % NWB
                bi = BI[n]
                if part:
                    S.dma('sp', wbuf[slot][:, 0:1024], wbf[l, bi, :, 0:1024], reads=wbf_keys(l, bi, True), writes=[('wb', slot)], slot='w%d' % slot)
                else:
                    S.dma('sp', wbuf[slot][:], wbf[l, bi], reads=wbf_keys(l, bi, False), writes=[('wb', slot)], slot='w%d' % slot)
                ws['issued'] += 1

            def w_use(l, n):
                k = ws['used']
                assert seq[k][0] == l and seq[k][1] == n, (seq[k], l, n)
                while ws['issued'] < min(k + NWB, len(seq)):
                    w_issue()
                ws['used'] += 1
                slot = k % NWB
                if c1['on']:
                    cast1_step()
                return wbuf[slot], ('wb', slot)

            def wv_f(wb):
                return wb[:].rearrange("p (f k c) -> p f k c", f=4, k=8)

            def rmsnorm(segs, lncol, l, out_f32=False):
                for (x3, xkeys, n, c0, okeys) in segs:
                    bank, bk = rot()
                    for kc in range(KC):
                        s = sq[kc % 2]
                        act(s[:, 0:n], x3[:, kc, :], AF.Square, xkeys, [('sq', kc % 2)])
                        S.op('pe', lambda e, s=s, n=n, kc=kc, bank=bank: e.matmul(bank[:, 0:n], lhsT=ones_b, rhs=s[:, 0:n], start=(kc == 0), stop=(kc == KC - 1)),
                             [('sq', kc % 2), 'cm'], [bk])
                    tmpf, ktmp = ftile()
                    ts('dve', tmpf[:, 0:n], bank[:, 0:n], float(D * EPS), None, ALU.add, None, [bk], [ktmp])
                    act(tmpf[:, 0:n], tmpf[:, 0:n], AF.Ln, [ktmp], [ktmp])
                    act(rstd[:, c0:c0 + n], tmpf[:, 0:n], AF.Exp, [ktmp], [('rstd', c0)], scale=-0.5)
                    for kc in range(KC):
                        eng = 'dve'
                        if out_f32:
                            stt(eng, x3[:, kc, :], x3[:, kc, :], sc[:, lncol + kc:lncol + kc + 1], rstd[:, c0:c0 + n], ALU.mult, ALU.mult,
                                [('rstd', c0)] + SCK + xkeys, xkeys)
                        else:
                            stt(eng, xnT[:, kc, c0:c0 + n], x3[:, kc, :], sc[:, lncol + kc:lncol + kc + 1], rstd[:, c0:c0 + n], ALU.mult, ALU.mult,
                                [('rstd', c0)] + SCK + xkeys, okeys)

            def xn_keys(c0):
                return [('xn', c0)]

            def proj_fm(wb, wk, fb, c0, n):
                bank, bk = rot()
                wv = wv_f(wb)
                mm(bank[:, 0:n], [(wv[:, fb, kc, :], xnT[:, kc, c0:c0 + n]) for kc in range(KC)],
                   [wk, ('xn', 0), ('xn', 128)], [bk])
                return bank, bk

            def proj_tm(wb, wk, fb0, nfb, c0):
                bank, bk = rot()
                wv = wv_f(wb)
                mm(bank[:, 0:nfb * 128].rearrange("p (f c) -> p f c", f=nfb),
                   [(xnT[:, kc, c0:c0 + 128], wv[:, fb0:fb0 + nfb, kc, :]) for kc in range(KC)],
                   [wk, ('xn', 0), ('xn', 128)], [bk])
                return bank, bk

            def rope_multi(calls):
                qr = [(qraw, K_QRAW), (qraw2, K_QRAW2)]
                pend = None
                st_ = []
                for i, (pf, c0, n, out, okeys) in enumerate(calls):
                    bank, bk = pf()
                    q_, qk_ = qr[i % 2]
                    act(q_[:, 0:n], bank[:, 0:n], AF.Copy, [bk], qk_)
                    cur = (bank, bk, q_, qk_, c0, n, out, okeys)
                    if pend is not None:
                        rope_finish(*pend)
                    pend = cur
                if pend is not None:
                    rope_finish(*pend)

            def rope_finish(bank, bk, q_, qk_, c0, n, out, okeys):
                b2, bk2 = rot()
                mm(b2[:, 0:n], [(Pm, q_[:, 0:n])], qk_ + ['cm'], [bk2])
                t1, k1 = ftile()
                tt('dve', t1[:, 0:n], bank[:, 0:n], cosT[:, c0:c0 + n], ALU.mult, [bk, 'cos'] + qk_, [k1])
                t2, k2 = ftile()
                tt('dve', t2[:, 0:n], b2[:, 0:n], sinT[:, c0:c0 + n], ALU.mult, [bk2, 'sin'], [k2])
                tt('pool', out, t1[:, 0:n], t2[:, 0:n], ALU.add, [k1, k2], okeys)

            def gcols(g):
                return slice(g * TG, (g + 1) * TG)

            def load_group(l, g, halo):
                src = xin if l == 0 else xs[l - 1]
                srck = ('xs', l - 1, g)
                S.dma('sp', xT[:], src[:, :, g * TG:(g + 1) * TG].rearrange("k p t -> p k t"), reads=[srck], writes=['xT'], slot='lx')
                if halo and g > 0:
                    S.dma('sp', xh[:], src[:, :, g * TG - 128:g * TG].rearrange("k p t -> p k t"), reads=[('xs', l - 1, g - 1)], writes=['xh'], slot='lh')

            def ring(t):
                return (t % 8) * 128

            def hgrn_prep2(l, h, dirs, zf, zq_bank, zq_k, hs, need_q):
                T = {}
                for d in dirs:
                    T[d] = [ftile(), ftile(), ftile()]
                sca = {}
                for d in dirs:
                    col = l * 8 + d * 4 + h
                    sca[d] = (sc[:, SC_HB + col:SC_HB + col + 1], sc[:, SC_HA + col:SC_HA + col + 1], sc[:, SC_NHB + col:SC_NHB + col + 1])
                for d in dirs:
                    (t0, k0) = T[d][0]
                    act(t0[:, 0:512], zf[d][0][:, :], AF.Tanh, [zf[d][1]], [k0], scale=0.5)
                for d in dirs:
                    (t0, k0), (t1, k1) = T[d][0], T[d][1]
                    hb, ha, nha = sca[d]
                    ts('dve', t1[:, 0:512], t0[:, 0:512], nha, ha, ALU.mult, ALU.add, [k0] + SCK, [k1])
                for d in dirs:
                    (t0, k0) = T[d][0]
                    hb, ha, nha = sca[d]
                    act(t0[:, 0:512], t0[:, 0:512], AF.Ln, [k0] + SCK, [k0], scale=ha, bias=hb)
                views = {}
                for d in dirs:
                    (t0, k0) = T[d][0]
                    LX = Lx[d]
                    S.op('dve', lambda e, LX=LX, t0=t0: e.tensor_tensor_scan(out=LX[:, 1:513], data0=sc[:, SC_ONE:SC_ONE + 1].to_broadcast([128, 512]),
                                                                             data1=t0[:, 0:512], initial=0.0, op0=ALU.mult, op1=ALU.add),
                         [k0] + SCK, [('Lx', d)])
                    views[d] = (LX[:, 0:512].rearrange("p (c t) -> p c t", t=64), LX[:, 1:513].rearrange("p (c t) -> p c t", t=64),
                                LX[:, 64:576].rearrange("p (c t) -> p c t", t=64))
                for d in dirs:
                    (t2, k2) = T[d][2]
                    L0, L1, L64 = views[d]
                    D3 = t2[:, 0:512].rearrange("p (c t) -> p c t", t=64)
                    tt('dve', D3, (L1 if d == 0 else L0), L0[:, :, 32:33].to_broadcast([128, 8, 64]), ALU.subtract, [('Lx', d)], [k2])
                for d in dirs:
                    L0, L1, L64 = views[d]
                    es = esub[:, d, :]
                    tt('dve', es[:, 0:8], L0[:, :, 32], L0[:, :, 0], ALU.subtract, [('Lx', d)], [('esub', d)])
                    tt('dve', es[:, 8:16], L64[:, :, 0], L0[:, :, 0], ALU.subtract, [('Lx', d)], [('esub', d)])
                    tt('dve', es[:, 16:24], L64[:, :, 0], L0[:, :, 32], ALU.subtract, [('Lx', d)], [('esub', d)])
                for d in dirs:
                    (t0, k0), (t2, k2) = T[d][0], T[d][2]
                    act(t0[:, 0:512], t2[:, 0:512], AF.Exp, [k2, ('Lx', d)], [k0])
                    act(t2[:, 0:512], t2[:, 0:512], AF.Exp, [k2], [k2], scale=-1.0)
                    act(ech[hs][:, d, :], esub[:, d, :], AF.Exp, [('esub', d)], [('ech', hs, d)])
                for d in dirs:
                    (t0, k0), (t1, k1), (t2, k2) = T[d]
                    E, Ei = t0, t2
                    Qm, kQ = hv(hs, d)
                    Km, kK = hv(hs, 2 + d)
                    if need_q:
                        tt('dve', Qm, zq_bank[:, :], (E if d == 0 else Ei)[:, 0:512], ALU.mult, [zq_k, k0, k2], [kQ])
                    tt('pool', Km, t1[:, 0:512], (Ei if d == 0 else E)[:, 0:512], ALU.mult, [k1, k0, k2], [kK])
                for d in dirs:
                    Km, kK = hv(hs, 2 + d)
                    KmT, kKT = hv(hs, 4 + d)

                    def tps(e, d=d, Km=Km):
                        for i in range(4):
                            r = e.transpose(TPB[:, d * 512 + i * 128:d * 512 + (i + 1) * 128], Km[:, i * 128:(i + 1) * 128], ident)
                        return r
                    S.op('pe', tps, [kK, 'cm'], [K_TP])
                    S.op('act', lambda e, d=d, KmT=KmT: e.activation(out=KmT, in_=TPB[:, d * 512:(d + 1) * 512], func=AF.Copy), [K_TP], [kKT])

            def escal(hs, d, c):
                e = ech[hs]
                ea, eb, ec = e[:, d, c:c + 1], e[:, d, 8 + c:9 + c], e[:, d, 16 + c:17 + c]
                return (ea, eb, ec) if d == 0 else (ec, eb, ea)

            def ds_compute(d, h, hs, c):
                tile_i, po = c // 2, (c % 2) * 64
                KmT, kKT = hv(hs, 4 + d)
                KmT3 = KmT.rearrange("p (t k) -> p t k", t=4)
                bank, bk = rot()
                mm(bank[:, 0:128], [(KmT3[po:po + 64, tile_i, :], Vt[po:po + 64, tile_i, h * 128:(h + 1) * 128])],
                   [kKT] + K_V, [bk])
                e1, eb, e3 = escal(hs, d, c)
                act(dSb[d][:, c, :], bank[:, 0:128], AF.Identity, [bk, ('ech', hs, d)], [('dS', d, c)], scale=e3)

            def st_src(d, h, k, Sst, skey):
                if k == 0:
                    return Sst[:, h, :], skey
                return Sall[d][:, k - 1, :], ('Sall', d, k - 1)

            def st_dst(d, h, k, Sst, skey):
                if k == 7:
                    return Sst[:, h, :], skey
                return Sall[d][:, k, :], ('Sall', d, k)

            def state_step(d, h, hs, k, c, Sst, skey):
                e1, eb, e3 = escal(hs, d, c)
                s_ap, s_k = st_src(d, h, k, Sst, skey)
                d_ap, d_k = st_dst(d, h, k, Sst, skey)
                stt('dve', d_ap, s_ap, eb, dSb[d][:, c, :], ALU.mult, ALU.add, [s_k, ('ech', hs, d), ('dS', d, c)], [d_k])

            def chunk_sb(d, h, hs, k, c, Sst, skey):
                e1, eb, e3 = escal(hs, d, c)
                i = d * 8 + k
                s_ap, s_k = st_src(d, h, k, Sst, skey)
                act(Sb[i][:], s_ap, AF.Identity, [s_k, ('ech', hs, d)], [('Sb', i)], scale=e1)

            def chunk_at(d, h, hs, k, c):
                tile_i, po = c // 2, (c % 2) * 64
                cs = slice(c * 64, (c + 1) * 64)
                Qm, kQ = hv(hs, d)
                Km, kK = hv(hs, 2 + d)
                i = d * 8 + k
                bank, bk = rot()
                mm(bank[po:po + 64, 0:64], [(Km[:, cs], Qm[:, cs])], [kK, kQ], [bk])
                tt('dve', At[i][po:po + 64, :], bank[po:po + 64, 0:64], MH[d][po:po + 64, :], ALU.mult, [bk, 'cm'], [('At', i)])

            def chunk_fin(d, h, hs, k, c, acc, acck):
                tile_i, po = c // 2, (c % 2) * 64
                cs = slice(c * 64, (c + 1) * 64)
                Qm, kQ = hv(hs, d)
                i = d * 8 + k

                def fn(e_):
                    e_.matmul(acc[:, cs], lhsT=Sb[i][:], rhs=Qm[:, cs], start=True, stop=False)
                    return e_.matmul(acc[:, cs], lhsT=Vt[po:po + 64, tile_i, h * 128:(h + 1) * 128], rhs=At[i][po:po + 64, :], start=False, stop=True)
                S.op('pe', fn, [('Sb', i), ('At', i), kQ] + K_V, [acck])

            def exchange(l, kind):
                if kind == 'S':
                    S.dma('sp', exS_i[l].ap(), S1[:].rearrange("p h v -> p (h v)"), reads=['S1'], writes=[('exSi', l)], slot='ex0')
                    S.coll(lambda e: e.collective_compute("AllGather", ALU.bypass, replica_groups=[[0, 1], [2, 3], [4, 5], [6, 7]],
                                                          ins=[exS_i[l].ap().opt()], outs=[exS_o[l].ap().opt()]),
                           reads=[('exSi', l)], writes=[('exSo', l)], slot='ccS%d' % l)
                    S.dma('sp', exs[:], exS_o[l].ap().rearrange("(r p) n -> p r n", p=128), reads=[('exSo', l)], writes=[('exs', 0), ('exs', 1)], slot='ex1')
                    s2f = S2[:].rearrange("p h v -> p (h v)")
                    ts('dve', s2f, exs[:, 0, :], ppt[:, P_SEL:P_SEL + 1], None, ALU.mult, None, [('exs', 0), ('exs', 1), 'pp'], ['S2'])
                    stt('dve', s2f, exs[:, 1, :], ppt[:, P_SEL + 1:P_SEL + 2], s2f, ALU.mult, ALU.add, [('exs', 0), ('exs', 1), 'pp', 'S2'], ['S2'])
                else:
                    tl = NT - 1
                    S.dma('sp', exK_i[l].ap()[:, 0:256].rearrange("p (a b) -> p a b", a=2), KT[:, :, ring(tl):ring(tl) + 128],
                          reads=['KT'], writes=[('exKi', l)], slot='ex2')
                    S.dma('sp', exK_i[l].ap()[:, 256:512], V2[:, tl % 8, :], reads=['V2'], writes=[('exKi', l, 1)], slot='ex3')
                    S.coll(lambda e: e.collective_compute("AllGather", ALU.bypass, replica_groups=[[0, 1], [2, 3], [4, 5], [6, 7]],
                                                          ins=[exK_i[l].ap().opt()], outs=[exK_o[l].ap().opt()]),
                           reads=[('exKi', l), ('exKi', l, 1)], writes=[('exKo', l)], slot='ccK%d' % l)
                    S.dma('sp', exb[:], exK_o[l].ap().rearrange("(r p) n -> p r n", p=128), reads=[('exKo', l)], writes=['exb'], slot='ex4')
                    kdst = KT[:, :, ring(NT):ring(NT) + 128]
                    e0 = exb[:, 0, 0:256].rearrange("p (a b) -> p a b", a=2)
                    e1 = exb[:, 1, 0:256].rearrange("p (a b) -> p a b", a=2)
                    ts('dve', kdst, e0, ppt[:, P_SEL:P_SEL + 1], None, ALU.mult, None, ['exb', 'pp', 'KT'], ['KT'])
                    stt('dve', kdst, e1, ppt[:, P_SEL + 1:P_SEL + 2], kdst, ALU.mult, ALU.add, ['exb', 'pp', 'KT'], ['KT'])
                    vdst = V2[:, NT % 8, :]
                    ts('dve', vdst, exb[:, 0, 256:512], ppt[:, P_SEL:P_SEL + 1], None, ALU.mult, None, ['exb', 'pp', 'V2'], ['V2'])
                    stt('dve', vdst, exb[:, 1, 256:512], ppt[:, P_SEL + 1:P_SEL + 2], vdst, ALU.mult, ALU.add, ['exb', 'pp', 'V2'], ['V2'])

            stop_cnt = {}

            def stop(tag):
                stop_cnt[tag] = stop_cnt.get(tag, 0) + 1
                if DEBUG_STOP[0] == tag or DEBUG_STOP[0] == '%s#%d' % (tag, stop_cnt[tag]):
                    raise _Stop()

            for l in range(NL):
                last = (l == NL - 1)
                S.dma('sp', bsb[:], bs_d[l:l + 1, :].partition_broadcast(128), writes=['bsb'], slot='c2')
                ft, fk = ftile()
                S.dma('sp', ft[:, 0:512], wst_d[l], writes=[fk], slot='c3')
                S.op('dve', lambda e, ft=ft: e.tensor_copy(out=wsT[:], in_=ft[:, 0:512]), [fk], ['wsT'])
                S.dma('sp', esink[:], sink_d[l:l + 1, :].partition_broadcast(128), writes=['esink'], slot='c4')
                act(esink[:], esink[:], AF.Exp, ['esink'], ['esink'])
                es3 = esink[:].rearrange("p (q two) -> p q two", two=2)
                S.op('dve', lambda e, es3=es3: e.tensor_copy(out=esink2[0:64, :], in_=es3[0:64, :, 0]), ['esink'], ['esink2'])
                S.op('dve', lambda e, es3=es3: e.tensor_copy(out=esink2[64:128, :], in_=es3[64:128, :, 1]), ['esink', 'esink2'], ['esink2'])
                S.op('pool', lambda e: e.memset(S1[:], 0.0), writes=['S1'])

                stop('consts')
                for g in range(NG):
                    load_group(l, g, halo=False)
                    rmsnorm([(xT, ['xT'], TG, 128, xn_keys(128))], SC_LN1 + l * 8, l)
                    S.dma('sp', s1s[l, g], S1[:].rearrange("p h v -> p (h v)"), reads=['S1'], writes=[('s1s', l, g)], slot='st1')
                    wb, wk = w_use(l, 'ai')
                    for t in range(4):
                        bank, bk = proj_tm(wb, wk, 0, 4, 128 + t * 128)
                        act(Vt[:, t, :], bank[:, :], AF.Copy, [bk], [K_V[t]])
                    for h in range(4):
                        hs = h % 2
                        wb, wk = w_use(l, 'h%d' % h)
                        zb, zk = proj_fm(wb, wk, 0, 128, TG)
                        hgrn_prep2(l, h, [0], {0: (zb, zk)}, None, None, hs, need_q=False)
                        for c in range(8):
                            ds_compute(0, h, hs, c)
                        for c in range(8):
                            e1_, eb_, e3_ = escal(hs, 0, c)
                            stt('dve', S1[:, h, :], S1[:, h, :], eb_, dSb[0][:, c, :], ALU.mult, ALU.add, ['S1', ('ech', hs, 0), ('dS', 0, c)], ['S1'])
                stop('pass1')
                exchange(l, 'S')
                stop('exS')

                c1['on'] = (l == 0 and NL == 2)
                for g in range(NG - 1, -1, -1):
                    has_lo = g > 0
                    has_hi = True
                    load_group(l, g, halo=True)
                    S.dma('sp', S1[:].rearrange("p h v -> p (h v)"), s1s[l, g], reads=[('s1s', l, g)], writes=['S1'], slot='ld1')
                    c_lo = g * TG - (128 if has_lo else 0)
                    ncs = TG + (128 if has_lo else 0)
                    o_lo = 0 if has_lo else 128
                    S.dma('sp', cosT[:, o_lo:640], rope_d[0, :, c_lo:c_lo + ncs], writes=['cos'], slot='lc')
                    S.dma('sp', sinT[:, o_lo:640], rope_d[1, :, c_lo:c_lo + ncs], writes=['sin'], slot='ls')
                    segs = []
                    if has_lo:
                        segs.append((xh, ['xh'], 128, 0, xn_keys(0)))
                    segs.append((xT, ['xT'], TG, 128, xn_keys(128)))
                    stop('p2load')
                    rmsnorm(segs, SC_LN1 + l * 8, l)
                    stop('p2norm')
                    wb, wk = w_use(l, 'ckv')
                    tiles = ([4 * g - 1] if has_lo else []) + [4 * g + i for i in range(4)]
                    kcalls = []
                    for kvf in range(2):
                        if has_lo:
                            kcalls.append((lambda kvf=kvf: proj_fm(wb, wk, kvf, 0, 128), 0, 128,
                                           KT[:, kvf, ring(4 * g - 1):ring(4 * g - 1) + 128], ['KT']))
                        kcalls.append((lambda kvf=kvf: proj_fm(wb, wk, kvf, 128, TG), 128, TG,
                                       KT[:, kvf, ring(4 * g):ring(4 * g) + TG], ['KT']))
                    rope_multi(kcalls)
                    stop('krope')
                    for t in tiles:
                        c0 = 128 + (t - 4 * g) * 128
                        bank, bk = proj_tm(wb, wk, 2, 2, c0)
                        act(V2[:, t % 8, :], bank[:, 0:256], AF.Copy, [bk], ['V2'])
                    stop('kv')
                    if g == NG - 1:
                        exchange(l, 'K')
                    stop('exK')
                    wb, wk = w_use(l, 'ai')
                    for t in range(4):
                        bank, bk = proj_tm(wb, wk, 0, 4, 128 + t * 128)
                        act(Vt[:, t, :], bank[:, :], AF.Copy, [bk], [K_V[t]])
                    def hg_prep(h):
                            hs = h % 2
                            wb, wk = w_use(l, 'h%d' % h)
                            zq, zqk = proj_fm(wb, wk, 2, 128, TG)
                            zf = {}
                            for d in range(2):
                                zf[d] = proj_fm(wb, wk, d, 128, TG)
                            hgrn_prep2(l, h, [0, 1], zf, zq, zqk, hs, need_q=True)
                            zg, zgk = proj_fm(wb, wk, 3, 128, TG)
                            thg, kthg = ftile()
                            act(thg[:, 0:512], zg[:, :], AF.Tanh, [zgk], [kthg], scale=0.5)
                            sg, ksg = hv(hs, 6)
                            stt('dve', sg, thg[:, 0:512], 1.0, zg[:, :], ALU.add, ALU.mult, [kthg, zgk], [ksg])
                    def gates_blocks(js):
                        for j in js:
                            wb_, wk_ = w_use(l, 'gz%d' % j)
                            for fb in range(4):
                                bank, bk = proj_fm(wb_, wk_, fb, 128, TG)
                                act(big[:, j * 4 + fb, :], bank[:, :], AF.Sigmoid, [bk], [('big', j * 4 + fb)])

                    def hg_recur(h):
                        hs = h % 2
                        sg, ksg = hv(hs, 6)
                        for d in range(2):
                            for c in range(8):
                                ds_compute(d, h, hs, c)
                        chunk_sb(0, h, hs, 0, 0, S1, 'S1')
                        chunk_sb(1, h, hs, 0, 7, S2, 'S2')
                        for k in range(8):
                            chunk_at(0, h, hs, k, k)
                            chunk_at(1, h, hs, k, 7 - k)
                        for k in range(8):
                            state_step(0, h, hs, k, k, S1, 'S1')
                            state_step(1, h, hs, k, 7 - k, S2, 'S2')
                        if h < 3:
                            gates_blocks([2 * h, 2 * h + 1])
                        for k in range(8):
                            if k > 0:
                                chunk_sb(0, h, hs, k, k, S1, 'S1')
                                chunk_sb(1, h, hs, k, 7 - k, S2, 'S2')
                            chunk_fin(0, h, hs, k, k, ACC[0][0], ACC[0][1])
                            chunk_fin(1, h, hs, k, 7 - k, ACC[1][0], ACC[1][1])
                        o1, ko1 = ftile()
                        act(o1[:, 0:512], ACC[0][0][:, :], AF.Copy, [ACC[0][1]], [ko1])
                        o, ko = ftile()
                        tt('dve', o[:, 0:512], ACC[1][0][:, :], o1[:, 0:512], ALU.add, [ACC[1][1], ko1], [ko])
                        act(osq[:], o[:, 0:512], AF.Square, [ko], ['osq'])
                        bank, bk = rot()
                        mm(bank[:, :], [(ones_b, osq[:])], ['osq', 'cm'], [bk])
                        rs, krs = ftile()
                        rs0, krs0 = ftile()
                        ts('dve', rs0[:, 0:512], bank[:, :], float(128 * EPS), None, ALU.add, None, [bk], [krs0])
                        act(rs0[:, 0:512], rs0[:, 0:512], AF.Ln, [krs0], [krs0])
                        act(rs[:, 0:512], rs0[:, 0:512], AF.Exp, [krs0], [krs], scale=-0.5)
                        t_, kt_ = ftile()
                        tt('dve', t_[:, 0:512], o[:, 0:512], rs[:, 0:512], ALU.mult, [ko, krs], [kt_])
                        stt('dve', yaT[:, h, :], t_[:, 0:512], sc[:, SC_AN + l:SC_AN + l + 1], sg, ALU.mult, ALU.mult, [kt_, ksg] + SCK, [('ya', h)])
                    hg_prep(0)
                    for h in range(4):
                        if h + 1 < 4:
                            hg_prep(h + 1)
                        hg_recur(h)
                    stop('hgrn')
                    wb, wk = w_use(l, 'bu')
                    for fb in range(4):
                        bank, bk = proj_fm(wb, wk, fb, 128, TG)
                        act(uT[:, fb, :], bank[:, :], AF.Gelu_apprx_tanh, [bk], [K_UT[fb]])
                    wb, wk = w_use(l, 'bv')
                    vs_ = []
                    for t in range(4):
                        bank, bk = proj_tm(wb, wk, 0, 4, 128 + t * 128)
                        v, kv = ftile()
                        act(v[:, 0:512], bank[:, :], AF.Gelu_apprx_tanh, [bk], [kv])
                        vs_.append((v, kv))
                        S.op('dve', lambda e, v=v, t=t: e.bn_stats(out=bnst[:, t * 6:(t + 1) * 6], in_=v[:, 0:512]), [kv], [('bn', t)])
                        S.op('dve', lambda e, t=t: e.bn_aggr(out=bnst[:, 24 + 2 * t:26 + 2 * t], in_=bnst[:, t * 6:(t + 1) * 6]), [('bn', t)], [('bn2', t)])
                    var4 = bnst[:, 24:32].rearrange("p (t two) -> p t two", two=2)[:, :, 1]
                    BN2 = [('bn2', t) for t in range(4)]
                    ts('dve', bnst[:, 32:36], var4, float(EPS), None, ALU.add, None, BN2, ['bn3'])
                    act(bnst[:, 32:36], bnst[:, 32:36], AF.Ln, ['bn3'], ['bn3'])
                    act(bnst[:, 36:40], bnst[:, 32:36], AF.Exp, ['bn3'], ['bn4'], scale=-0.5)
                    for t in range(4):
                        v, kv = vs_[t]
                        ts('dve', vn[:, t, :], v[:, 0:512], bnst[:, 24 + 2 * t:25 + 2 * t], bnst[:, 36 + t:37 + t], ALU.subtract, ALU.mult,
                           [kv, ('bn2', t), 'bn4'], [K_VN[t]])
                    wsT3 = wsT[:].rearrange("p (g q) -> p g q", g=4)
                    for t in range(4):
                        bank, bk = rot()

                        def fn(e, bank=bank, t=t):
                            for gg in range(4):
                                r = e.matmul(bank[:, gg * 128:(gg + 1) * 128], lhsT=vn[:, t, gg * 128:(gg + 1) * 128], rhs=wsT3[:, gg, :], start=True, stop=True)
                            return r
                        S.op('pe', fn, [K_VN[t], 'wsT'], [bk])
                        mb, kmb = ftile()
                        tt('dve', mb[:, 0:512], bank[:, :], bsb[:], ALU.add, [bk, 'bsb'], [kmb])
                        tt('pool', ybT[:, :, t * 128:(t + 1) * 128], mb[:, 0:512].rearrange("p (g q) -> p g q", g=4), uT[:, :, t * 128:(t + 1) * 128], ALU.mult,
                           [kmb] + K_UT, [('yb', t)])
                    YBK = [('yb', t) for t in range(4)]
                    stop('gmlp')
                    wb, wk = w_use(l, 'cq')
                    rope_multi([(lambda fb=fb: proj_fm(wb, wk, fb, 128, TG), 128, TG, QT[:, fb, :], [K_QT[fb]]) for fb in range(4)])
                    kts = ([4 * g - 1] if has_lo else []) + [4 * g + i for i in range(4)] + [4 * g + 4]
                    arot_i = [0]

                    def arot():
                        i = arot_i[0] % 4
                        arot_i[0] += 1
                        return psb[i], ('ps', i)
                    OD = [((psb[5], ('ps', 5)), (psb[6], ('ps', 6))), ((psb[7], ('ps', 7)), (psb[4], ('ps', 4)))]

                    def att_scores(qb, hh):
                        kvh = qb // 2
                        pr = slice(hh * 64, hh * 64 + 64)
                        inf = {}
                        for j, kt in enumerate(kts):
                            qlo, qhi = max(kt - 1, 4 * g), min(kt + 1, 4 * g + 3)
                            nq = qhi - qlo + 1
                            qc0 = (qlo - 4 * g) * 128
                            bank, bk = arot()
                            mlist = []
                            for qi in range(nq):
                                rel = (qlo + qi) - kt
                                if kt == NT:
                                    mlist.append((qi, NEGM['c']))
                                elif rel != 0:
                                    mlist.append((qi, NEGM[rel]))

                            def fn(e, bank=bank, pr=pr, kvh=kvh, kt=kt, qb=qb, qc0=qc0, nq=nq, mlist=mlist):
                                r = e.matmul(bank[:, 0:nq * 128], lhsT=KT[pr, kvh, ring(kt):ring(kt) + 128], rhs=QT[pr, qb, qc0:qc0 + nq * 128],
                                             start=True, stop=(len(mlist) == 0))
                                for ii, (qi, ng) in enumerate(mlist):
                                    r = e.matmul(bank[:, qi * 128:(qi + 1) * 128], lhsT=ident, rhs=ng, start=False, stop=(ii == len(mlist) - 1))
                                return r
                            S.op('pe', fn, ['KT', K_QT[qb], 'cm'], [bk])
                            act(PT[hh][j][:, 0:nq * 128], bank[:, 0:nq * 128], AF.Exp, [bk], K_PT[hh][j], scale=0.125)
                            inf[kt] = (j, qlo)
                        return inf

                    def att_pv(qb, hh, inf):
                        kvh = qb // 2
                        (obank, obk), (dbank, dbk) = OD[qb % 2]
                        pr = slice(hh * 64, hh * 64 + 64)
                        for qt in range(4 * g, 4 * g + 4):
                            qcs = slice((qt - 4 * g) * 128, (qt - 4 * g + 1) * 128)
                            use = [kt for kt in (qt - 1, qt, qt + 1) if kt in inf]
                            pairs_o, pairs_d, rk = [], [], []
                            for kt in use:
                                j, qlo = inf[kt]
                                p_ap = PT[hh][j][:, (qt - qlo) * 128:(qt - qlo + 1) * 128]
                                vc = kvh * 128 + hh * 64
                                pairs_o.append((V2[:, kt % 8, vc:vc + 64], p_ap))
                                pairs_d.append((ones_b[:, 0:64], p_ap))
                                rk += K_PT[hh][j]
                            mm(obank[pr, qcs], pairs_o, rk + ['V2'], [obk])
                            mm(dbank[pr, qcs], pairs_d, rk + ['cm'], [dbk])

                    def att_norm(qb):
                        (obank, obk), (dbank, dbk) = OD[qb % 2]
                        rd, krd = ftile()
                        ts('dve', rd[:, 0:512], dbank[:, :], esink2[:, qb:qb + 1], None, ALU.add, None, [dbk, 'esink2'], [krd])
                        S.op('dve', lambda e, rd=rd: e.reciprocal(out=rd[:, 0:512], in_=rd[:, 0:512]), [krd], [krd])
                        tt('dve', ycT[:, qb, :], obank[:, :], rd[:, 0:512], ALU.mult, [obk, krd], [('yc', qb, 0)])

                    units = [(qb, hh) for qb in range(4) for hh in range(2)]
                    infos = {units[0]: att_scores(*units[0])}
                    for ui, u in enumerate(units):
                        if ui + 1 < len(units):
                            infos[units[ui + 1]] = att_scores(*units[ui + 1])
                        att_pv(u[0], u[1], infos[u])
                        if u[1] == 1:
                            att_norm(u[0])
                    YCK = [('yc', qb, 0) for qb in range(4)]
                    YAK = [('ya', h) for h in range(4)]
                    stop('attn')
                    stop('gates')
                    ysrc = [(yaT, YAK), (ybT, YBK), (ycT, YCK)]
                    for j in range(4):
                        wb, wk = w_use(l, 'br%d' % j)
                        wv = wb[:, 0:3072].rearrange("p (b k c) -> p b k c", b=3, k=4)
                        for o2 in range(2):
                            ob = 2 * j + o2
                            tl_ = []
                            for br in range(3):
                                ysb, ykeys = ysrc[br]
                                bank, bk = rot()
                                mm(bank[:, :], [(wv[:, br, kc, o2 * 128:(o2 + 1) * 128], ysb[:, kc, :]) for kc in range(4)], [wk] + ykeys, [bk])
                                tf, ktf = ftile()
                                tt('dve', tf[:, 0:512], bank[:, :], big[:, br * 8 + ob, :], ALU.mult, [bk, ('big', br * 8 + ob)], [ktf])
                                tl_.append((tf, ktf))
                            tt('pool', tl_[0][0][:, 0:512], tl_[0][0][:, 0:512], tl_[1][0][:, 0:512], ALU.add, [tl_[0][1], tl_[1][1]], [tl_[0][1]])
                            tt('pool', big[:, 24 + ob, :], tl_[0][0][:, 0:512], tl_[2][0][:, 0:512], ALU.add, [tl_[0][1], tl_[2][1]], [('big', 24 + ob)])
                    stop('merge')
                    for j in range(2):
                        wb, wk = w_use(l, 'wo%d' % j)
                        wv = wb[:].rearrange("p (k c) -> p k c", k=8)
                        for o4 in range(4):
                            ob = j * 4 + o4
                            bank, bk = rot()
                            mm(bank[:, :], [(wv[:, kc, o4 * 128:(o4 + 1) * 128], big[:, 24 + kc, :]) for kc in range(KC)],
                               [wk] + [('big', 24 + kc) for kc in range(KC)], [bk])
                            tt('dve', xT[:, ob, :], xT[:, ob, :], bank[:, :], ALU.add, [bk, 'xT'], ['xT'])
                    stop('wout')
                    rmsnorm([(xT, ['xT'], TG, 128, xn_keys(128))], SC_LN2 + l * 8, l)
                    ri = 0
                    for j in range(8):
                        wb, wk = w_use(l, 'up%d' % j)
                        for fb in range(4):
                            bank, bk = proj_fm(wb, wk, fb, 128, TG)
                            r = rl[ri % 2]
                            rk_ = ('rl', ri % 2)
                            ri += 1
                            act(r[:], bank[:, :], AF.Relu, [bk], [rk_])
                            tt('pool', big[:, j * 4 + fb, :], r[:], r[:], ALU.mult, [rk_], [('big', j * 4 + fb)])
                    for ob in range(8):
                        wb, wk = w_use(l, 'dn%d' % ob)
                        wv = wb[:].rearrange("p (k c) -> p k c", k=32)
                        bank, bk = rot()
                        mm(bank[:, :], [(wv[:, kc, :], big[:, kc, :]) for kc in range(32)], [wk] + BIGK, [bk])
                        tt('dve', xT[:, ob, :], xT[:, ob, :], bank[:, :], ALU.add, [bk, 'xT'], ['xT'])
                    stop('ffn')
                    if last:
                        rmsnorm([(xT, ['xT'], TG, 128, None)], SC_FN, l, out_f32=True)
                        S.dma('sp', yout[:, :, g * TG:(g + 1) * TG].rearrange("k p t -> p k t"), xT[:], reads=['xT'], writes=[('y', g)], slot='sto')
                    else:
                        S.dma('sp', xs[l][:, :, g * TG:(g + 1) * TG].rearrange("k p t -> p k t"), xT[:], reads=['xT'], writes=[('xs', l, g)], slot='sto')
                if c1['on']:
                    while c1['done'] < len(cast1_steps):
                        cast1_step()
                    c1['on'] = False
        try:
            main_body()
        except _Stop:
            S.dma('sp', yout[:, :, 0:TG].rearrange("k p t -> p k t"), xT[:], reads=['xT'], writes=[('y', 0)], slot='sto')
        S.final_wait('sp', ['sto'])
        print("ops:", {e: len(S.lists[e]) for e in S.ENGS})
        with nc.Block() as block:
            S.emit(block)
    return nc


def _consts():
    r = np.arange(128)
    ident = np.eye(128, dtype=np.float32)
    pm = np.zeros((128, 128), np.float32)
    for c in range(128):
        d = c % 64
        if d < 8:
            pm[c + 8, c] = 1.0
        elif d < 16:
            pm[c - 8, c] = 1.0
    m = r[:, None]
    a = r[None, :]
    m3 = np.concatenate([(m <= a), np.ones((128, 128), bool), (m >= a)], axis=1).astype(np.float32)
    mc = ((m + a) >= 127).astype(np.float32)
    s = (r % 64)[:, None]
    t = np.arange(64)[None, :]
    mh1 = (s <= t).astype(np.float32)
    mh2 = (s >= t).astype(np.float32)
    ones = np.ones((128, 128), np.float32)
    NEG = np.float32(-30000.0)
    nb_ = np.where(m <= a, 0.0, NEG).astype(np.float32)
    na_ = np.where(m >= a, 0.0, NEG).astype(np.float32)
    nc_ = np.where((m + a) >= 127, 0.0, NEG).astype(np.float32)
    cm = np.concatenate([ident, pm, m3, mc, mh1, mh2, ones, nb_, na_, nc_], axis=1)
    assert cm.shape[1] == C_END
    return cm.astype(ml_dtypes.bfloat16)


def _rope_tables(pos):
    inv = np.float32(500000.0) ** (-(np.arange(8, dtype=np.float32) * np.float32(2.0 / 16)))
    ang = pos.astype(np.float32)[:, None] * inv[None, :]
    cos = np.cos(ang).astype(np.float32)
    sin = np.sin(ang).astype(np.float32)
    T = pos.shape[0]
    C = np.ones((128, T), np.float32)
    Sn = np.zeros((128, T), np.float32)
    for rr in range(128):
        d = rr % 64
        if d < 8:
            C[rr] = cos[:, d]
            Sn[rr] = -sin[:, d]
        elif d < 16:
            C[rr] = cos[:, d - 8]
            Sn[rr] = sin[:, d - 8]
    return np.stack([C, Sn], 0)


def _wblocks(w_in, w_br, w_out, w_up, w_down, half):
    NL = w_in.shape[0]
    out = np.zeros((NL, NB, 128, 4096), np.float32)

    def fbk(W, cols):
        return W[:, cols].reshape(8, 128, 128).transpose(1, 0, 2)

    def blk(W, colsets):
        return np.stack([fbk(W, c) for c in colsets], 1).reshape(128, 4096)
    ar = np.arange
    for l in range(NL):
        W = w_in[l]
        f1o, f2o = (1024, 1536) if half == 0 else (1536, 1024)
        out[l, BI['ai']] = blk(W, [512 + h * 128 + ar(128) for h in range(4)])
        for h in range(4):
            out[l, BI['h%d' % h]] = blk(W, [f1o + h * 128 + ar(128), f2o + h * 128 + ar(128), h * 128 + ar(128), 2048 + h * 128 + ar(128)])
        out[l, BI['bu']] = blk(W, [2560 + f * 128 + ar(128) for f in range(4)])
        out[l, BI['bv']] = blk(W, [3072 + f * 128 + ar(128) for f in range(4)])
        out[l, BI['cq']] = blk(W, [3584 + f * 128 + ar(128) for f in range(4)])
        k0, k1 = 4096 + ar(64), 4160 + ar(64)
        v0, v1 = 4224 + ar(64), 4288 + ar(64)
        cc = np.concatenate
        out[l, BI['ckv']] = blk(W, [cc([k0, k0]), cc([k1, k1]), cc([v0, v0]), cc([v1, v1])])
        for j in range(6):
            out[l, BI['gz%d' % j]] = blk(W, [4352 + (j * 4 + f) * 128 + ar(128) for f in range(4)])
        for j in range(4):
            a = w_br[l][:, :, j * 256:(j + 1) * 256].reshape(3, 4, 128, 256).transpose(2, 0, 1, 3)
            out[l, BI['br%d' % j], :, 0:3072] = a.reshape(128, 3072)
        for j in range(2):
            a = w_out[l][:, j * 512:(j + 1) * 512].reshape(8, 128, 512).transpose(1, 0, 2)
            out[l, BI['wo%d' % j]] = a.reshape(128, 4096)
        for j in range(8):
            out[l, BI['up%d' % j]] = blk(w_up[l], [j * 512 + f * 128 + ar(128) for f in range(4)])
        for ob in range(8):
            a = w_down[l][:, ob * 128:(ob + 1) * 128].reshape(32, 128, 128).transpose(1, 0, 2)
            out[l, BI['dn%d' % ob]] = a.reshape(128, 4096)
    return out


_PROG_CACHE = {}


def kernel(x, w_in, ln1, lb_logits, a_norm, w_s, b_s, sink, w_br, w_out, ln2, w_up, w_down, final_norm):
    x = np.asarray(x, np.float32)
    f = lambda a: np.asarray(a, np.float32)
    w_in, ln1, lb_logits, a_norm, w_s, b_s, sink = map(f, (w_in, ln1, lb_logits, a_norm, w_s, b_s, sink))
    w_br, w_out, ln2, w_up, w_down, final_norm = map(f, (w_br, w_out, ln2, w_up, w_down, final_norm))
    B, SEQ, _ = x.shape
    NL = w_in.shape[0]
    TOK = SEQ // 2
    assert B * 2 == NCORES and TOK % TG == 0
    key = (TOK, NL)
    if key not in _PROG_CACHE:
        _PROG_CACHE[key] = build_program(TOK, NL, SEQ)
    nc = _PROG_CACHE[key]
    cm = _consts()
    in_maps = []
    variants = {}
    for half in range(2):
        wsrc = _wblocks(w_in, w_br, w_out, w_up, w_down, half)
        pos = np.arange(TOK) if half == 0 else (SEQ - 1 - np.arange(TOK))
        rope = _rope_tables(pos)
        pp = np.zeros((128, P_END), np.float32)
        for l in range(NL):
            pp[:, P_LN1 + l * 8:P_LN1 + l * 8 + 8] = ln1[l].reshape(8, 128).T
            pp[:, P_LN2 + l * 8:P_LN2 + l * 8 + 8] = ln2[l].reshape(8, 128).T
            for d in range(2):
                dirn = d if half == 0 else 1 - d
                pp[:, P_LBL + l * 8 + d * 4:P_LBL + l * 8 + d * 4 + 4] = lb_logits[l, dirn].reshape(4, 128).T
            pp[:, P_AN + l] = a_norm[l]
        pp[:, P_FN:P_FN + 8] = final_norm.reshape(8, 128).T
        pp[:, P_SEL] = 0.0 if half == 0 else 1.0
        pp[:, P_SEL + 1] = 1.0 if half == 0 else 0.0
        if half == 0:
            wst = np.ascontiguousarray(w_s.transpose(0, 3, 1, 2)).reshape(NL, 128, 512)
            bsv = b_s.reshape(NL, 512)
        else:
            wst = np.ascontiguousarray(w_s[:, :, ::-1, ::-1].transpose(0, 3, 1, 2)).reshape(NL, 128, 512)
            bsv = np.ascontiguousarray(b_s[:, :, ::-1]).reshape(NL, 512)
        variants[half] = dict(wsrc=wsrc, pp=pp, cm=cm, bsv=np.ascontiguousarray(bsv), wst=wst,
                              sinkv=np.ascontiguousarray(sink), rope=rope)
    for c in range(NCORES):
        b, half = c // 2, c % 2
        xs_ = x[b, :TOK] if half == 0 else x[b, TOK:][::-1]
        xin = np.ascontiguousarray(xs_.T).reshape(KC, 128, TOK)
        m = dict(variants[half])
        m['xin'] = xin
        in_maps.append(m)
    res = run_bass_kernel_spmd(nc, in_maps, core_ids=list(range(NCORES)))
    y = np.empty((B, SEQ, D), np.float32)
    for c in range(NCORES):
        b, half = c // 2, c % 2
        yt = np.asarray(res.results[c]["yout"]).reshape(D, TOK).T
        if half == 0:
            y[b, :TOK] = yt
        else:
            y[b, TOK:] = yt[::-1]
    return y
```
